# Optimizing a Trainium2 kernel written in Bass

```python
import math
import jax, jax.numpy as jnp
from jax import lax
import numpy as np

D_MODEL = 1024
BATCH = 8
SEQ = 8192
DEPTH = 1
DEC_BATCH = 1
DEC_SEQ = 16384
PAST_LEN = 128

SSM_WIDTH = D_MODEL // 2
SSM_GROUP = 16
SSM_GROUPS = SSM_WIDTH // SSM_GROUP
SSM_STATE = 64
N_HEADS = 8
N_KV_HEADS = 2
HEAD_DIM = 64
ATTN_WIDTH = N_HEADS * HEAD_DIM
KV_WIDTH = N_KV_HEADS * HEAD_DIM
MIX_WIDTH = SSM_WIDTH + ATTN_WIDTH
IN_WIDTH = SSM_WIDTH + ATTN_WIDTH + 2 * KV_WIDTH
D_FF = 2816
CONV_W = 3
GRID_W = 64
ROPE_THETA = 10000.0
Q_BLOCK = 128
EPS = 1e-6
DT_MIN = 1e-3
DT_MAX = 1e-1

kernel_name = "hymba_s5_gqa_axial_convffn_encoder"

F32 = jnp.float32


def rms_norm(x, g):
    xf = x.astype(F32)
    y = xf * lax.rsqrt(jnp.mean(xf * xf, axis=-1, keepdims=True) + EPS)
    return (y * g.astype(F32)).astype(x.dtype)


def axial_rope_tables(seq_len):
    rows = seq_len // GRID_W
    row_id = jnp.broadcast_to(jnp.arange(rows)[:, None], (rows, GRID_W)).reshape(-1).astype(F32)
    col_id = jnp.broadcast_to(jnp.arange(GRID_W)[None, :], (rows, GRID_W)).reshape(-1).astype(F32)
    axis_dim = HEAD_DIM // 2
    inv_freq = ROPE_THETA ** (-jnp.arange(0, axis_dim, 2, dtype=F32) / axis_dim)
    ang = jnp.concatenate([row_id[:, None] * inv_freq, col_id[:, None] * inv_freq], axis=-1)
    return jnp.cos(ang), jnp.sin(ang)


def apply_rope(x, cos, sin):
    xf = x.astype(F32)
    half = HEAD_DIM // 2
    x1, x2 = xf[..., :half], xf[..., half:]
    c = cos[:, None, :]
    s = sin[:, None, :]
    return jnp.concatenate([x1 * c - x2 * s, x2 * c + x1 * s], axis=-1).astype(x.dtype)


def gqa_block_attention(q, k, v):
    b, l = q.shape[0], q.shape[1]
    rep = N_HEADS // N_KV_HEADS
    nb = l // Q_BLOCK
    qb = q.reshape(b, nb, Q_BLOCK, N_KV_HEADS, rep, HEAD_DIM).transpose(1, 0, 2, 3, 4, 5)
    scale = HEAD_DIM ** -0.5

    def one_block(qblk):
        s = jnp.einsum('bqkgd,bskd->bkgqs', qblk, k).astype(F32) * scale
        p = jax.nn.softmax(s, axis=-1).astype(v.dtype)
        return jnp.einsum('bkgqs,bskd->bqkgd', p, v)

    o = lax.map(one_block, qb)
    return o.transpose(1, 0, 2, 3, 4, 5).reshape(b, l, ATTN_WIDTH)


def s5_bidirectional(u, lam_re, lam_im, log_dt, b_re, b_im, c_re, c_im, d_skip):
    bsz, l = u.shape[0], u.shape[1]
    uf = u.astype(F32).reshape(bsz, l, SSM_GROUPS, SSM_GROUP)
    dt = jnp.exp(log_dt.astype(F32))[..., None]
    lr = lam_re.astype(F32)
    li = lam_im.astype(F32)
    mag = jnp.exp(lr * dt)
    ab_re = mag * jnp.cos(li * dt)
    ab_im = mag * jnp.sin(li * dt)
    den = lr * lr + li * li
    nr = ab_re - 1.0
    ni = ab_im
    zr = (nr * lr + ni * li) / den
    zi = (ni * lr - nr * li) / den
    br = b_re.astype(F32)
    bi = b_im.astype(F32)
    bb_re = zr[..., None] * br - zi[..., None] * bi
    bb_im = zr[..., None] * bi + zi[..., None] * br
    cr = c_re.astype(F32)
    ci = c_im.astype(F32)

    def combine(e1, e2):
        a1r, a1i, x1r, x1i = e1
        a2r, a2i, x2r, x2i = e2
        return (a2r * a1r - a2i * a1i,
                a2r * a1i + a2i * a1r,
                a2r * x1r - a2i * x1i + x2r,
                a2r * x1i + a2i * x1r + x2i)

    def one_direction(useq, d, reverse):
        xr = jnp.einsum('lgh,gph->lgp', useq, bb_re[d])
        xi = jnp.einsum('lgh,gph->lgp', useq, bb_im[d])
        ar = jnp.broadcast_to(ab_re[d], xr.shape)
        ai = jnp.broadcast_to(ab_im[d], xr.shape)
        _, _, sr, si = lax.associative_scan(combine, (ar, ai, xr, xi), reverse=reverse, axis=0)
        return jnp.einsum('lgp,ghp->lgh', sr, cr[d]) - jnp.einsum('lgp,ghp->lgh', si, ci[d])

    def one_seq(useq):
        return one_direction(useq, 0, False) + one_direction(useq, 1, True)

    y = lax.map(one_seq, uf) + uf * d_skip.astype(F32)
    return y.reshape(bsz, l, SSM_WIDTH).astype(u.dtype)


def dwconv_centered(h, w, b):
    pad = CONV_W // 2
    l = h.shape[1]
    hp = jnp.pad(h, ((0, 0), (pad, pad), (0, 0)))
    out = b
    for j in range(CONV_W):
        out = out + hp[:, j:j + l] * w[j]
    return out


def encoder(x, norm1_g, w_in, lam_re, lam_im, log_dt, b_re, b_im, c_re, c_im, d_skip,
            w_glu, b_glu, q_norm_g, k_norm_g, ssm_out_g, attn_out_g, w_out,
            norm2_g, w_gate, w_up, conv_w, conv_b, w_down, final_norm_g):
    bsz, l = x.shape[0], x.shape[1]
    cos, sin = axial_rope_tables(l)
    for i in range(DEPTH):
        h = rms_norm(x, norm1_g[i])
        z = h @ w_in[i]
        u = z[..., :SSM_WIDTH]
        q = z[..., SSM_WIDTH:SSM_WIDTH + ATTN_WIDTH]
        k = z[..., SSM_WIDTH + ATTN_WIDTH:SSM_WIDTH + ATTN_WIDTH + KV_WIDTH]
        v = z[..., SSM_WIDTH + ATTN_WIDTH + KV_WIDTH:]

        s = s5_bidirectional(u, lam_re[i], lam_im[i], log_dt[i], b_re[i], b_im[i],
                             c_re[i], c_im[i], d_skip[i])
        s = jax.nn.gelu(s)
        s = s * jax.nn.sigmoid(s @ w_glu[i] + b_glu[i])

        q = q.reshape(bsz, l, N_HEADS, HEAD_DIM)
        k = k.reshape(bsz, l, N_KV_HEADS, HEAD_DIM)
        v = v.reshape(bsz, l, N_KV_HEADS, HEAD_DIM)
        q = apply_rope(rms_norm(q, q_norm_g[i]), cos, sin)
        k = apply_rope(rms_norm(k, k_norm_g[i]), cos, sin)
        o = gqa_block_attention(q, k, v)

        merged = jnp.concatenate([rms_norm(s, ssm_out_g[i]), rms_norm(o, attn_out_g[i])], axis=-1)
        x = x + merged @ w_out[i]

        h2 = rms_norm(x, norm2_g[i])
        g = dwconv_centered(h2 @ w_gate[i], conv_w[i], conv_b[i])
        x = x + (jax.nn.silu(g) * (h2 @ w_up[i])) @ w_down[i]
    return rms_norm(x, final_norm_g)


def setup_inputs(seed: int = 0) -> dict:
    key = jax.random.key(seed)
    ks = jax.random.split(key, 32)
    G, P, H = SSM_GROUPS, SSM_STATE, SSM_GROUP

    def nrm(k, shape, scale):
        return jax.random.normal(k, shape, F32) * scale

    def gain(k, shape):
        return 1.0 + 0.02 * jax.random.normal(k, shape, F32)

    lam_im_base = math.pi * jnp.arange(P, dtype=F32)
    return {
        "x_prompt": jax.random.normal(ks[0], (BATCH, SEQ, D_MODEL), F32),
        "x_sample": jax.random.normal(ks[1], (DEC_BATCH, DEC_SEQ, D_MODEL), F32),
        "norm1_g": gain(ks[2], (DEPTH, D_MODEL)),
        "w_in": nrm(ks[3], (DEPTH, D_MODEL, IN_WIDTH), D_MODEL ** -0.5),
        "lam_re": -0.5 + 0.01 * jax.random.normal(ks[4], (DEPTH, 2, G, P), F32),
        "lam_im": lam_im_base + 0.01 * jax.random.normal(ks[5], (DEPTH, 2, G, P), F32),
        "log_dt": jax.random.uniform(ks[6], (DEPTH, 2, G), F32, math.log(DT_MIN), math.log(DT_MAX)),
        "b_re": nrm(ks[7], (DEPTH, 2, G, P, H), (2.0 * H) ** -0.5),
        "b_im": nrm(ks[8], (DEPTH, 2, G, P, H), (2.0 * H) ** -0.5),
        "c_re": nrm(ks[9], (DEPTH, 2, G, H, P), (2.0 * P) ** -0.5),
        "c_im": nrm(ks[10], (DEPTH, 2, G, H, P), (2.0 * P) ** -0.5),
        "d_skip": nrm(ks[11], (DEPTH, G, H), 1.0),
        "w_glu": nrm(ks[12], (DEPTH, SSM_WIDTH, SSM_WIDTH), SSM_WIDTH ** -0.5),
        "b_glu": nrm(ks[13], (DEPTH, SSM_WIDTH), 0.02),
        "q_norm_g": gain(ks[14], (DEPTH, HEAD_DIM)),
        "k_norm_g": gain(ks[15], (DEPTH, HEAD_DIM)),
        "ssm_out_g": gain(ks[16], (DEPTH, SSM_WIDTH)),
        "attn_out_g": gain(ks[17], (DEPTH, ATTN_WIDTH)),
        "w_out": nrm(ks[18], (DEPTH, MIX_WIDTH, D_MODEL), MIX_WIDTH ** -0.5),
        "norm2_g": gain(ks[19], (DEPTH, D_MODEL)),
        "w_gate": nrm(ks[20], (DEPTH, D_MODEL, D_FF), D_MODEL ** -0.5),
        "w_up": nrm(ks[21], (DEPTH, D_MODEL, D_FF), D_MODEL ** -0.5),
        "conv_w": nrm(ks[22], (DEPTH, CONV_W, D_FF), CONV_W ** -0.5),
        "conv_b": nrm(ks[23], (DEPTH, D_FF), 0.02),
        "w_down": nrm(ks[24], (DEPTH, D_FF, D_MODEL), D_FF ** -0.5),
        "final_norm_g": gain(ks[25], (D_MODEL,)),
    }


def reference(x_prompt, x_sample, norm1_g, w_in, lam_re, lam_im, log_dt, b_re, b_im, c_re, c_im,
              d_skip, w_glu, b_glu, q_norm_g, k_norm_g, ssm_out_g, attn_out_g, w_out,
              norm2_g, w_gate, w_up, conv_w, conv_b, w_down, final_norm_g):
    y_prompt = encoder(x_prompt, norm1_g, w_in, lam_re, lam_im, log_dt, b_re, b_im, c_re, c_im,
                       d_skip, w_glu, b_glu, q_norm_g, k_norm_g, ssm_out_g, attn_out_g, w_out,
                       norm2_g, w_gate, w_up, conv_w, conv_b, w_down, final_norm_g)
    y_sample = encoder(x_sample, norm1_g, w_in, lam_re, lam_im, log_dt, b_re, b_im, c_re, c_im,
                       d_skip, w_glu, b_glu, q_norm_g, k_norm_g, ssm_out_g, attn_out_g, w_out,
                       norm2_g, w_gate, w_up, conv_w, conv_b, w_down, final_norm_g)
    return (y_prompt, y_sample)
```

```python
import contextlib
import math
import numpy as np
import concourse.bass as bass
import concourse.mybir as mybir
from concourse.bass_utils import run_bass_kernel_spmd

F32 = mybir.dt.float32
BF16 = mybir.dt.bfloat16
I32 = mybir.dt.int32
ALU = mybir.AluOpType
AF = mybir.ActivationFunctionType
AX = mybir.AxisListType

D = 1024
LP = 8192
LS = 16384
LOWN = 2048
LEXT = LOWN + 256
DFF = 2816
NFF = 22
EPS = 1e-6
NDMA = 12
TC = 32
TWO_PI = 2.0 * math.pi
DEBUG = False
NQB_LIMIT = None
ONLY_P = False
STOP = 99


class Prog:
    ENG = ("pe", "act", "dve", "pool", "sp")

    def __init__(self, nc, sems, dsems):
        self.nc = nc
        self.cnt = {e: 0 for e in self.ENG}
        self.sem = dict(zip(self.ENG, sems))
        self.dsem = dsems
        self.dcnt = [0] * len(dsems)
        self.dnext = 0
        self.seen = {e: {} for e in self.ENG}
        self.begin()

    def begin(self):
        self.ops = {e: [] for e in self.ENG}
        self.lastw = {}
        self.readers = {}
        for e in self.ENG:
            for e2 in self.ENG:
                self.seen[e][e2] = self.cnt[e2]
            for k, c in enumerate(self.dcnt):
                self.seen[e][k] = c

    def _deps(self, e, reads, writes):
        deps = {}

        def need(p):
            if p is not None:
                deps[p[0]] = max(deps.get(p[0], 0), p[1])

        for b in reads:
            need(self.lastw.get(b))
        for b in writes:
            need(self.lastw.get(b))
            for r in self.readers.get(b, ()):
                need(r)
        waits = []
        for k, v in deps.items():
            if k == "pe" and e == "pe":
                continue
            if self.seen[e].get(k, 0) < v:
                self.seen[e][k] = v
                waits.append((k, v))
        return waits

    def _semof(self, k):
        return self.sem[k] if isinstance(k, str) else self.dsem[k]

    def _commit(self, ident, reads, writes):
        for b in reads:
            lst = self.readers.setdefault(b, [])
            lst[:] = [r for r in lst if r[0] != ident[0]] + [ident]
        for b in writes:
            self.lastw[b] = ident
            self.readers[b] = []

    def add(self, e, fn, reads=(), writes=()):
        waits = self._deps(e, reads, writes)
        self.cnt[e] += 1
        ident = (e, self.cnt[e])
        mysem = self.sem[e]
        wl = [(self._semof(k), v) for k, v in waits]

        def run(eng):
            for s, v in wl:
                eng.wait_ge(s, v)
            fn(eng).then_inc(mysem, 1)

        self.ops[e].append(run)
        self._commit(ident, reads, writes)

    def dma(self, q, out, in_, reads=(), writes=()):
        k = self.dnext
        self.dnext = (self.dnext + 1) % len(self.dsem)
        waits = self._deps(q, reads, writes)
        prev = self.dcnt[k]
        if prev and self.seen[q].get(k, 0) < prev:
            self.seen[q][k] = prev
            waits.append((k, prev))
        self.dcnt[k] += 16
        ident = (k, self.dcnt[k])
        wl = [(self._semof(kk), v) for kk, v in waits]
        dsem = self.dsem[k]

        def run(eng):
            for s, v in wl:
                eng.wait_ge(s, v)
            eng.dma_start(out=out, in_=in_).then_inc(dsem, 16)

        self.ops[q].append(run)
        self._commit(ident, reads, writes)

    def finish(self, block):
        finals = [(self.dsem[k], c) for k, c in enumerate(self.dcnt) if c]
        esem = [(self.sem[e], self.cnt[e]) for e in self.ENG if self.cnt[e]]

        def tail(eng):
            for s, v in finals + esem:
                eng.wait_ge(s, v)

        self.ops["sp"].append(tail)
        ops = self.ops

        @block.tensor
        def _(eng):
            for f in ops["pe"]:
                f(eng)

        @block.scalar
        def _(eng):
            for f in ops["act"]:
                f(eng)

        @block.vector
        def _(eng):
            for f in ops["dve"]:
                f(eng)

        @block.gpsimd
        def _(eng):
            for f in ops["pool"]:
                f(eng)

        @block.sync
        def _(eng):
            for f in ops["sp"]:
                f(eng)


def rap(t, p0, npart, foff, dims):
    pitch = 1
    for s in list(t.shape)[1:]:
        pitch *= int(s)
    return bass.AP(t, p0 * pitch + foff, [[pitch, npart]] + [list(d) for d in dims])


def tt(P, eng, out, in0, in1, op, r, w):
    P.add(eng, lambda e: e.tensor_tensor(out=out, in0=in0, in1=in1, op=op), r, w)


def ts(P, eng, out, in0, s1, op0, r, w, s2=None, op1=None):
    if op1 is None:
        P.add(eng, lambda e: e.tensor_scalar(out=out, in0=in0, scalar1=s1, scalar2=None, op0=op0), r, w)
    else:
        P.add(eng, lambda e: e.tensor_scalar(out=out, in0=in0, scalar1=s1, scalar2=s2, op0=op0, op1=op1), r, w)


def stt(P, out, in0, scalar, in1, op0, op1, r, w):
    P.add("dve", lambda e: e.scalar_tensor_tensor(out=out, in0=in0, scalar=scalar, in1=in1, op0=op0, op1=op1), r, w)


def act(P, out, in_, func, r, w, scale=1.0, bias=None, accum=None):
    def f(e):
        kw = {}
        if bias is not None:
            kw["bias"] = bias
        if accum is not None:
            kw["accum_out"] = accum
        return e.activation(out=out, in_=in_, func=func, scale=scale, **kw)
    P.add("act", f, r, w)


def cp(P, eng, out, in_, r, w):
    if eng == "act":
        P.add("act", lambda e: e.copy(out=out, in_=in_), r, w)
    else:
        P.add(eng, lambda e: e.tensor_copy(out=out, in_=in_), r, w)


def mm(P, out, lhsT, rhs, start, stop, r, w, **kw):
    P.add("pe", lambda e: e.matmul(out, lhsT=lhsT, rhs=rhs, start=start, stop=stop, **kw), r, w)


def tr(P, out, in_, ident, r, w):
    P.add("pe", lambda e: e.transpose(out, in_, ident), r, w)


def rsqrt_col(P, out, in_, tmp, scale, r, w):
    tn = w[0] + "_t"
    act(P, tmp, in_, AF.Sqrt, r, [tn], scale=scale, bias=EPS_AP[0])
    P.add("dve", lambda e: e.reciprocal(out=out, in_=tmp), [tn], w)


EPS_AP = [None]


def exp_acc(P, out, x, k, r, ni, nx, nout, pre):
    LOG2E = 1.4426950408889634
    ts(P, "dve", k, x, LOG2E, ALU.mult, [nx], [pre + "k"])
    cp(P, "dve", ni, k, [pre + "k"], [pre + "ni"])
    cp(P, "dve", r, ni, [pre + "ni"], [pre + "nf"])
    tt(P, "dve", k, k, r, ALU.subtract, [pre + "k", pre + "nf"], [pre + "f"])
    ts(P, "dve", r, k, math.log(2.0), ALU.mult, [pre + "f"], [pre + "r"])
    K = 12
    ts(P, "dve", out, r, 1.0 / K, ALU.mult, [pre + "r"], [nout], s2=1.0, op1=ALU.add)
    for kk in range(K - 1, 0, -1):
        tt(P, "dve", out, out, r, ALU.mult, [nout, pre + "r"], [nout])
        ts(P, "dve", out, out, 1.0 / kk, ALU.mult, [nout], [nout], s2=1.0, op1=ALU.add)
    ts(P, "dve", ni, ni, 127, ALU.add, [pre + "ni", pre + "nf"], [pre + "ni2"])
    ts(P, "dve", ni, ni, 23, ALU.logical_shift_left, [pre + "ni2"], [pre + "ni3"])
    tt(P, "dve", out, out, ni.bitcast(F32), ALU.mult, [nout, pre + "ni3"], [nout])


class Ctx:
    pass


class NCProxy:
    _uid = [0]

    def __init__(self, nc):
        self._nc = nc
        NCProxy._uid[0] += 1
        self._sfx = "_i%d" % NCProxy._uid[0]

    def sbuf_tensor(self, name, *a, **k):
        return self._nc.sbuf_tensor(name + self._sfx, *a, **k)

    def psum_tensor(self, name, *a, **k):
        return self._nc.psum_tensor(name + self._sfx, *a, **k)

    def __getattr__(self, n):
        return getattr(self._nc, n)


def dram(nc, name, shape, dt, kind="Internal"):
    return nc.dram_tensor(name, list(shape), dt, kind=kind).ap()


def declare_io(nc, dbg):
    g = Ctx()
    I = lambda n, s: dram(nc, n, s, F32, "ExternalInput")
    g.xp = I("xp", [LP, D])
    g.xsf = I("xsf", [LS, D])
    g.xse = I("xse", [LEXT, D])
    g.misc = I("misc", [128, 8])
    g.oh = I("oh", [128, 2, LS // TC])
    g.wu = I("wu", [128, 8, 1024])
    g.wqkv = I("wqkv", [128, 8, 768])
    g.g1 = I("g1", [128, 8])
    g.wglu = I("wglu", [128, 8, 1024])
    g.bglu = I("bglu", [128, 8])
    g.dskip = I("dskip", [128, 8])
    g.gso = I("gso", [128, 8])
    g.wos = I("wos", [128, 8, 1024])
    g.woa = I("woa", [128, 4, 1024])
    g.gao = I("gao", [128, 4])
    g.g2 = I("g2", [128, 8])
    g.wg = I("wg", [128, 8, DFF])
    g.wup = I("wup", [128, 8, DFF])
    g.cw = I("cw", [128, NFF, 3])
    g.cb = I("cb", [128, NFF])
    g.wdn = I("wdn", [128, NFF, 1024])
    g.gf = I("gf", [128, D])
    g.gq = I("gq", [128, 64])
    g.gk = I("gk", [128, 64])
    g.lamA = I("lamA", [64, 3, 64])
    g.lamB = I("lamB", [128, 3, 32])
    g.bA = I("bA", [64, 2, 64, 32])
    g.cA = I("cA", [64, 2, 64, 32])
    g.cB = I("cB", [128, 2, 32, 32])
    g.yp = dram(nc, "yp", [LP, D], F32, "ExternalOutput")
    g.ys = dram(nc, "ys", [LOWN, D], F32, "ExternalOutput")
    S = lambda n, s, dt=BF16: dram(nc, n, s, dt, "ExternalOutput" if n in dbg else "Internal")
    g.Wu = S("Wu", [128, 8, 1024]); g.Wqkv = S("Wqkv", [128, 8, 768]); g.Wglu = S("Wglu", [128, 8, 1024])
    g.Wos = S("Wos", [128, 8, 1024]); g.Woa = S("Woa", [128, 4, 1024])
    g.Wg = S("Wg", [128, 8, DFF]); g.Wup = S("Wup", [128, 8, DFF]); g.Wdn = S("Wdn", [128, NFF, 1024])
    g.KTp = S("KTp", [128, LP]); g.KTs = S("KTs", [128, LS])
    g.Vp = S("Vp", [128, LP // 128, 130]); g.Vs = S("Vs", [128, LS // 128, 130])
    g.Qp = S("Qp", [LP // 128, 128, 512]); g.Qs = S("Qs", [LEXT // 128, 128, 512])
    g.Up = S("Up", [8, 128, LP]); g.Us = S("Us", [8, 128, LS]); g.Ue = S("Ue", [8, 128, LEXT])
    g.Sp = S("Sp", [8, 128, LP]); g.Se = S("Se", [8, 128, LEXT])
    g.X1p = S("X1p", [LP, D], F32); g.X1e = S("X1e", [LEXT, D], F32)
    g.VTd = S("VTd", [8, 128, 2 * 2 * TC * 64]); g.Rd = S("Rd", [8, 128, 4 * 2 * TC * 32])
    g.BDd = S("BDd", [8, 128, 2 * TC * 128])
    g.ATd = S("ATd", [128, 2, 32], F32)
    g.dbg = S("dbg", [128, 16384], F32)
    g.X2p = S("X2p", [LP, D], F32); g.X2e = S("X2e", [LEXT, D], F32)
    g.INITd = S("INITd", [128, 32, 2], F32)
    g.SBp = S("SBp", [128, (LP // TC) * 64]); g.SBe = S("SBe", [128, (LEXT // TC) * 64])
    return g


def phase_weights(nc, P, g):
    nc = NCProxy(nc)
    with contextlib.ExitStack() as st:
        E = st.enter_context
        stg = [E(nc.sbuf_tensor("w_stg%d" % i, [128, DFF], F32)) for i in range(2)]
        stb = [E(nc.sbuf_tensor("w_stb%d" % i, [128, DFF], BF16)) for i in range(2)]
        gains = E(nc.sbuf_tensor("w_gains", [128, 32], F32))
        blk = E(nc.Block())
        P.begin()
        for i, (src, n) in enumerate([(g.g1, 8), (g.gso, 8), (g.gao, 4), (g.g2, 8)]):
            P.dma("sp", gains[:, 8 * i:8 * i + n], src[:, :], writes=["gains"])
        jobs = [(g.wu, g.Wu, 8, 1024, 0), (g.wqkv, g.Wqkv, 8, 768, 0), (g.wglu, g.Wglu, 8, 1024, None),
                (g.wos, g.Wos, 8, 1024, 8), (g.woa, g.Woa, 4, 1024, 16), (g.wg, g.Wg, 8, DFF, 24),
                (g.wup, g.Wup, 8, DFF, 24), (g.wdn, g.Wdn, NFF, 1024, None)]
        n = 0
        for src, dst, kcs, width, goff in jobs:
            for kc in range(kcs):
                b = n % 2
                n += 1
                P.dma("sp", stg[b][:, 0:width], src[:, kc, :], writes=["stg%d" % b])
                if goff is None:
                    cp(P, "dve" if n % 2 else "pool", stb[b][:, 0:width], stg[b][:, 0:width],
                       ["stg%d" % b], ["stb%d" % b])
                else:
                    ts(P, "dve", stb[b][:, 0:width], stg[b][:, 0:width], gains[:, goff + kc:goff + kc + 1], ALU.mult,
                       ["stg%d" % b, "gains"], ["stb%d" % b])
                P.dma("act", dst[:, kc, :], stb[b][:, 0:width], reads=["stb%d" % b])
        P.finish(blk)


def build_rope(nc, P, E, name, ntiles, rowbase_ap):
    CS = E(nc.sbuf_tensor(name, [128, ntiles, 64], F32))
    pi_ = E(nc.sbuf_tensor(name + "_pi", [128, 4], I32))
    pf = E(nc.sbuf_tensor(name + "_pf", [128, 4], F32))
    fi = E(nc.sbuf_tensor(name + "_fi", [128, 16], I32))
    invf = E(nc.sbuf_tensor(name + "_invf", [128, 16], F32))
    ti = E(nc.sbuf_tensor(name + "_ti", [128, ntiles], I32))
    rowp = E(nc.sbuf_tensor(name + "_rowp", [128, ntiles], F32))
    ang = E(nc.sbuf_tensor(name + "_ang", [128, ntiles, 32], F32))
    t1 = E(nc.sbuf_tensor(name + "_t1", [128, ntiles, 32], F32))
    t2 = E(nc.sbuf_tensor(name + "_t2", [128, ntiles, 32], F32))
    ni = E(nc.sbuf_tensor(name + "_ni", [128, ntiles, 32], I32))
    N = name
    P.add("pool", lambda e: e.iota(pi_[:, 0:1], [[0, 1]], base=0, channel_multiplier=1), [], [N + "pi0"])
    P.add("pool", lambda e: e.iota(fi[:], [[1, 16]], base=0, channel_multiplier=0), [], [N + "fi"])
    P.add("pool", lambda e: e.iota(ti[:], [[2, ntiles]], base=0, channel_multiplier=0), [], [N + "ti"])
    ts(P, "dve", pi_[:, 1:2], pi_[:, 0:1], 6, ALU.arith_shift_right, [N + "pi0"], [N + "pi1"])
    ts(P, "dve", pi_[:, 2:3], pi_[:, 0:1], 63, ALU.bitwise_and, [N + "pi0"], [N + "pi2"])
    cp(P, "dve", pf[:, 1:3], pi_[:, 1:3], [N + "pi1", N + "pi2"], [N + "pf"])
    cp(P, "dve", invf[:], fi[:], [N + "fi"], [N + "invf0"])
    act(P, invf[:], invf[:], AF.Exp, [N + "invf0"], [N + "invf"], scale=-math.log(10000.0) / 16.0)
    cp(P, "dve", rowp[:], ti[:], [N + "ti"], [N + "rowp0"])
    if rowbase_ap is not None:
        ts(P, "dve", rowp[:], rowp[:], pf[:, 1:2], ALU.add, [N + "rowp0", N + "pf", "misc"], [N + "rowp"],
           s2=rowbase_ap, op1=ALU.add)
    else:
        ts(P, "dve", rowp[:], rowp[:], pf[:, 1:2], ALU.add, [N + "rowp0", N + "pf"], [N + "rowp"])
    tt(P, "dve", ang[:, :, 0:16], rap(rowp, 0, 128, 0, [[1, ntiles], [0, 16]]),
       rap(invf, 0, 128, 0, [[0, ntiles], [1, 16]]), ALU.mult, [N + "rowp", N + "invf"], [N + "angA"])
    ts(P, "dve", ang[:, :, 16:32], rap(invf, 0, 128, 0, [[0, ntiles], [1, 16]]), pf[:, 2:3], ALU.mult,
       [N + "invf", N + "pf"], [N + "angB"])
    for which, off in ((0, 0.25), (1, 0.0)):
        ts(P, "dve", t1[:], ang[:], 1.0 / TWO_PI, ALU.mult, [N + "angA", N + "angB"], [N + "t1"],
           s2=off, op1=ALU.add)
        sin_frac(P, CS[:, :, 32 * which:32 * which + 32], t1[:], t2[:], ni[:], N + "t1", N + "t2", N + "ni",
                 N + "w%dout" % which)
    return CS


def sin_frac(P, out, t, tmp, ni, nt, ntmp, nni, nout):
    cp(P, "dve", ni, t, [nt], [nni])
    cp(P, "dve", tmp, ni, [nni], [ntmp])
    tt(P, "dve", t, t, tmp, ALU.subtract, [nt, ntmp], [nt])
    ts(P, "dve", tmp, t, 0.5, ALU.is_gt, [nt], [ntmp])
    tt(P, "dve", t, t, tmp, ALU.subtract, [nt, ntmp], [nt])
    ts(P, "dve", tmp, t, -0.5, ALU.is_lt, [nt], [ntmp])
    tt(P, "dve", t, t, tmp, ALU.add, [nt, ntmp], [nt])
    act(P, out, t, AF.Sin, [nt], [nout], scale=TWO_PI)


def phase_inproj(nc, P, g, seq):
    nc = NCProxy(nc)
    xin, L = {"p": (g.xp, LP), "s": (g.xsf, LS), "e": (g.xse, LEXT)}[seq]
    do_kv = seq in ("p", "s")
    do_q = seq in ("p", "e")
    KT, V = (g.KTp, g.Vp) if seq == "p" else (g.KTs, g.Vs)
    Q = g.Qp if seq == "p" else g.Qs
    U = {"p": g.Up, "s": g.Us, "e": g.Ue}[seq]
    ntile = L // 128
    with contextlib.ExitStack() as st:
        E = st.enter_context
        wu = E(nc.sbuf_tensor("i_wu", [128, 8, 1024], BF16))
        wqkv = E(nc.sbuf_tensor("i_wqkv", [128, 8, 768], BF16))
        ident = E(nc.sbuf_tensor("i_ident", [128, 128], BF16))
        identf = E(nc.sbuf_tensor("i_identf", [128, 128], F32))
        misc = E(nc.sbuf_tensor("i_misc", [128, 8], F32))
        gq = E(nc.sbuf_tensor("i_gq", [128, 64], F32))
        gk = E(nc.sbuf_tensor("i_gk", [128, 64], F32))
        epsc = E(nc.sbuf_tensor("i_eps", [128, 1], F32))
        xt = [E(nc.sbuf_tensor("i_xt%d" % i, [128, D], F32)) for i in range(2)]
        junk = E(nc.sbuf_tensor("i_junk", [128, D], F32))
        xn = [E(nc.sbuf_tensor("i_xn%d" % i, [128, D], BF16)) for i in range(2)]
        stat = [E(nc.sbuf_tensor("i_stat%d" % i, [128, 48], F32)) for i in range(2)]
        hT = [E(nc.sbuf_tensor("i_hT%d" % i, [128, 8, 512], BF16)) for i in range(2)]
        ust = [E(nc.sbuf_tensor("i_ust%d" % i, [128, 8, 512], BF16)) for i in range(2)]
        kst = [E(nc.sbuf_tensor("i_kst%d" % i, [128, 512], BF16)) for i in range(2)]
        vst = [E(nc.sbuf_tensor("i_vst%d" % i, [128, 4, 130], BF16)) for i in range(2)]
        qst = [E(nc.sbuf_tensor("i_qst%d" % i, [128, 4, 128], BF16)) for i in range(2)]
        sq2 = [E(nc.sbuf_tensor("i_sq%d" % i, [128, 640], F32)) for i in range(2)]
        qn2 = [E(nc.sbuf_tensor("i_qn%d" % i, [128, 10, 64], F32)) for i in range(2)]
        ra2 = [E(nc.sbuf_tensor("i_ra%d" % i, [128, 10, 32], F32)) for i in range(2)]
        rb2 = [E(nc.sbuf_tensor("i_rb%d" % i, [128, 10, 32], F32)) for i in range(2)]
        qr2 = [E(nc.sbuf_tensor("i_qr%d" % i, [128, 10, 64], BF16)) for i in range(2)]
        pT = E(nc.psum_tensor("i_pT", [128, 1024], BF16))
        pT2 = E(nc.psum_tensor("i_pT2", [128, 1024], BF16))
        pT3 = E(nc.psum_tensor("i_pT3", [128, 1024], BF16))
        pQ2 = [E(nc.psum_tensor("i_pQ%d" % i, [128, 512], F32)) for i in range(2)]
        pKVb = E(nc.psum_tensor("i_pKV", [128, 512], F32))
        pU = [E(nc.psum_tensor("i_pU%d" % i, [128, 512], F32)) for i in range(2)]
        blk = E(nc.Block())
        P.begin()
        EPS_AP[0] = epsc[:, 0:1]
        P.add("pool", lambda e: e.memset(epsc[:], EPS), [], ["eps"])
        P.dma("sp", wu[:], g.Wu[:, :, :], writes=["wu"])
        P.dma("sp", wqkv[:], g.Wqkv[:, :, :], writes=["wqkv"])
        P.dma("sp", misc[:], g.misc[:, :], writes=["misc"])
        P.dma("sp", gq[:], g.gq[:, :], writes=["gq"])
        P.dma("sp", gk[:], g.gk[:, :], writes=["gk"])
        make_ident(nc, P, E, ident, identf)
        CS = build_rope(nc, P, E, "i_cs", ntile, misc[:, 0:1] if seq == "e" else None)
        for b in range(2):
            P.add("pool", lambda e, b=b: e.memset(vst[b][:], 1.0), [], ["vst%d" % b])
        for t in range(ntile):
            b = t % 2
            sb = (t // 4) % 2
            sub = t % 4
            X, S_ = xt[b], stat[b]
            sq, qn, ra, rb, qr, pQ = sq2[b], qn2[b], ra2[b], rb2[b], qr2[b], pQ2[b]
            pKV = pKVb[:, 256 * b:256 * b + 256]
            B_ = "_%d" % b
            P.dma("sp", X[:], xin[t * 128:(t + 1) * 128, :], writes=["xt%d" % b])
            act(P, junk[:], X[:], AF.Square, ["xt%d" % b], ["junk", "ssq%d" % b], accum=S_[:, 0:1])
            rsqrt_col(P, S_[:, 2:3], S_[:, 0:1], S_[:, 1:2], 1.0 / D, ["ssq%d" % b, "eps"], ["rs%d" % b])
            ts(P, "dve", xn[b][:], X[:], S_[:, 2:3], ALU.mult, ["xt%d" % b, "rs%d" % b], ["xn%d" % b])
            for kc in range(8):
                tr(P, pT[:, kc * 128:(kc + 1) * 128], xn[b][:, kc * 128:(kc + 1) * 128], ident[:],
                   ["xn%d" % b, "ident"], ["pT"])
            cp(P, "act", hT[sb][:, :, sub * 128:(sub + 1) * 128], pT[:].rearrange("p (k t) -> p k t", k=8),
               ["pT"], ["hT%d_%d" % (sb, sub)])
            hname = "hT%d_%d" % (sb, sub)
            nh = 0
            if do_q:
                for kc in range(8):
                    mm(P, pQ[:], hT[sb][:, kc, sub * 128:(sub + 1) * 128], wqkv[:, kc, 0:512], kc == 0, kc == 7,
                       [hname, "wqkv"], ["pQ" + B_])
                nh = 8
            if do_kv:
                for kc in range(8):
                    mm(P, pKV, hT[sb][:, kc, sub * 128:(sub + 1) * 128], wqkv[:, kc, 512:768],
                       kc == 0, kc == 7, [hname, "wqkv"], ["pKV" + B_])
            H = nh + (2 if do_kv else 0)
            if do_q:
                act(P, sq[:, 0:512], pQ[:], AF.Square, ["pQ" + B_], ["sqq" + B_])
            if do_kv:
                act(P, sq[:, 512:640], pKV[:, 0:128], AF.Square, ["pKV" + B_], ["sqk" + B_])
            h0 = 0 if do_q else 8
            P.add("dve", lambda e, S_=S_, h0=h0, H=H, sq=sq: e.tensor_reduce(
                out=S_[:, 4 + h0:4 + h0 + H], in_=sq[:, h0 * 64:(h0 + H) * 64].rearrange("p (h d) -> p h d", d=64),
                axis=AX.X, op=ALU.add), ["sqq" + B_, "sqk" + B_], ["hss%d" % b])
            rsqrt_col(P, S_[:, 28 + h0:28 + h0 + H], S_[:, 4 + h0:4 + h0 + H], S_[:, 16 + h0:16 + h0 + H],
                      1.0 / 64.0, ["hss%d" % b, "eps"], ["hrs%d" % b])
            if do_q:
                tt(P, "dve", rap(qn, 0, 128, 0, [[64, 2], [128, 4], [1, 64]]),
                   rap(pQ, 0, 128, 0, [[256, 2], [64, 4], [1, 64]]),
                   rap(S_, 0, 128, 28, [[4, 2], [1, 4], [0, 64]]), ALU.mult, ["pQ" + B_, "hrs%d" % b], ["qnq" + B_])
                tt(P, "pool", qn[:, 0:8, :], qn[:, 0:8, :], rap(gq, 0, 128, 0, [[0, 8], [1, 64]]), ALU.mult,
                   ["qnq" + B_, "gq"], ["qnq" + B_])
            if do_kv:
                tt(P, "dve", qn[:, 8:10, :], pKV[:, 0:128].rearrange("p (h d) -> p h d", d=64),
                   rap(S_, 0, 128, 36, [[1, 2], [0, 64]]), ALU.mult, ["pKV" + B_, "hrs%d" % b], ["qnk" + B_])
                tt(P, "pool", qn[:, 8:10, :], qn[:, 8:10, :], rap(gk, 0, 128, 0, [[0, 2], [1, 64]]), ALU.mult,
                   ["qnk" + B_, "gk"], ["qnk" + B_])
            cosb = rap(CS, 0, 128, t * 64, [[0, H], [1, 32]])
            sinb = rap(CS, 0, 128, t * 64 + 32, [[0, H], [1, 32]])
            x1 = qn[:, h0:h0 + H, 0:32]
            x2 = qn[:, h0:h0 + H, 32:64]
            rd = ["qnq" + B_, "qnk" + B_, "i_csw0out", "i_csw1out"]
            tt(P, "dve", ra[:, 0:H, :], x1, cosb, ALU.mult, rd, ["ra" + B_])
            tt(P, "pool", rb[:, 0:H, :], x2, sinb, ALU.mult, rd, ["rb" + B_])
            tt(P, "dve", qr[:, h0:h0 + H, 0:32], ra[:, 0:H, :], rb[:, 0:H, :], ALU.subtract, ["ra" + B_, "rb" + B_], ["qr" + B_])
            tt(P, "dve", ra[:, 0:H, :], x2, cosb, ALU.mult, rd + ["qr" + B_], ["ra" + B_])
            tt(P, "pool", rb[:, 0:H, :], x1, sinb, ALU.mult, rd + ["qr" + B_], ["rb" + B_])
            tt(P, "dve", qr[:, h0:h0 + H, 32:64], ra[:, 0:H, :], rb[:, 0:H, :], ALU.add, ["ra" + B_, "rb" + B_], ["qr" + B_])
            if do_q:
                for hh in range(4):
                    tr(P, pT2[:, hh * 128:(hh + 1) * 128],
                       qr[:, 2 * hh:2 * hh + 2, :].rearrange("p h d -> p (h d)"), ident[:],
                       ["qr" + B_, "qr" + B_, "ident"], ["pT2q"])
                cp(P, "act", qst[b][:], pT2[:, 0:512].rearrange("p (h t) -> p h t", h=4), ["pT2q"], ["qst%d" % b])
                P.dma("act", Q[t, :, :], qst[b][:].rearrange("p h t -> p (h t)"), reads=["qst%d" % b])
            if do_kv:
                tr(P, pT3[:, 0:128], qr[:, 8:10, :].rearrange("p h d -> p (h d)"), ident[:],
                   ["qr" + B_, "qr" + B_, "ident"], ["pT3k"])
                cp(P, "act", kst[sb][:, sub * 128:(sub + 1) * 128], pT3[:, 0:128], ["pT3k"], ["kst%d_%d" % (sb, sub)])
                cp(P, "dve", vst[sb][:, sub, :].rearrange("p (h e) -> p h e", h=2)[:, :, 0:64],
                   pKV[:, 128:256].rearrange("p (h d) -> p h d", d=64), ["pKV" + B_, "vst%d" % sb], ["vst%d_%d" % (sb, sub)])
            if sub == 3 or t == ntile - 1:
                s0 = (t // 4) * 512
                nt = (sub + 1) * 128
                for ut in range(8):
                    pu = pU[ut % 2]
                    for kc in range(8):
                        mm(P, pu[:, 0:nt], wu[:, kc, ut * 128:(ut + 1) * 128], hT[sb][:, kc, 0:nt], kc == 0, kc == 7,
                           ["hT%d_%d" % (sb, i) for i in range(sub + 1)] + ["wu"], ["pU%d" % (ut % 2)])
                    cp(P, "act" if ut % 2 else "dve", ust[sb][:, ut, 0:nt], pu[:, 0:nt], ["pU%d" % (ut % 2)],
                       ["ust%d_%d" % (sb, ut)])
                P.dma("pool", U[:, :, s0:s0 + nt].rearrange("u p t -> p u t"), ust[sb][:, :, 0:nt],
                      reads=["ust%d_%d" % (sb, i) for i in range(8)])
                if do_kv:
                    P.dma("pool", KT[:, s0:s0 + 512], kst[sb][:],
                          reads=["kst%d_%d" % (sb, i) for i in range(4)])
                    P.dma("pool", V[:, (t // 4) * 4:(t // 4) * 4 + 4, :], vst[sb][:],
                          reads=["vst%d_%d" % (sb, i) for i in range(4)])
        if DEBUG and seq == "p":
            P.dma("sp", g.dbg[:, 0:4096], CS[:].rearrange("p t c -> p (t c)"), reads=["i_csw0out", "i_csw1out"])
            P.dma("sp", g.dbg[:, 4096:4144], stat[1][:], reads=["hrs1"])
            P.dma("sp", g.dbg[:, 4200:4840], qn[:].rearrange("p h d -> p (h d)"), reads=["qnq" + B_, "qnk" + B_])
            P.dma("sp", g.dbg[:, 5000:5640], sq[:], reads=["sqq" + B_, "sqk" + B_])
        P.finish(blk)


def make_ident(nc, P, E, ident, identf):
    ci = E(nc.sbuf_tensor(ident.name + "_ci", [128, 128], I32))
    pi_ = E(nc.sbuf_tensor(ident.name + "_pi", [128, 1], I32))
    cf = E(nc.sbuf_tensor(ident.name + "_cf", [128, 128], F32))
    pf = E(nc.sbuf_tensor(ident.name + "_pf", [128, 1], F32))
    P.add("pool", lambda e: e.iota(ci[:], [[1, 128]], base=0, channel_multiplier=0), [], ["id_ci"])
    P.add("pool", lambda e: e.iota(pi_[:], [[0, 1]], base=0, channel_multiplier=1), [], ["id_pi"])
    cp(P, "dve", cf[:], ci[:], ["id_ci"], ["id_cf"])
    cp(P, "dve", pf[:], pi_[:], ["id_pi"], ["id_pf"])
    ts(P, "dve", identf[:], cf[:], pf[:, 0:1], ALU.is_equal, ["id_cf", "id_pf"], ["identf"])
    cp(P, "dve", ident[:], identf[:], ["identf"], ["ident"])


def build(dbg=(), upto=99):
    nc = bass.Bass("TRN2", target_bir_lowering=False)
    g = declare_io(nc, dbg)
    with contextlib.ExitStack() as st:
        E = st.enter_context
        E(nc.allow_low_precision("bf16 matmul operands, fp32 accumulation"))
        sems = [E(nc.semaphore("s_" + e)) for e in Prog.ENG]
        dsems = [E(nc.semaphore("d%d" % i)) for i in range(NDMA)]
        P = Prog(nc, sems, dsems)
        phase_weights(nc, P, g)
        if upto >= 1:
            phase_s5_derive(nc, P, g)
        if upto >= 2:
            phase_inproj(nc, P, g, "p")
            if not ONLY_P:
                phase_inproj(nc, P, g, "s")
                phase_inproj(nc, P, g, "e")
        if upto >= 3:
            phase_s5_states(nc, P, g, "p")
            if STOP > 20:
                phase_s5_out(nc, P, g, "p")
            if not ONLY_P:
                phase_s5_states(nc, P, g, "s")
                phase_s5_states(nc, P, g, "e")
                phase_s5_out(nc, P, g, "e")
        if upto >= 4:
            phase_attn(nc, P, g, "p", nqb_limit=NQB_LIMIT)
            if not ONLY_P:
                phase_attn(nc, P, g, "e", nqb_limit=NQB_LIMIT)
        if upto >= 5:
            for seq in (("p",) if ONLY_P else ("p", "e")):
                phase_ffn2(nc, P, g, seq, 0, 11, True, False)
                phase_ffn2(nc, P, g, seq, 11, 22, False, True)
    return nc


def slot_pad_cols(w):
    out = np.zeros(w.shape[:-1] + (32, 32), w.dtype)
    out[..., :, :16] = w.reshape(w.shape[:-1] + (32, 16))
    return out.reshape(w.shape[:-1] + (1024,))


def kcl(w, kc):
    return np.ascontiguousarray(w.reshape(kc, 128, -1).transpose(1, 0, 2))


def col(v, kc):
    return np.ascontiguousarray(v.reshape(kc, 128).T)


def prep_inputs(inp):
    f = lambda a: np.ascontiguousarray(a, dtype=np.float32)
    w_in = f(inp["w_in"])[0]
    sh = {}
    sh["wu"] = kcl(slot_pad_cols(w_in[:, 0:512]), 8)
    sh["wqkv"] = kcl(w_in[:, 512:1280], 8)
    sh["g1"] = col(f(inp["norm1_g"])[0], 8)
    wglu = f(inp["w_glu"])[0]
    sh["wglu"] = kcl(slot_pad_cols(slot_pad_cols(wglu).T).T, 8)
    sh["bglu"] = col(slot_pad_cols(f(inp["b_glu"])[0]), 8)
    sh["dskip"] = col(slot_pad_cols(f(inp["d_skip"])[0].reshape(512)), 8)
    sh["gso"] = col(slot_pad_cols(f(inp["ssm_out_g"])[0]), 8)
    w_out = f(inp["w_out"])[0]
    sh["wos"] = kcl(slot_pad_cols(w_out[0:512].T).T, 8)
    sh["woa"] = kcl(w_out[512:1024], 4)
    sh["gao"] = col(f(inp["attn_out_g"])[0], 4)
    sh["g2"] = col(f(inp["norm2_g"])[0], 8)
    sh["wg"] = kcl(f(inp["w_gate"])[0], 8)
    sh["wup"] = kcl(f(inp["w_up"])[0], 8)
    cw = f(inp["conv_w"])[0]
    sh["cw"] = np.ascontiguousarray(cw.reshape(3, NFF, 128).transpose(2, 1, 0))
    sh["cb"] = col(f(inp["conv_b"])[0], NFF)
    sh["wdn"] = kcl(f(inp["w_down"])[0], NFF)
    sh["gf"] = f(np.broadcast_to(f(inp["final_norm_g"])[None, :], (128, D)))
    sh["gq"] = f(np.broadcast_to(f(inp["q_norm_g"])[0][None, :], (128, 64)))
    sh["gk"] = f(np.broadcast_to(f(inp["k_norm_g"])[0][None, :], (128, 64)))
    lam_re, lam_im, log_dt = f(inp["lam_re"])[0], f(inp["lam_im"])[0], f(inp["log_dt"])[0]
    ldt = np.broadcast_to(log_dt[:, :, None], (2, 32, 64))
    A3 = np.stack([lam_re, lam_im, ldt], 0)
    sh["lamA"] = np.ascontiguousarray(A3.transpose(3, 0, 1, 2).reshape(64, 3, 64))
    sh["lamB"] = np.ascontiguousarray(A3.transpose(1, 3, 0, 2).reshape(128, 3, 32))
    b2 = np.stack([f(inp["b_re"])[0], f(inp["b_im"])[0]], 0)
    bA = np.zeros((64, 2, 64, 32), np.float32)
    bA[..., :16] = b2.transpose(3, 0, 1, 2, 4).reshape(64, 2, 64, 16)
    sh["bA"] = bA
    c2 = np.stack([f(inp["c_re"])[0], f(inp["c_im"])[0]], 0)
    cA = np.zeros((64, 2, 64, 32), np.float32)
    cA[..., :16] = c2.transpose(4, 0, 1, 2, 3).reshape(64, 2, 64, 16)
    sh["cA"] = cA
    cB = np.zeros((128, 2, 32, 32), np.float32)
    cB[..., :16] = c2.transpose(1, 4, 0, 2, 3).reshape(128, 2, 32, 16)
    sh["cB"] = cB
    return sh


def core_inputs(inp, sh, c):
    xp = np.ascontiguousarray(inp["x_prompt"], dtype=np.float32)
    xs = np.ascontiguousarray(inp["x_sample"], dtype=np.float32)[0]
    m = dict(sh)
    m["xp"] = xp[c]
    m["xsf"] = xs
    lo = c * LOWN - 128
    xe = np.zeros((LEXT, D), np.float32)
    a, b = max(lo, 0), min(lo + LEXT, LS)
    xe[a - lo:b - lo] = xs[a:b]
    m["xse"] = xe
    misc = np.zeros((128, 8), np.float32)
    misc[:, 0] = lo // 64
    misc[:, 1] = 1.0 if c > 0 else 0.0
    misc[:, 2] = 1.0 if c < 7 else 0.0
    m["misc"] = misc
    oh = np.zeros((128, 2, LS // TC), np.float32)
    cl = lo // TC - 1
    cr = (lo + LEXT) // TC
    if cl >= 0:
        oh[:, 0, cl] = 1.0
    if cr < LS // TC:
        oh[:, 1, cr] = 1.0
    m["oh"] = oh
    return m


def kernel(**inp):
    sh = prep_inputs(inp)
    nc = build()
    in_maps = [core_inputs(inp, sh, c) for c in range(8)]
    res = run_bass_kernel_spmd(nc, in_maps, core_ids=list(range(8)))
    yp = np.stack([res.results[c]["yp"] for c in range(8)], axis=0)
    ys = np.concatenate([res.results[c]["ys"] for c in range(8)], axis=0)[None]
    return yp.astype(np.float32), ys.astype(np.float32)


def phase_attn(nc, P, g, seq, nqb_limit=None):
    nc = NCProxy(nc)
    if seq == "p":
        KTd, Vd, Qd, Sd, xin, X1, L, nqb = g.KTp, g.Vp, g.Qp, g.Sp, g.xp, g.X1p, LP, LP // 128
    else:
        KTd, Vd, Qd, Sd, xin, X1, L, nqb = g.KTs, g.Vs, g.Qs, g.Se, g.xse, g.X1e, LS, LEXT // 128
    nkb = L // 128
    if nqb_limit:
        nqb = min(nqb, nqb_limit)
    with contextlib.ExitStack() as st:
        E = st.enter_context
        KT = E(nc.sbuf_tensor("a_KT", [128, L], BF16))
        V = E(nc.sbuf_tensor("a_V", [128, nkb, 130], BF16))
        wglu = E(nc.sbuf_tensor("a_wglu", [128, 8, 1024], BF16))
        wos = E(nc.sbuf_tensor("a_wos", [128, 8, 1024], BF16))
        woa = E(nc.sbuf_tensor("a_woa", [128, 4, 1024], BF16))
        bglu = E(nc.sbuf_tensor("a_bglu", [128, 8], F32))
        ident = E(nc.sbuf_tensor("a_ident", [128, 128], BF16))
        identf = E(nc.sbuf_tensor("a_identf", [128, 128], F32))
        ones = E(nc.sbuf_tensor("a_ones", [128, 2], BF16))
        epsc = E(nc.sbuf_tensor("a_eps", [128, 1], F32))
        nbglu = E(nc.sbuf_tensor("a_nbglu", [128, 8], F32))
        nhalf = E(nc.sbuf_tensor("a_nhalf", [128, 1], F32))
        ge = E(nc.sbuf_tensor("a_ge", [128, 8, 128], F32))
        Qb = [E(nc.sbuf_tensor("a_Qb%d" % i, [128, 512], BF16)) for i in range(2)]
        PT = [E(nc.sbuf_tensor("a_PT%d" % i, [128, 1024], BF16)) for i in range(3)]
        OT = [E(nc.sbuf_tensor("a_OT%d" % i, [65, 512], F32)) for i in range(2)]
        o = E(nc.sbuf_tensor("a_o", [128, 8, 64], F32))
        junk = E(nc.sbuf_tensor("a_junk", [128, 512], F32))
        on = E(nc.sbuf_tensor("a_on", [128, 512], F32))
        onT = E(nc.sbuf_tensor("a_onT", [128, 4, 128], BF16))
        stl = [E(nc.sbuf_tensor("a_st%d" % i, [128, 8, 128], BF16)) for i in range(2)]
        gt = E(nc.sbuf_tensor("a_gt", [128, 8, 128], BF16))
        s2 = E(nc.sbuf_tensor("a_s2", [128, 8, 128], BF16))
        s2sq = E(nc.sbuf_tensor("a_s2sq", [128, 8, 128], BF16))
        xt = [E(nc.sbuf_tensor("a_xt%d" % i, [128, D], F32)) for i in range(2)]
        x1 = [E(nc.sbuf_tensor("a_x1%d" % i, [128, D], F32)) for i in range(2)]
        stat = [E(nc.sbuf_tensor("a_stat%d" % i, [128, 32], F32)) for i in range(2)]
        pS = E(nc.psum_tensor("a_pS", [128, 2048], F32))
        pO = [E(nc.psum_tensor("a_pO%d" % i, [128, 512], F32)) for i in range(2)]
        pG = [E(nc.psum_tensor("a_pG%d" % i, [128, 512], F32)) for i in range(2)]
        blk = E(nc.Block())
        P.begin()
        EPS_AP[0] = epsc[:, 0:1]
        P.add("pool", lambda e: e.memset(epsc[:], EPS), [], ["eps"])
        P.add("pool", lambda e: e.memset(ones[:], 1.0), [], ["ones"])
        nk4 = 8
        for i in range(nk4):
            w = L // nk4
            P.dma("sp", KT[:, i * w:(i + 1) * w], KTd[:, i * w:(i + 1) * w], writes=["KT"])
            kw = nkb // nk4
            P.dma("sp", V[:, i * kw:(i + 1) * kw, :], Vd[:, i * kw:(i + 1) * kw, :], writes=["V"])
        P.dma("sp", wglu[:], g.Wglu[:, :, :], writes=["wglu"])
        P.dma("sp", wos[:], g.Wos[:, :, :], writes=["wos"])
        P.dma("sp", woa[:], g.Woa[:, :, :], writes=["woa"])
        P.dma("sp", bglu[:], g.bglu[:, :], writes=["bglu"])
        ts(P, "dve", nbglu[:], bglu[:], -1.0, ALU.mult, ["bglu"], ["nbglu"])
        P.add("pool", lambda e: e.memset(nhalf[:], -0.5), [], ["nhalf"])

        def rsqrt_pow(out, in_, tmp, scale, r, w):
            tn = w[0] + "_t"
            ts(P, "dve", tmp, in_, scale, ALU.mult, r, [tn], s2=EPS, op1=ALU.add)
            tt(P, "pool", out, tmp, nhalf[:, 0:1], ALU.pow, [tn, "nhalf"], w)
        make_ident(nc, P, E, ident, identf)

        def tail_gen(qb, b, S_):
            for j in range(2):
                for hh in range(4):
                    tr(P, pG[j][:, hh * 65:(hh + 1) * 65], OT[j][:, hh * 128:(hh + 1) * 128], identf[0:65, 0:65],
                       ["OT%d" % j, "identf"], ["pG%d" % j])
                P.add("dve", lambda e, j=j, S_=S_: e.reciprocal(out=S_[:, 8 + 4 * j:12 + 4 * j],
                                                                 in_=rap(pG[j], 0, 128, 64, [[65, 4]])),
                      ["pG%d" % j], ["rden%d_%d" % (b, j)])
                tt(P, "dve", o[:, 4 * j:4 * j + 4, :], rap(pG[j], 0, 128, 0, [[65, 4], [1, 64]]),
                   rap(S_, 0, 128, 8 + 4 * j, [[1, 4], [0, 64]]), ALU.mult,
                   ["pG%d" % j, "rden%d_%d" % (b, j)], ["o%d" % j])
            yield
            of = o[:].rearrange("p h d -> p (h d)")
            act(P, junk[:], of, AF.Square, ["o0", "o1"], ["junk", "ssqa%d" % b], accum=S_[:, 0:1])
            rsqrt_pow(S_[:, 2:3], S_[:, 0:1], S_[:, 1:2], 1.0 / 512.0, ["ssqa%d" % b], ["rsa%d" % b])
            ts(P, "dve", on[:], of, S_[:, 2:3], ALU.mult, ["o0", "o1", "rsa%d" % b], ["on"])
            for k in range(4):
                tr(P, pG[0][:, k * 128:(k + 1) * 128], on[:, k * 128:(k + 1) * 128], identf[:], ["on", "identf"], ["pG0"])
            cp(P, "act", onT[:], pG[0][:].rearrange("p (k t) -> p k t", k=4), ["pG0"], ["onT"])
            yield
            sname = "st%d" % b
            for m in range(8):
                for k in range(8):
                    mm(P, pG[m // 4][:, (m % 4) * 128:(m % 4 + 1) * 128], wglu[:, k, m * 128:(m + 1) * 128],
                       stl[b][:, k, :], k == 0, k == 7, ["wglu", sname], ["pG%d" % (m // 4)])
                if m % 2 == 1:
                    yield
            for m in range(8):
                act(P, ge[:, m, :], pG[m // 4][:, (m % 4) * 128:(m % 4 + 1) * 128], AF.Exp,
                    ["pG%d" % (m // 4), "nbglu"], ["ge%d" % m], scale=-1.0, bias=nbglu[:, m:m + 1])
            gen_ = ["ge%d" % m for m in range(8)]
            ts(P, "dve", ge[:], ge[:], 1.0, ALU.add, gen_, gen_)
            P.add("dve", lambda e: e.reciprocal(out=gt[:], in_=ge[:]), gen_, ["gt"])
            tt(P, "dve", s2[:], stl[b][:], gt[:], ALU.mult, [sname, "gt"], ["s2"])
            tt(P, "pool", s2sq[:], s2[:], s2[:], ALU.mult, ["s2"], ["s2sq"])
            yield
            for k in range(8):
                mm(P, pG[1][:, 0:1], s2sq[:, k, :], ones[:, 0:1], k == 0, k == 7, ["s2sq", "ones"], ["pG1"])
            rsqrt_pow(S_[:, 5:6], pG[1][:, 0:1], S_[:, 4:5], 1.0 / 512.0, ["pG1"], ["rss%d" % b])
            yield
            for half in range(2):
                for k in range(8):
                    mm(P, pG[half][:], s2[:, k, :], wos[:, k, half * 512:(half + 1) * 512], k == 0, k == 7,
                       ["s2", "wos"], ["pG%d" % half])
                stt(P, x1[b][:, half * 512:(half + 1) * 512], pG[half][:], S_[:, 5:6],
                    xt[b][:, half * 512:(half + 1) * 512], ALU.mult, ALU.add,
                    ["pG%d" % half, "rss%d" % b, "xt%d" % b], ["x1a%d_%d" % (b, half)])
                yield
            for half in range(2):
                for k in range(4):
                    mm(P, pG[half][:], onT[:, k, :], woa[:, k, half * 512:(half + 1) * 512], k == 0, k == 3,
                       ["onT", "woa"], ["pG%d" % half])
                tt(P, "dve", x1[b][:, half * 512:(half + 1) * 512], x1[b][:, half * 512:(half + 1) * 512],
                   pG[half][:], ALU.add, ["pG%d" % half, "x1a%d_%d" % (b, half)], ["x1_%d_%d" % (b, half)])
                yield
            P.dma("pool", X1[qb * 128:(qb + 1) * 128, :], x1[b][:],
                  reads=["x1_%d_0" % b, "x1_%d_1" % b, "x1a%d_0" % b, "x1a%d_1" % b])

        pending = []
        npt = 0
        for qb in range(nqb):
            b = qb % 2
            S_ = stat[b]
            P.dma("sp", Qb[b][:], Qd[qb, :, :], writes=["Qb%d" % b])
            P.dma("sp", stl[b][:], Sd[:, :, qb * 128:(qb + 1) * 128].rearrange("u p t -> p u t"), writes=["st%d" % b])
            P.dma("sp", xt[b][:], xin[qb * 128:(qb + 1) * 128, :], writes=["xt%d" % b])
            def emitS(kb):
                for j in range(2):
                    c0 = (kb % 2) * 1024 + j * 512
                    mm(P, pS[:, c0:c0 + 512], KT[64 * j:64 * j + 64, kb * 128:(kb + 1) * 128],
                       Qb[b][64 * j:64 * j + 64, :], True, True, ["KT", "Qb%d" % b], ["pS%d" % (kb % 2)])

            emitS(0)
            emitS(1)
            for kb in range(nkb):
                if pending and kb >= 2 and kb % 3 == 2:
                    try:
                        next(pending[0])
                    except StopIteration:
                        pending.pop(0)
                pt = PT[npt % 3]
                ptn = "PT%d" % (npt % 3)
                npt += 1
                act(P, pt[:], pS[:, (kb % 2) * 1024:(kb % 2) * 1024 + 1024], AF.Exp, ["pS%d" % (kb % 2)], [ptn],
                    scale=0.125)
                if kb + 2 < nkb:
                    emitS(kb + 2)
                for j in range(2):
                    mm(P, pO[j][0:65, :], V[:, kb, j * 65:(j + 1) * 65], pt[:, j * 512:(j + 1) * 512],
                       kb == 0, kb == nkb - 1, [ptn, "V"], ["pO%d" % j])
            for j in range(2):
                cp(P, "act" if j else "dve", OT[j][:], pO[j][0:65, :], ["pO%d" % j], ["OT%d" % j])
            pending.append(tail_gen(qb, b, S_))
        for gen_ in pending:
            for _ in gen_:
                pass
        P.finish(blk)


def phase_ffn(nc, P, g, seq, mlo, mhi, first, last, ntile_limit=None):
    nc = NCProxy(nc)
    if seq == "p":
        X1, X2, Y, ntile, t_lo, t_hi = g.X1p, g.X2p, g.yp, LP // 128, 0, LP // 128
    else:
        X1, X2, Y, ntile, t_lo, t_hi = g.X1e, g.X2e, g.ys, LEXT // 128, 1, LEXT // 128 - 1
    if ntile_limit:
        ntile = min(ntile, ntile_limit)
        t_hi = min(t_hi, ntile)
    nm = mhi - mlo
    with contextlib.ExitStack() as st:
        E = st.enter_context
        wg = E(nc.sbuf_tensor("f_wg", [128, 8, nm * 128], BF16))
        wup = E(nc.sbuf_tensor("f_wup", [128, 8, nm * 128], BF16))
        wdn = E(nc.sbuf_tensor("f_wdn", [128, nm, 1024], BF16))
        cw = E(nc.sbuf_tensor("f_cw", [128, NFF, 3], F32))
        cb = E(nc.sbuf_tensor("f_cb", [128, NFF], F32))
        gf = E(nc.sbuf_tensor("f_gf", [128, D], F32))
        misc = E(nc.sbuf_tensor("f_misc", [128, 8], F32))
        ident = E(nc.sbuf_tensor("f_ident", [128, 128], BF16))
        identf = E(nc.sbuf_tensor("f_identf", [128, 128], F32))
        epsc = E(nc.sbuf_tensor("f_eps", [128, 1], F32))
        x1 = [E(nc.sbuf_tensor("f_x1%d" % i, [128, D], F32)) for i in range(3)]
        xr = [E(nc.sbuf_tensor("f_xr%d" % i, [128, D], F32)) for i in range(3)] if not first else x1
        junk = E(nc.sbuf_tensor("f_junk", [128, D], F32))
        h2 = E(nc.sbuf_tensor("f_h2", [128, D], BF16))
        h2T = E(nc.sbuf_tensor("f_h2T", [128, 8, 128], BF16))
        Gb = [E(nc.sbuf_tensor("f_G%d" % i, [128, nm, 130], BF16)) for i in range(2)]
        Ub = [E(nc.sbuf_tensor("f_U%d" % i, [128, nm, 128], BF16)) for i in range(2)]
        tcv = [E(nc.sbuf_tensor("f_tc%d" % i, [128, 128], F32)) for i in range(2)]
        sg = [E(nc.sbuf_tensor("f_sg%d" % i, [128, 128], F32)) for i in range(2)]
        A = E(nc.sbuf_tensor("f_A", [128, nm, 128], BF16))
        y = [E(nc.sbuf_tensor("f_y%d" % i, [128, D], F32)) for i in range(2)]
        stat = [E(nc.sbuf_tensor("f_stat%d" % i, [128, 16], F32)) for i in range(2)]
        pT = E(nc.psum_tensor("f_pT", [128, 1024], BF16))
        pG = [E(nc.psum_tensor("f_pG%d" % i, [128, 512], F32)) for i in range(2)]
        pU = [E(nc.psum_tensor("f_pU%d" % i, [128, 512], F32)) for i in range(2)]
        pD = [E(nc.psum_tensor("f_pD%d" % i, [128, 512], F32)) for i in range(2)]
        blk = E(nc.Block())
        P.begin()
        EPS_AP[0] = epsc[:, 0:1]
        P.add("pool", lambda e: e.memset(epsc[:], EPS), [], ["eps"])
        for k in range(8):
            P.dma("sp", wg[:, k, :], g.Wg[:, k, mlo * 128:mhi * 128], writes=["wg"])
            P.dma("sp", wup[:, k, :], g.Wup[:, k, mlo * 128:mhi * 128], writes=["wup"])
        P.dma("sp", wdn[:], g.Wdn[:, mlo:mhi, :], writes=["wdn"])
        P.dma("sp", cw[:], g.cw[:, :, :], writes=["cw"])
        P.dma("sp", cb[:], g.cb[:, :], writes=["cb"])
        P.dma("sp", gf[:], g.gf[:, :], writes=["gf"])
        P.dma("sp", misc[:], g.misc[:, :], writes=["misc"])
        make_ident(nc, P, E, ident, identf)

        def finalize(i):
            b, b3 = i % 2, i % 3
            G, U_ = Gb[b], Ub[b]
            for mi in range(nm):
                m = mlo + mi
                t, s = tcv[mi % 2], sg[mi % 2]
                tn, sn = "tc%d" % (mi % 2), "sg%d" % (mi % 2)
                gin = ["G%d" % b, "Gl%d" % b, "Gr%d" % b, "cw", "cb"]
                ts(P, "dve", t[:], G[:, mi, 0:128], cw[:, m, 0:1], ALU.mult, gin, [tn], s2=cb[:, m:m + 1], op1=ALU.add)
                stt(P, t[:], G[:, mi, 1:129], cw[:, m, 1:2], t[:], ALU.mult, ALU.add, gin + [tn], [tn])
                stt(P, t[:], G[:, mi, 2:130], cw[:, m, 2:3], t[:], ALU.mult, ALU.add, gin + [tn], [tn])
                act(P, s[:], t[:], AF.Silu, [tn], [sn])
                tt(P, "pool", A[:, mi, :], s[:], U_[:, mi, :], ALU.mult, [sn, "U%d" % b], ["A%d" % mi])
            for half in range(2):
                for mi in range(nm):
                    mm(P, pD[half][:], A[:, mi, :], wdn[:, mi, half * 512:(half + 1) * 512], mi == 0, mi == nm - 1,
                       ["A%d" % mi, "wdn"], ["pD%d" % half])
            yb = y[b]
            for half in range(2):
                hs = slice(half * 512, (half + 1) * 512)
                tt(P, "dve", yb[:, hs], pD[half][:], xr[b3][:, hs], ALU.add,
                   ["pD%d" % half, ("x1%d" if first else "xr%d") % b3],
                   ["y%d_%d" % (b, half)])
            yn = ["y%d_0" % b, "y%d_1" % b]
            if last:
                S_ = stat[b]
                act(P, junk[:], yb[:], AF.Square, yn, ["junk", "fssq%d" % b], accum=S_[:, 4:5])
                rsqrt_col(P, S_[:, 6:7], S_[:, 4:5], S_[:, 5:6], 1.0 / D, ["fssq%d" % b, "eps"], ["frs%d" % b])
                stt(P, yb[:], yb[:], S_[:, 6:7], gf[:], ALU.mult, ALU.mult, yn + ["frs%d" % b, "gf"], ["yo%d" % b])
                P.dma("pool", Y[(i - t_lo) * 128:(i - t_lo + 1) * 128, :], yb[:], reads=["yo%d" % b] + yn)
            else:
                P.dma("pool", X2[i * 128:(i + 1) * 128, :], yb[:], reads=yn)

        for i in range(ntile):
            b, b3 = i % 2, i % 3
            S_ = stat[b]
            P.dma("sp", x1[b3][:], X1[i * 128:(i + 1) * 128, :], writes=["x1%d" % b3])
            if not first:
                P.dma("sp", xr[b3][:], X2[i * 128:(i + 1) * 128, :], writes=["xr%d" % b3])
            else:
                pass
            xn1 = "x1%d" % b3
            act(P, junk[:], x1[b3][:], AF.Square, [xn1], ["junk", "ssq%d" % b], accum=S_[:, 0:1])
            rsqrt_col(P, S_[:, 2:3], S_[:, 0:1], S_[:, 1:2], 1.0 / D, ["ssq%d" % b, "eps"], ["rs%d" % b])
            ts(P, "dve", h2[:], x1[b3][:], S_[:, 2:3], ALU.mult, [xn1, "rs%d" % b], ["h2"])
            for kc in range(8):
                tr(P, pT[:, kc * 128:(kc + 1) * 128], h2[:, kc * 128:(kc + 1) * 128], ident[:], ["h2", "ident"], ["pT"])
            cp(P, "act", h2T[:], pT[:].rearrange("p (k t) -> p k t", k=8), ["pT"], ["h2T"])
            G, U_ = Gb[b], Ub[b]
            need_u = t_lo <= i < t_hi
            for m0 in range(0, nm, 4):
                mc = min(4, nm - m0)
                pg, pu = pG[(m0 // 4) % 2], pU[(m0 // 4) % 2]
                pgn, pun = "pG%d" % ((m0 // 4) % 2), "pU%d" % ((m0 // 4) % 2)
                for mi in range(m0, m0 + mc):
                    for kc in range(8):
                        mm(P, pg[:, (mi - m0) * 128:(mi - m0 + 1) * 128], wg[:, kc, mi * 128:(mi + 1) * 128],
                           h2T[:, kc, :], kc == 0, kc == 7, ["wg", "h2T"], [pgn])
                cp(P, "act", G[:, m0:m0 + mc, 1:129], pg[:, 0:mc * 128].rearrange("p (m t) -> p m t", m=mc),
                   [pgn], ["G%d" % b])
                if need_u:
                    for mi in range(m0, m0 + mc):
                        for kc in range(8):
                            mm(P, pu[:, (mi - m0) * 128:(mi - m0 + 1) * 128], wup[:, kc, mi * 128:(mi + 1) * 128],
                               h2T[:, kc, :], kc == 0, kc == 7, ["wup", "h2T"], [pun])
                    cp(P, "dve", U_[:, m0:m0 + mc, :], pu[:, 0:mc * 128].rearrange("p (m t) -> p m t", m=mc),
                       [pun], ["U%d" % b])
            if i == 0:
                P.add("pool", lambda e, G=G: e.memset(G[:, :, 0:1], 0.0), [], ["Gl%d" % b])
            else:
                Gp = Gb[1 - b]
                if seq == "e" and i == 1:
                    ts(P, "dve", G[:, :, 0:1], Gp[:, :, 128:129], misc[:, 1:2], ALU.mult, ["G%d" % (1 - b), "misc"],
                       ["Gl%d" % b])
                else:
                    cp(P, "dve", G[:, :, 0:1], Gp[:, :, 128:129], ["G%d" % (1 - b)], ["Gl%d" % b])
                if seq == "e" and i == ntile - 1:
                    ts(P, "dve", Gp[:, :, 129:130], G[:, :, 1:2], misc[:, 2:3], ALU.mult, ["G%d" % b, "misc"],
                       ["Gr%d" % (1 - b)])
                else:
                    cp(P, "dve", Gp[:, :, 129:130], G[:, :, 1:2], ["G%d" % b], ["Gr%d" % (1 - b)])
                if t_lo <= i - 1 < t_hi:
                    finalize(i - 1)
        if t_lo <= ntile - 1 < t_hi:
            bl = (ntile - 1) % 2
            P.add("pool", lambda e: e.memset(Gb[bl][:, :, 129:130], 0.0), [], ["Gr%d" % bl])
            finalize(ntile - 1)
        P.finish(blk)


def s5_scalars(nc, P, E, pre, lam, npart, nf, sh):
    NE = TC + 1
    T = lambda n, shp, dt=F32: E(nc.sbuf_tensor(pre + n, shp, dt))
    V = lambda t: t[0:npart, :, 0:nf]
    dt_, lrdt, ang = T("dt", [npart, nf]), T("lrdt", [npart, nf]), T("ang", [npart, nf])
    ei, ev = T("ei", [npart, NE], I32), T("ev", [npart, NE])
    earg, t1, t2, ni, magp = sh
    PR, PI = T("PR", [npart, NE, nf]), T("PI", [npart, NE, nf])
    den, nr, za, zb = T("den", [npart, nf]), T("nr", [npart, nf]), T("za", [npart, nf]), T("zb", [npart, nf])
    ZR, ZI = T("ZR", [npart, nf]), T("ZI", [npart, nf])
    N = pre
    lr, li, ldt = lam[:, 0, :], lam[:, 1, :], lam[:, 2, :]
    ek, er, eni = T("ek", [npart, nf]), T("er", [npart, nf]), T("eni", [npart, nf], I32)
    exp_acc(P, dt_[:], ldt, ek[:], er[:], eni[:], N + "lam", N + "dt", N + "x")
    tt(P, "dve", lrdt[:], lr, dt_[:], ALU.mult, [N + "lam", N + "dt"], [N + "lrdt"])
    tt(P, "dve", ang[:], li, dt_[:], ALU.mult, [N + "lam", N + "dt"], [N + "ang"])
    P.add("pool", lambda e: e.iota(ei[:], [[1, NE]], base=0, channel_multiplier=0), [], [N + "ei"])
    cp(P, "dve", ev[:], ei[:], [N + "ei"], [N + "ev"])
    evb = rap(ev, 0, npart, 0, [[1, NE], [0, nf]])
    tt(P, "dve", V(earg), evb, rap(lrdt, 0, npart, 0, [[0, NE], [1, nf]]), ALU.mult, [N + "ev", N + "lrdt"], ["sh_earg"])
    act(P, V(magp), V(earg), AF.Exp, ["sh_earg"], ["sh_magp"])
    tt(P, "dve", V(earg), evb, rap(ang, 0, npart, 0, [[0, NE], [1, nf]]), ALU.mult,
       [N + "ev", N + "ang", "sh_magp"], ["sh_earg"])
    for which, off, dst in ((0, 0.25, PR), (1, 0.0, PI)):
        ts(P, "dve", V(t1), V(earg), 1.0 / TWO_PI, ALU.mult, ["sh_earg"], ["sh_t1"], s2=off, op1=ALU.add)
        sin_frac(P, dst[:], V(t1), V(t2), V(ni), "sh_t1", "sh_t2", "sh_ni", N + "trig%d" % which)
    tt(P, "dve", PR[:], V(magp), PR[:], ALU.mult, ["sh_magp", N + "trig0"], [N + "PR"])
    tt(P, "dve", PI[:], V(magp), PI[:], ALU.mult, ["sh_magp", N + "trig1"], [N + "PI"])
    ar, ai = PR[:, 1, :], PI[:, 1, :]
    tt(P, "dve", den[:], lr, lr, ALU.mult, [N + "lam"], [N + "den0"])
    tt(P, "dve", nr[:], li, li, ALU.mult, [N + "lam"], [N + "nr0"])
    tt(P, "dve", den[:], den[:], nr[:], ALU.add, [N + "den0", N + "nr0"], [N + "den1"])
    P.add("dve", lambda e: e.reciprocal(out=den[:], in_=den[:]), [N + "den1"], [N + "rden"])
    ts(P, "dve", nr[:], ar, -1.0, ALU.add, [N + "PR", N + "den1"], [N + "nr"])
    tt(P, "dve", za[:], nr[:], lr, ALU.mult, [N + "nr", N + "lam"], [N + "za0"])
    tt(P, "dve", zb[:], ai, li, ALU.mult, [N + "PI", N + "lam"], [N + "zb0"])
    tt(P, "dve", za[:], za[:], zb[:], ALU.add, [N + "za0", N + "zb0"], [N + "za1"])
    tt(P, "dve", ZR[:], za[:], den[:], ALU.mult, [N + "za1", N + "rden"], [N + "ZR"])
    tt(P, "dve", za[:], ai, lr, ALU.mult, [N + "PI", N + "lam", N + "ZR"], [N + "za2"])
    tt(P, "dve", zb[:], nr[:], li, ALU.mult, [N + "nr", N + "lam", N + "za1"], [N + "zb2"])
    tt(P, "dve", za[:], za[:], zb[:], ALU.subtract, [N + "za2", N + "zb2"], [N + "za3"])
    tt(P, "dve", ZI[:], za[:], den[:], ALU.mult, [N + "za3", N + "rden"], [N + "ZI"])
    return PR, PI, ZR, ZI


def phase_s5_derive(nc, P, g):
    nc = NCProxy(nc)
    with contextlib.ExitStack() as st:
        E = st.enter_context
        lamA = E(nc.sbuf_tensor("d_lamA", [64, 3, 64], F32))
        lamB = E(nc.sbuf_tensor("d_lamB", [128, 3, 32], F32))
        bA = E(nc.sbuf_tensor("d_bA", [64, 2, 64, 32], F32))
        cA = E(nc.sbuf_tensor("d_cA", [64, 2, 64, 32], F32))
        cB = E(nc.sbuf_tensor("d_cB", [128, 2, 32, 32], F32))
        dsk = E(nc.sbuf_tensor("d_dsk", [128, 8], F32))
        BbR = E(nc.sbuf_tensor("d_BbR", [64, 64, 32], F32))
        BbI = E(nc.sbuf_tensor("d_BbI", [64, 64, 32], F32))
        tA = E(nc.sbuf_tensor("d_tA", [64, 64, 32], F32))
        Wre = [E(nc.sbuf_tensor("d_Wre%d" % i, [64, 2, 4, 32], F32)) for i in range(2)]
        Wim = [E(nc.sbuf_tensor("d_Wim%d" % i, [64, 2, 4, 32], F32)) for i in range(2)]
        tW = [E(nc.sbuf_tensor("d_tW%d" % i, [64, 2, 4, 32], F32)) for i in range(2)]
        Rre = [E(nc.sbuf_tensor("d_Rre%d" % i, [128, 4, 32], F32)) for i in range(2)]
        Rim = [E(nc.sbuf_tensor("d_Rim%d" % i, [128, 4, 32], F32)) for i in range(2)]
        tR = [E(nc.sbuf_tensor("d_tR%d" % i, [128, 4, 32], F32)) for i in range(2)]
        VTst = E(nc.sbuf_tensor("d_VTst", [128, 2 * 2 * TC * 64], BF16))
        BDst = E(nc.sbuf_tensor("d_BDst", [128, 2 * TC * 128], BF16))
        Rst = E(nc.sbuf_tensor("d_Rst", [128, 4 * 2 * TC * 32], BF16))
        ident = E(nc.sbuf_tensor("d_ident", [128, 128], BF16))
        identf = E(nc.sbuf_tensor("d_identf", [128, 128], F32))
        mask = E(nc.sbuf_tensor("d_mask", [128, 128], F32))
        mi_ = E(nc.sbuf_tensor("d_mi", [128, 132], I32))
        mf = E(nc.sbuf_tensor("d_mf", [128, 132], F32))
        AT = E(nc.sbuf_tensor("d_AT", [128, 2, 32], F32))
        pV = [E(nc.psum_tensor("d_pV%d" % i, [128, 512], F32)) for i in range(2)]
        pB = [E(nc.psum_tensor("d_pB%d" % i, [128, 512], F32)) for i in range(2)]
        blk = E(nc.Block())
        P.begin()
        P.dma("sp", lamA[:], g.lamA[:, :, :], writes=["Alam"])
        P.dma("sp", lamB[:], g.lamB[:, :, :], writes=["Blam"])
        P.dma("sp", bA[:], g.bA[:, :, :, :], writes=["bA"])
        P.dma("sp", cA[:], g.cA[:, :, :, :], writes=["cA"])
        P.dma("sp", cB[:], g.cB[:, :, :, :], writes=["cB"])
        P.dma("sp", dsk[:], g.dskip[:, :], writes=["dsk"])
        make_ident(nc, P, E, ident, identf)
        P.add("pool", lambda e: e.iota(mi_[:, 0:128], [[1, 128]], base=0, channel_multiplier=0), [], ["mi0"])
        P.add("pool", lambda e: e.iota(mi_[:, 128:129], [[0, 1]], base=0, channel_multiplier=1), [], ["mi1"])
        ts(P, "dve", mi_[:, 0:129], mi_[:, 0:129], 5, ALU.arith_shift_right, ["mi0", "mi1"], ["mi2"])
        cp(P, "dve", mf[:, 0:129], mi_[:, 0:129], ["mi2"], ["mf"])
        ts(P, "dve", mask[:], mf[:, 0:128], mf[:, 128:129], ALU.is_equal, ["mf"], ["mask"])
        shf = [E(nc.sbuf_tensor("d_sh%d" % i, [128, TC + 1, 64], I32 if i == 3 else F32)) for i in range(5)]
        if STOP <= 0:
            P.finish(blk)
            return
        PRA, PIA, ZRA, ZIA = s5_scalars(nc, P, E, "A", lamA, 64, 64, shf)
        if STOP <= 1:
            P.finish(blk)
            return
        PRB, PIB, _, _ = s5_scalars(nc, P, E, "B", lamB, 128, 32, shf)
        cp(P, "dve", AT[:, 0, :], PRB[:, TC, :], ["BPR"], ["AT0"])
        cp(P, "dve", AT[:, 1, :], PIB[:, TC, :], ["BPI"], ["AT1"])
        P.dma("sp", g.ATd[:, :, :], AT[:], reads=["AT0", "AT1"])
        if STOP <= 2:
            P.finish(blk)
            return
        zrb = rap(ZRA, 0, 64, 0, [[1, 64], [0, 32]])
        zib = rap(ZIA, 0, 64, 0, [[1, 64], [0, 32]])
        tt(P, "dve", BbR[:], bA[:, 0, :, :], zrb, ALU.mult, ["bA", "AZR"], ["BbR0"])
        tt(P, "dve", tA[:], bA[:, 1, :, :], zib, ALU.mult, ["bA", "AZI"], ["tA"])
        tt(P, "dve", BbR[:], BbR[:], tA[:], ALU.subtract, ["BbR0", "tA"], ["BbR"])
        tt(P, "dve", BbI[:], bA[:, 1, :, :], zrb, ALU.mult, ["bA", "AZR"], ["BbI0"])
        tt(P, "dve", tA[:], bA[:, 0, :, :], zib, ALU.mult, ["bA", "AZI", "BbR"], ["tA2"])
        tt(P, "dve", BbI[:], BbI[:], tA[:], ALU.add, ["BbI0", "tA2"], ["BbI"])
        ts(P, "dve", cA[:, 1, :, :], cA[:, 1, :, :], -1.0, ALU.mult, ["cA"], ["cAn"])
        if STOP <= 3:
            P.finish(blk)
            return
        n = 0
        for ut in range(8 if STOP > 10 else 1):
            g0 = 4 * ut
            for e in range(TC + 1):
                b = n % 2
                n += 1
                if e < TC:
                    def psel(tn):
                        return rap(tn, 0, 64, e * 64 + g0, [[32, 2], [1, 4], [0, 32]])
                    BR4 = rap(BbR, 0, 64, g0 * 32, [[32 * 32, 2], [32, 4], [1, 32]])
                    BI4 = rap(BbI, 0, 64, g0 * 32, [[32 * 32, 2], [32, 4], [1, 32]])
                    wr, wi, tw = Wre[b], Wim[b], tW[b]
                    tt(P, "dve", wr[:], BR4, psel(PRA), ALU.mult, ["BbR", "APR"], ["wr%d" % b, "Wre%d" % b])
                    tt(P, "pool", tw[:], BI4, psel(PIA), ALU.mult, ["BbI", "API"], ["tw%d" % b, "tw2%d" % b])
                    tt(P, "dve", wr[:], wr[:], tw[:], ALU.subtract, ["wr%d" % b, "tw%d" % b], ["Wre%d" % b])
                    tt(P, "dve", wi[:], BI4, psel(PRA), ALU.mult, ["BbI", "APR"], ["wi%d" % b, "Wim%d" % b])
                    tt(P, "pool", tw[:], BR4, psel(PIA), ALU.mult, ["BbR", "API", "Wre%d" % b], ["tw2%d" % b])
                    tt(P, "dve", wi[:], wi[:], tw[:], ALU.add, ["wi%d" % b, "tw2%d" % b], ["Wim%d" % b])
                    if STOP <= 4:
                        continue
                    pv = pV[b]
                    for d in range(2):
                        for part, wsrc in ((0, wr), (1, wi)):
                            col = (d * 2 + part) * 64
                            P.add("pe", lambda e_, pv=pv, col=col, wsrc=wsrc, d=d: e_.transpose(
                                pv[:, col:col + 64], wsrc[:, d, :, :].rearrange("p g h -> p (g h)"),
                                identf[0:64, 0:64]), ["Wre%d" % b, "Wim%d" % b, "identf"], ["pV%d" % b])
                    for d in range(2):
                        j = (TC - 1 - e) if d == 0 else e
                        outap = rap(VTst, 0, 128, (d * 2 * TC + j) * 64, [[TC * 64, 2], [1, 64]])
                        inap = rap(pv, 0, 128, d * 128, [[64, 2], [1, 64]])
                        cp(P, "act", outap, inap, ["pV%d" % b], ["VTst"])
                    if STOP <= 5:
                        continue
                    pb = pB[b]
                    for d in range(2):
                        col = d * 128
                        gsl = slice(d * 32 + g0, d * 32 + g0 + 4)
                        mm(P, pb[:, col:col + 128], wr[:, d, :, :].rearrange("p g h -> p (g h)"),
                           cA[:, 0, gsl, :].rearrange("p g h -> p (g h)"), True, False,
                           ["Wre%d" % b, "cA", "cAn"], ["pB%d" % b])
                        mm(P, pb[:, col:col + 128], wi[:, d, :, :].rearrange("p g h -> p (g h)"),
                           cA[:, 1, gsl, :].rearrange("p g h -> p (g h)"), False, True,
                           ["Wim%d" % b, "cA", "cAn"], ["pB%d" % b])
                    outap = rap(BDst, 0, 128, e * 128, [[TC * 128, 2], [1, 128]])
                    tt(P, "dve", outap, pb[:, 0:256].rearrange("p (d c) -> p d c", d=2),
                       rap(mask, 0, 128, 0, [[0, 2], [1, 128]]), ALU.mult, ["pB%d" % b, "mask"], ["BDst"])
                if e >= 1 and STOP > 6:
                    CR = cB[:, 0, g0:g0 + 4, :]
                    CI = cB[:, 1, g0:g0 + 4, :]
                    prb = rap(PRB, 0, 128, e * 32 + g0, [[1, 4], [0, 32]])
                    pib = rap(PIB, 0, 128, e * 32 + g0, [[1, 4], [0, 32]])
                    rr, ri, tr_ = Rre[b], Rim[b], tR[b]
                    tt(P, "dve", rr[:], CR, prb, ALU.mult, ["cB", "BPR"], ["rr%d" % b, "Rre%d" % b])
                    tt(P, "pool", tr_[:], CI, pib, ALU.mult, ["cB", "BPI"], ["tr%d" % b, "tr2%d" % b])
                    tt(P, "dve", rr[:], rr[:], tr_[:], ALU.subtract, ["rr%d" % b, "tr%d" % b], ["Rre%d" % b])
                    tt(P, "dve", ri[:], CR, pib, ALU.mult, ["cB", "BPI"], ["ri%d" % b, "Rim%d" % b])
                    tt(P, "pool", tr_[:], CI, prb, ALU.mult, ["cB", "BPR", "Rre%d" % b], ["tr2%d" % b])
                    stt(P, ri[:], ri[:], -1.0, tr_[:], ALU.mult, ALU.subtract, ["ri%d" % b, "tr2%d" % b], ["Rim%d" % b])
                    for part, src in ((0, rr), (1, ri)):
                        for d in range(2):
                            t = (e - 1) if d == 0 else (TC - e)
                            outap = rap(Rst, 64 * d, 64, (part * TC + t) * 32, [[2 * TC * 32, 4], [1, 32]])
                            inap = rap(src, 64 * d, 64, 0, [[32, 4], [1, 32]])
                            cp(P, "act" if d else "pool", outap, inap, ["Rre%d" % b, "Rim%d" % b], ["Rst"])
            o_ = rap(BDst, 0, 128, 0, [[1, 128]])
            stt(P, o_, identf[:], dsk[:, ut:ut + 1], o_, ALU.mult, ALU.add, ["BDst", "identf", "dsk"], ["BDst"])
            P.dma("sp", g.VTd[ut, :, :], VTst[:], reads=["VTst"])
            P.dma("sp", g.BDd[ut, :, :], BDst[:], reads=["BDst"])
            P.dma("sp", g.Rd[ut, :, :], Rst[:], reads=["Rst"])
        P.finish(blk)


def phase_s5_states(nc, P, g, seq):
    nc = NCProxy(nc)
    U, L = {"p": (g.Up, LP), "s": (g.Us, LS), "e": (g.Ue, LEXT)}[seq]
    nch = L // TC
    PIECE = min(L, 8192)
    npiece = L // PIECE
    ncp = PIECE // TC
    with contextlib.ExitStack() as st:
        E = st.enter_context
        XS = E(nc.sbuf_tensor("s_XS", [128, nch, 32, 2], F32))
        VT = [E(nc.sbuf_tensor("s_VT%d" % i, [128, 2 * 2 * TC * 64], BF16)) for i in range(2)]
        ub = [E(nc.sbuf_tensor("s_u%d" % i, [128, PIECE], BF16)) for i in range(2)]
        AT = E(nc.sbuf_tensor("s_AT", [128, 2, 32], F32))
        A1 = E(nc.sbuf_tensor("s_A1", [128, 32, 2], F32))
        A2 = E(nc.sbuf_tensor("s_A2", [128, 32, 2], F32))
        INIT = E(nc.sbuf_tensor("s_INIT", [128, 32, 2], F32))
        t1 = E(nc.sbuf_tensor("s_t1", [128, 32, 2], F32))
        t2 = E(nc.sbuf_tensor("s_t2", [128, 32, 2], F32))
        ohc = E(nc.sbuf_tensor("s_ohc", [128, LS // TC], F32))
        SB = E(nc.sbuf_tensor("s_SB", [128, nch if seq != "s" else 1, 64], BF16))
        pX = [E(nc.psum_tensor("s_pX%d" % i, [128, 512], F32)) for i in range(8)]
        blk = E(nc.Block())
        P.begin()
        P.dma("sp", AT[:], g.ATd[:, :, :], writes=["AT"])
        cp(P, "dve", A1[:, :, 0], AT[:, 0, :], ["AT"], ["A1a"])
        cp(P, "dve", A1[:, :, 1], AT[:, 0, :], ["AT"], ["A1b"])
        ts(P, "dve", A2[:, :, 0], AT[:, 1, :], -1.0, ALU.mult, ["AT"], ["A2a"])
        cp(P, "dve", A2[:, :, 1], AT[:, 1, :], ["AT"], ["A2b"])
        if seq == "e":
            P.dma("sp", INIT[:], g.INITd[:, :, :], writes=["INIT"])
        else:
            P.add("pool", lambda e: e.memset(INIT[:], 0.0), [], ["INIT"])
        if seq == "s":
            P.dma("sp", ohc[0:64, :], g.oh[0:64, 0, :], writes=["ohc0"])
            P.dma("sp", ohc[64:128, :], g.oh[64:128, 1, :], writes=["ohc1"])
        nld = 0
        nev = 0
        for gh in range(2):
            for ul in range(4):
                ut = 4 * gh + ul
                vb = VT[ut % 2]
                P.dma("sp", vb[:], g.VTd[ut, :, :], writes=["VT%d" % (ut % 2)])
                for pc in range(npiece):
                    k = nld % 2
                    nld += 1
                    P.dma("sp", ub[k][:], U[ut, :, pc * PIECE:(pc + 1) * PIECE], writes=["u%d" % k])
                    for part in range(2):
                        for j in range(TC):
                            for slot in range(4):
                                bank = pX[part * 4 + slot]
                                for d in range(2):
                                    off = ((d * 2 + part) * TC + j) * 64
                                    mm(P, bank[64 * d:64 * d + 64, 0:ncp], vb[32 * slot:32 * slot + 32, off:off + 64],
                                       rap(ub[k], 32 * slot, 32, j, [[TC, ncp]]), j == 0, j == TC - 1,
                                       ["VT%d" % (ut % 2), "u%d" % k], ["pX%d" % (part * 4 + slot)],
                                       tile_position=(32 * slot, 64 * d))
                        for slot in range(4):
                            outap = rap(XS, 0, 128, pc * ncp * 64 + (ut * 4 + slot) * 2 + part, [[64, ncp]])
                            nev += 1
                            cp(P, "act" if nev % 2 else "dve", outap, pX[part * 4 + slot][:, 0:ncp],
                               ["pX%d" % (part * 4 + slot)], ["XS%d" % gh])
            go = 16 * gh * 2
            for eng, p0, order, nm in (("dve", 0, range(nch), "XSf%d" % gh),
                                       ("pool" if gh == 0 else "dve", 64, range(nch - 1, -1, -1), "XSb%d" % gh)):
                prev_t, prev_off = INIT, go
                rd = ["XS%d" % gh, "INIT", "A1a", "A1b", "A2a", "A2b"]
                a1 = rap(A1, p0, 64, go, [[2, 16], [1, 2]])
                a2 = rap(A2, p0, 64, go, [[2, 16], [1, 2]])
                tA = rap(t1, p0, 64, go, [[2, 16], [1, 2]])
                tB = rap(t2, p0, 64, go, [[2, 16], [1, 2]])
                tn = "t" + nm
                for c in order:
                    sp_ = rap(prev_t, p0, 64, prev_off, [[2, 16], [1, 2]])
                    sps = rap(prev_t, p0, 64, prev_off + 1, [[2, 16], [-1, 2]])
                    x = rap(XS, p0, 64, c * 64 + go, [[2, 16], [1, 2]])
                    tt(P, eng, tA, sp_, a1, ALU.mult, rd + [nm], [tn + "1"])
                    tt(P, eng, tB, sps, a2, ALU.mult, rd + [nm], [tn + "2"])
                    tt(P, eng, x, x, tA, ALU.add, [tn + "1", "XS%d" % gh, nm], [nm])
                    tt(P, eng, x, x, tB, ALU.add, [tn + "2", nm], [nm])
                    prev_t, prev_off = XS, c * 64 + go
        fin = ["XSf0", "XSb0", "XSf1", "XSb1"]
        tnames = ["tXSf01", "tXSb01", "tXSf11", "tXSb11", "tXSf02", "tXSb02", "tXSf12", "tXSb12"]
        xs3 = XS[:].rearrange("p c g t -> p c (g t)")
        if seq == "s":
            tt(P, "dve", xs3, xs3, rap(ohc, 0, 128, 0, [[1, nch], [0, 64]]), ALU.mult, fin + ["ohc0", "ohc1"], ["XSm"])
            P.add("dve", lambda e: e.tensor_reduce(out=t1[:].rearrange("p g t -> p (g t)"),
                                                   in_=rap(XS, 0, 128, 0, [[1, 64], [64, nch]]), axis=AX.X, op=ALU.add),
                  ["XSm"] + tnames, ["sel"])
            P.dma("sp", g.INITd[:, :, :], t1[:], reads=["sel"])
        else:
            SBd = g.SBp if seq == "p" else g.SBe
            cp(P, "dve", SB[0:64, 1:nch, :], xs3[0:64, 0:nch - 1, :], fin, ["SB1"])
            cp(P, "pool", SB[64:128, 0:nch - 1, :], xs3[64:128, 1:nch, :], fin, ["SB1b"])
            cp(P, "dve", SB[0:64, 0, :], INIT[0:64].rearrange("p g t -> p (g t)"), ["INIT"], ["SB2"])
            cp(P, "pool", SB[64:128, nch - 1, :], INIT[64:128].rearrange("p g t -> p (g t)"), ["INIT"], ["SB3"])
            P.dma("sp", SBd[:, :], SB[:].rearrange("p c x -> p (c x)"), reads=["SB1", "SB1b", "SB2", "SB3"])
        P.finish(blk)


def phase_s5_out(nc, P, g, seq):
    nc = NCProxy(nc)
    U, Sd, SBd, L = {"p": (g.Up, g.Sp, g.SBp, LP), "e": (g.Ue, g.Se, g.SBe, LEXT)}[seq]
    nch = L // TC
    NPT = 512 if seq == "p" else 256
    ncq = NPT // TC
    npiece = L // NPT
    with contextlib.ExitStack() as st:
        E = st.enter_context
        SB = E(nc.sbuf_tensor("o_SB", [128, nch, 64], BF16))
        YS = E(nc.sbuf_tensor("o_YS", [128, nch, TC], F32))
        R = E(nc.sbuf_tensor("o_R", [128, 4 * 2 * TC * 32], BF16))
        BD = E(nc.sbuf_tensor("o_BD", [128, 2 * TC * 128], BF16))
        ub = [E(nc.sbuf_tensor("o_u%d" % i, [128, L], BF16)) for i in range(2)]
        yv = [E(nc.sbuf_tensor("o_yv%d" % i, [128, NPT], F32)) for i in range(2)]
        x2 = [E(nc.sbuf_tensor("o_x2%d" % i, [128, NPT], F32)) for i in range(2)]
        sgm = [E(nc.sbuf_tensor("o_sg%d" % i, [128, NPT], F32)) for i in range(2)]
        sst = [E(nc.sbuf_tensor("o_ss%d" % i, [128, NPT], BF16)) for i in range(2)]
        pY = [E(nc.psum_tensor("o_pY%d" % i, [128, 512], F32)) for i in range(2)]
        pF = [E(nc.psum_tensor("o_pF%d" % i, [128, 512], F32)) for i in range(2)]
        blk = E(nc.Block())
        P.begin()
        P.dma("sp", SB[:].rearrange("p c x -> p (c x)"), SBd[:, :], writes=["SB"])
        npc = 0
        for ut in range(8):
            ul = ut
            u = ub[ul % 2]
            un = "u%d" % (ul % 2)
            P.dma("sp", u[:], U[ut, :, :], writes=[un])
            P.dma("sp", R[:], g.Rd[ut, :, :], writes=["R"])
            P.dma("sp", BD[:], g.BDd[ut, :, :], writes=["BD"])
            for t in range(TC):
                py = pY[t % 2]
                pn = "pY%d" % (t % 2)
                for part in range(2):
                    for slot in range(4):
                        off = ((slot * 2 + part) * TC + t) * 32
                        soff = (ut * 4 + slot) * 2 + part
                        mm(P, py[32 * slot:32 * slot + 32, 0:nch], R[:, off:off + 32],
                           rap(SB, 0, 128, soff, [[64, nch]]), part == 0, part == 1, ["R", "SB"], [pn],
                           tile_position=(0, 32 * slot))
                cp(P, "act", rap(YS, 0, 128, t, [[TC, nch]]), py[:, 0:nch], [pn], ["YS"])
            for pc in range(npiece if STOP > 21 else 0):
                k = npc % 2
                npc += 1
                pf = pF[k]
                pfn = "pF%d" % k
                u3 = u[:, pc * NPT:(pc + 1) * NPT].rearrange("p (c t) -> p c t", t=TC)
                f3 = pf[:, 0:NPT].rearrange("p (c t) -> p c t", t=TC)
                for d in range(2):
                    for lag in range(TC):
                        lo_, hi_ = (lag, TC) if d == 0 else (0, TC - lag)
                        ro_, rh_ = (0, TC - lag) if d == 0 else (lag, TC)
                        mm(P, f3[:, :, lo_:hi_], BD[:, (d * TC + lag) * 128:(d * TC + lag + 1) * 128],
                           u3[:, :, ro_:rh_], d == 0 and lag == 0, d == 1 and lag == TC - 1, ["BD", un], [pfn])
                if STOP <= 22:
                    continue
                y_, x2_, sg_, ss_ = yv[k], x2[k], sgm[k], sst[k]
                tt(P, "dve", y_[:], pf[:, 0:NPT], YS[:, pc * ncq:(pc + 1) * ncq, :].rearrange("p c t -> p (c t)"),
                   ALU.add, [pfn, "YS"], ["yv%d" % k])
                tt(P, "pool", x2_[:], y_[:], y_[:], ALU.mult, ["yv%d" % k], ["x2%d" % k])
                ts(P, "dve", x2_[:], x2_[:], 0.044715, ALU.mult, ["x2%d" % k], ["x2%d" % k], s2=1.0, op1=ALU.add)
                tt(P, "pool", x2_[:], x2_[:], y_[:], ALU.mult, ["x2%d" % k, "yv%d" % k], ["x2%d" % k])
                act(P, sg_[:], x2_[:], AF.Sigmoid, ["x2%d" % k], ["sg%d" % k], scale=1.5957691216057308)
                tt(P, "dve", ss_[:], y_[:], sg_[:], ALU.mult, ["yv%d" % k, "sg%d" % k], ["ss%d" % k])
                P.dma("pool", Sd[ut, :, pc * NPT:(pc + 1) * NPT], ss_[:], reads=["ss%d" % k])
        P.finish(blk)


def phase_ffn2(nc, P, g, seq, mlo, mhi, first, last):
    nc = NCProxy(nc)
    if seq == "p":
        X1, X2, Y, ntile, t_lo, t_hi = g.X1p, g.X2p, g.yp, LP // 128, 0, LP // 128
    else:
        X1, X2, Y, ntile, t_lo, t_hi = g.X1e, g.X2e, g.ys, LEXT // 128, 1, LEXT // 128 - 1
    nm = mhi - mlo
    nsup = (ntile + 3) // 4
    with contextlib.ExitStack() as st:
        E = st.enter_context
        wg = E(nc.sbuf_tensor("f_wg", [128, 8, nm * 128], BF16))
        wup = E(nc.sbuf_tensor("f_wup", [128, 8, nm * 128], BF16))
        wdn = E(nc.sbuf_tensor("f_wdn", [128, nm, 1024], BF16))
        cw = E(nc.sbuf_tensor("f_cw", [128, NFF, 3], F32))
        cb = E(nc.sbuf_tensor("f_cb", [128, NFF], F32))
        gf = E(nc.sbuf_tensor("f_gf", [128, D], F32))
        misc = E(nc.sbuf_tensor("f_misc", [128, 8], F32))
        ident = E(nc.sbuf_tensor("f_ident", [128, 128], BF16))
        identf = E(nc.sbuf_tensor("f_identf", [128, 128], F32))
        epsc = E(nc.sbuf_tensor("f_eps", [128, 1], F32))
        if first:
            xr = [E(nc.sbuf_tensor("f_xr%d" % i, [128, 4, D], F32)) for i in range(2)]
            x1 = None
        else:
            xr = [E(nc.sbuf_tensor("f_xr%d" % i, [128, 4, D], F32)) for i in range(2)]
            x1 = [E(nc.sbuf_tensor("f_x1%d" % i, [128, D], F32)) for i in range(2)]
        junk = E(nc.sbuf_tensor("f_junk", [128, D], F32))
        h2 = [E(nc.sbuf_tensor("f_h2%d" % i, [128, D], BF16)) for i in range(2)]
        h2T = E(nc.sbuf_tensor("f_h2T", [128, 8, 512], BF16))
        Gb = [E(nc.sbuf_tensor("f_G%d" % i, [128, nm, 514], BF16)) for i in range(2)]
        Ub = [E(nc.sbuf_tensor("f_U%d" % i, [128, nm, 512], BF16)) for i in range(2)]
        tcv = [E(nc.sbuf_tensor("f_tc%d" % i, [128, 512], F32)) for i in range(2)]
        sg = [E(nc.sbuf_tensor("f_sg%d" % i, [128, 512], F32)) for i in range(2)]
        A = E(nc.sbuf_tensor("f_A", [128, nm, 512], BF16))
        y = [E(nc.sbuf_tensor("f_y%d" % i, [128, D], F32)) for i in range(2)]
        stat = [E(nc.sbuf_tensor("f_stat%d" % i, [128, 16], F32)) for i in range(2)]
        pT = E(nc.psum_tensor("f_pT", [128, 1024], BF16))
        pG = [E(nc.psum_tensor("f_pG%d" % i, [128, 512], F32)) for i in range(2)]
        pU = [E(nc.psum_tensor("f_pU%d" % i, [128, 512], F32)) for i in range(2)]
        pD = [E(nc.psum_tensor("f_pD%d" % i, [128, 512], F32)) for i in range(2)]
        blk = E(nc.Block())
        P.begin()
        EPS_AP[0] = epsc[:, 0:1]
        P.add("pool", lambda e: e.memset(epsc[:], EPS), [], ["eps"])
        for k in range(8):
            P.dma("sp", wg[:, k, :], g.Wg[:, k, mlo * 128:mhi * 128], writes=["wg"])
            P.dma("sp", wup[:, k, :], g.Wup[:, k, mlo * 128:mhi * 128], writes=["wup"])
        P.dma("sp", wdn[:], g.Wdn[:, mlo:mhi, :], writes=["wdn"])
        P.dma("sp", cw[:], g.cw[:, :, :], writes=["cw"])
        P.dma("sp", cb[:], g.cb[:, :], writes=["cb"])
        P.dma("sp", gf[:], g.gf[:, :], writes=["gf"])
        P.dma("sp", misc[:], g.misc[:, :], writes=["misc"])
        make_ident(nc, P, E, ident, identf)
        widths = [min(4, ntile - 4 * k) * 128 for k in range(nsup)]
        ny = [0]

        def finalize(k):
            b = k % 2
            W = widths[k]
            G, U_ = Gb[b], Ub[b]
            gin = ["G%d" % b, "Gl%d" % b, "Gr%d" % b, "Gm%d" % b, "cw", "cb"]
            for mi in range(nm):
                m = mlo + mi
                t, s = tcv[mi % 2], sg[mi % 2]
                tn, sn = "tc%d" % (mi % 2), "sg%d" % (mi % 2)
                ts(P, "dve", t[:, 0:W], G[:, mi, 0:W], cw[:, m, 0:1], ALU.mult, gin, [tn], s2=cb[:, m:m + 1], op1=ALU.add)
                stt(P, t[:, 0:W], G[:, mi, 1:W + 1], cw[:, m, 1:2], t[:, 0:W], ALU.mult, ALU.add, gin + [tn], [tn])
                stt(P, t[:, 0:W], G[:, mi, 2:W + 2], cw[:, m, 2:3], t[:, 0:W], ALU.mult, ALU.add, gin + [tn], [tn])
                act(P, s[:, 0:W], t[:, 0:W], AF.Silu, [tn], [sn])
                tt(P, "pool", A[:, mi, 0:W], s[:, 0:W], U_[:, mi, 0:W], ALU.mult, [sn, "U%d" % b], ["A%d" % mi])
            for sub in range(W // 128):
                i = 4 * k + sub
                if not (t_lo <= i < t_hi):
                    continue
                yb = y[ny[0] % 2]
                yi = ny[0] % 2
                ny[0] += 1
                for half in range(2):
                    for mi in range(nm):
                        mm(P, pD[half][:], A[:, mi, sub * 128:(sub + 1) * 128], wdn[:, mi, half * 512:(half + 1) * 512],
                           mi == 0, mi == nm - 1, ["A%d" % mi, "wdn"], ["pD%d" % half])
                for half in range(2):
                    hs = slice(half * 512, (half + 1) * 512)
                    tt(P, "dve", yb[:, hs], pD[half][:], xr[b][:, sub, hs], ALU.add,
                       ["pD%d" % half, "xr%d_%d" % (b, sub)], ["y%d_%d" % (yi, half)])
                yn = ["y%d_0" % yi, "y%d_1" % yi]
                if last:
                    S_ = stat[yi]
                    act(P, junk[:], yb[:], AF.Square, yn, ["junk", "fssq%d" % yi], accum=S_[:, 4:5])
                    rsqrt_col(P, S_[:, 6:7], S_[:, 4:5], S_[:, 5:6], 1.0 / D, ["fssq%d" % yi, "eps"], ["frs%d" % yi])
                    stt(P, yb[:], yb[:], S_[:, 6:7], gf[:], ALU.mult, ALU.mult, yn + ["frs%d" % yi, "gf"], ["yo%d" % yi])
                    P.dma("pool", Y[(i - t_lo) * 128:(i - t_lo + 1) * 128, :], yb[:], reads=["yo%d" % yi] + yn)
                else:
                    P.dma("pool", X2[i * 128:(i + 1) * 128, :], yb[:], reads=yn)

        nt = 0
        for k in range(nsup):
            b = k % 2
            W = widths[k]
            nsub = W // 128
            G, U_ = Gb[b], Ub[b]
            for sub in range(nsub):
                i = 4 * k + sub
                bb = nt % 2
                nt += 1
                S_ = stat[bb]
                if first:
                    xs_, xn1 = xr[b][:, sub, :], "xr%d_%d" % (b, sub)
                    P.dma("sp", xs_, X1[i * 128:(i + 1) * 128, :], writes=[xn1])
                else:
                    xs_, xn1 = x1[bb][:], "x1%d" % bb
                    P.dma("sp", xs_, X1[i * 128:(i + 1) * 128, :], writes=[xn1])
                    P.dma("sp", xr[b][:, sub, :], X2[i * 128:(i + 1) * 128, :], writes=["xr%d_%d" % (b, sub)])
                act(P, junk[:], xs_, AF.Square, [xn1], ["junk", "ssq%d" % bb], accum=S_[:, 0:1])
                rsqrt_col(P, S_[:, 2:3], S_[:, 0:1], S_[:, 1:2], 1.0 / D, ["ssq%d" % bb, "eps"], ["rs%d" % bb])
                ts(P, "dve", h2[bb][:], xs_, S_[:, 2:3], ALU.mult, [xn1, "rs%d" % bb], ["h2%d" % bb])
                for kc in range(8):
                    tr(P, pT[:, kc * 128:(kc + 1) * 128], h2[bb][:, kc * 128:(kc + 1) * 128], ident[:],
                       ["h2%d" % bb, "ident"], ["pT"])
                cp(P, "act", h2T[:, :, sub * 128:(sub + 1) * 128], pT[:].rearrange("p (k t) -> p k t", k=8),
                   ["pT"], ["h2T%d" % sub])
            hn = ["h2T%d" % sub for sub in range(nsub)]
            for mi in range(nm):
                pg, pu = pG[mi % 2], pU[mi % 2]
                for kc in range(8):
                    mm(P, pg[:, 0:W], wg[:, kc, mi * 128:(mi + 1) * 128], h2T[:, kc, 0:W], kc == 0, kc == 7,
                       ["wg"] + hn, ["pG%d" % (mi % 2)])
                cp(P, "act", G[:, mi, 1:W + 1], pg[:, 0:W], ["pG%d" % (mi % 2)], ["G%d" % b])
                for kc in range(8):
                    mm(P, pu[:, 0:W], wup[:, kc, mi * 128:(mi + 1) * 128], h2T[:, kc, 0:W], kc == 0, kc == 7,
                       ["wup"] + hn, ["pU%d" % (mi % 2)])
                cp(P, "dve", U_[:, mi, 0:W], pu[:, 0:W], ["pU%d" % (mi % 2)], ["U%d" % b])
            mk = []
            if seq == "e" and k == 0:
                ts(P, "dve", G[:, :, 128:129], G[:, :, 128:129], misc[:, 1:2], ALU.mult, ["G%d" % b, "misc"], ["Gm%d" % b])
            if seq == "e" and k == nsup - 1:
                c0 = 1 + ((ntile - 1) % 4) * 128
                ts(P, "dve", G[:, :, c0:c0 + 1], G[:, :, c0:c0 + 1], misc[:, 2:3], ALU.mult, ["G%d" % b, "misc"], ["Gm%d" % b])
            if k == 0:
                P.add("pool", lambda e, G=G: e.memset(G[:, :, 0:1], 0.0), [], ["Gl%d" % b])
            else:
                Gp = Gb[1 - b]
                Wp = widths[k - 1]
                cp(P, "dve", G[:, :, 0:1], Gp[:, :, Wp:Wp + 1], ["G%d" % (1 - b), "Gm%d" % (1 - b)], ["Gl%d" % b])
                cp(P, "dve", Gp[:, :, Wp + 1:Wp + 2], G[:, :, 1:2], ["G%d" % b, "Gm%d" % b], ["Gr%d" % (1 - b)])
                finalize(k - 1)
        bl = (nsup - 1) % 2
        Wl = widths[nsup - 1]
        P.add("pool", lambda e: e.memset(Gb[bl][:, :, Wl + 1:Wl + 2], 0.0), [], ["Gr%d" % bl])
        finalize(nsup - 1)
        P.finish(blk)
```

```python
import contextlib
import math
import numpy as np
import concourse.bass as bass
import concourse.mybir as mybir
from concourse.bass_utils import run_bass_kernel_spmd

F32 = mybir.dt.float32
BF16 = mybir.dt.bfloat16
I32 = mybir.dt.int32
ALU = mybir.AluOpType
AF = mybir.ActivationFunctionType
AX = mybir.AxisListType

D = 1024
LP = 8192
LS = 16384
LOWN = 2048
LEXT = LOWN + 256
DFF = 2816
NFF = 22
EPS = 1e-6
NDMA = 32
TC = 32
TWO_PI = 2.0 * math.pi
DEBUG = False
NQB_LIMIT = None
ONLY_P = False
STOP = 99


class Prog:
    ENG = ("pe", "act", "dve", "pool", "sp")

    def __init__(self, nc, sems, dsems):
        self.nc = nc
        self.cnt = {e: 0 for e in self.ENG}
        self.sem = dict(zip(self.ENG, sems))
        self.dsem = dsems
        self.dcnt = [0] * len(dsems)
        self.dnext = 0
        self.seen = {e: {} for e in self.ENG}
        self.begin()

    def begin(self):
        self.ops = {e: [] for e in self.ENG}
        self.lastw = {}
        self.readers = {}
        for e in self.ENG:
            for e2 in self.ENG:
                self.seen[e][e2] = self.cnt[e2]
            for k, c in enumerate(self.dcnt):
                self.seen[e][k] = c

    def _deps(self, e, reads, writes):
        deps = {}

        def need(p):
            if p is not None:
                deps[p[0]] = max(deps.get(p[0], 0), p[1])

        for b in reads:
            need(self.lastw.get(b))
        for b in writes:
            need(self.lastw.get(b))
            for r in self.readers.get(b, ()):
                need(r)
        waits = []
        for k, v in deps.items():
            if k == "pe" and e == "pe":
                continue
            if self.seen[e].get(k, 0) < v:
                self.seen[e][k] = v
                waits.append((k, v))
        return waits

    def _semof(self, k):
        return self.sem[k] if isinstance(k, str) else self.dsem[k]

    def _commit(self, ident, reads, writes):
        for b in reads:
            lst = self.readers.setdefault(b, [])
            lst[:] = [r for r in lst if r[0] != ident[0]] + [ident]
        for b in writes:
            self.lastw[b] = ident
            self.readers[b] = []

    def add(self, e, fn, reads=(), writes=()):
        waits = self._deps(e, reads, writes)
        self.cnt[e] += 1
        ident = (e, self.cnt[e])
        mysem = self.sem[e]
        wl = [(self._semof(k), v) for k, v in waits]

        def run(eng):
            for s, v in wl:
                eng.wait_ge(s, v)
            fn(eng).then_inc(mysem, 1)

        self.ops[e].append(run)
        self._commit(ident, reads, writes)

    def dma(self, q, out, in_, reads=(), writes=()):
        k = self.dnext
        self.dnext = (self.dnext + 1) % len(self.dsem)
        waits = self._deps(q, reads, writes)
        prev = self.dcnt[k]
        if prev and self.seen[q].get(k, 0) < prev:
            self.seen[q][k] = prev
            waits.append((k, prev))
        self.dcnt[k] += 16
        ident = (k, self.dcnt[k])
        wl = [(self._semof(kk), v) for kk, v in waits]
        dsem = self.dsem[k]

        def run(eng):
            for s, v in wl:
                eng.wait_ge(s, v)
            eng.dma_start(out=out, in_=in_).then_inc(dsem, 16)

        self.ops[q].append(run)
        self._commit(ident, reads, writes)

    def finish(self, block):
        finals = [(self.dsem[k], c) for k, c in enumerate(self.dcnt) if c]
        esem = [(self.sem[e], self.cnt[e]) for e in self.ENG if self.cnt[e]]

        def tail(eng):
            for s, v in finals + esem:
                eng.wait_ge(s, v)

        self.ops["sp"].append(tail)
        ops = self.ops

        @block.tensor
        def _(eng):
            for f in ops["pe"]:
                f(eng)

        @block.scalar
        def _(eng):
            for f in ops["act"]:
                f(eng)

        @block.vector
        def _(eng):
            for f in ops["dve"]:
                f(eng)

        @block.gpsimd
        def _(eng):
            for f in ops["pool"]:
                f(eng)

        @block.sync
        def _(eng):
            for f in ops["sp"]:
                f(eng)


def rap(t, p0, npart, foff, dims):
    pitch = 1
    for s in list(t.shape)[1:]:
        pitch *= int(s)
    return bass.AP(t, p0 * pitch + foff, [[pitch, npart]] + [list(d) for d in dims])


def tt(P, eng, out, in0, in1, op, r, w):
    P.add(eng, lambda e: e.tensor_tensor(out=out, in0=in0, in1=in1, op=op), r, w)


def ts(P, eng, out, in0, s1, op0, r, w, s2=None, op1=None):
    if op1 is None:
        P.add(eng, lambda e: e.tensor_scalar(out=out, in0=in0, scalar1=s1, scalar2=None, op0=op0), r, w)
    else:
        P.add(eng, lambda e: e.tensor_scalar(out=out, in0=in0, scalar1=s1, scalar2=s2, op0=op0, op1=op1), r, w)


def stt(P, out, in0, scalar, in1, op0, op1, r, w):
    P.add("dve", lambda e: e.scalar_tensor_tensor(out=out, in0=in0, scalar=scalar, in1=in1, op0=op0, op1=op1), r, w)


def act(P, out, in_, func, r, w, scale=1.0, bias=None, accum=None):
    def f(e):
        kw = {}
        if bias is not None:
            kw["bias"] = bias
        if accum is not None:
            kw["accum_out"] = accum
        return e.activation(out=out, in_=in_, func=func, scale=scale, **kw)
    P.add("act", f, r, w)


def cp(P, eng, out, in_, r, w):
    if eng == "act":
        P.add("act", lambda e: e.copy(out=out, in_=in_), r, w)
    else:
        P.add(eng, lambda e: e.tensor_copy(out=out, in_=in_), r, w)


def mm(P, out, lhsT, rhs, start, stop, r, w, **kw):
    P.add("pe", lambda e: e.matmul(out, lhsT=lhsT, rhs=rhs, start=start, stop=stop, **kw), r, w)


def tr(P, out, in_, ident, r, w):
    P.add("pe", lambda e: e.transpose(out, in_, ident), r, w)


def rsqrt_col(P, out, in_, tmp, scale, r, w):
    tn = w[0] + "_t"
    act(P, tmp, in_, AF.Sqrt, r, [tn], scale=scale, bias=EPS_AP[0])
    P.add("dve", lambda e: e.reciprocal(out=out, in_=tmp), [tn], w)


EPS_AP = [None]


def exp_acc(P, out, x, k, r, ni, nx, nout, pre):
    LOG2E = 1.4426950408889634
    ts(P, "dve", k, x, LOG2E, ALU.mult, [nx], [pre + "k"])
    cp(P, "dve", ni, k, [pre + "k"], [pre + "ni"])
    cp(P, "dve", r, ni, [pre + "ni"], [pre + "nf"])
    tt(P, "dve", k, k, r, ALU.subtract, [pre + "k", pre + "nf"], [pre + "f"])
    ts(P, "dve", r, k, math.log(2.0), ALU.mult, [pre + "f"], [pre + "r"])
    K = 12
    ts(P, "dve", out, r, 1.0 / K, ALU.mult, [pre + "r"], [nout], s2=1.0, op1=ALU.add)
    for kk in range(K - 1, 0, -1):
        tt(P, "dve", out, out, r, ALU.mult, [nout, pre + "r"], [nout])
        ts(P, "dve", out, out, 1.0 / kk, ALU.mult, [nout], [nout], s2=1.0, op1=ALU.add)
    ts(P, "dve", ni, ni, 127, ALU.add, [pre + "ni", pre + "nf"], [pre + "ni2"])
    ts(P, "dve", ni, ni, 23, ALU.logical_shift_left, [pre + "ni2"], [pre + "ni3"])
    tt(P, "dve", out, out, ni.bitcast(F32), ALU.mult, [nout, pre + "ni3"], [nout])


class Ctx:
    pass


class NCProxy:
    _uid = [0]

    def __init__(self, nc):
        self._nc = nc
        NCProxy._uid[0] += 1
        self._sfx = "_i%d" % NCProxy._uid[0]

    def sbuf_tensor(self, name, *a, **k):
        return self._nc.sbuf_tensor(name + self._sfx, *a, **k)

    def psum_tensor(self, name, *a, **k):
        return self._nc.psum_tensor(name + self._sfx, *a, **k)

    def __getattr__(self, n):
        return getattr(self._nc, n)


def dram(nc, name, shape, dt, kind="Internal"):
    return nc.dram_tensor(name, list(shape), dt, kind=kind).ap()


def declare_io(nc, dbg):
    g = Ctx()
    I = lambda n, s: dram(nc, n, s, F32, "ExternalInput")
    g.xp = I("xp", [LP, D])
    g.xsf = I("xsf", [LS, D])
    g.xse = I("xse", [LEXT, D])
    g.misc = I("misc", [128, 8])
    g.oh = I("oh", [128, 2, LS // TC])
    g.wu = I("wu", [128, 8, 1024])
    g.wqkv = I("wqkv", [128, 8, 768])
    g.g1 = I("g1", [128, 8])
    g.wglu = I("wglu", [128, 8, 1024])
    g.bglu = I("bglu", [128, 8])
    g.dskip = I("dskip", [128, 8])
    g.gso = I("gso", [128, 8])
    g.wos = I("wos", [128, 8, 1024])
    g.woa = I("woa", [128, 4, 1024])
    g.gao = I("gao", [128, 4])
    g.g2 = I("g2", [128, 8])
    g.wg = I("wg", [128, 8, DFF])
    g.wup = I("wup", [128, 8, DFF])
    g.cw = I("cw", [128, NFF, 3])
    g.cb = I("cb", [128, NFF])
    g.wdn = I("wdn", [128, NFF, 1024])
    g.gf = I("gf", [128, D])
    g.gq = I("gq", [128, 64])
    g.gk = I("gk", [128, 64])
    g.lamA = I("lamA", [64, 3, 64])
    g.lamB = I("lamB", [128, 3, 32])
    g.bA = I("bA", [64, 2, 64, 32])
    g.cA = I("cA", [64, 2, 64, 32])
    g.cB = I("cB", [128, 2, 32, 32])
    g.yp = dram(nc, "yp", [LP, D], F32, "ExternalOutput")
    g.ys = dram(nc, "ys", [LOWN, D], F32, "ExternalOutput")
    S = lambda n, s, dt=BF16: dram(nc, n, s, dt, "ExternalOutput" if n in dbg else "Internal")
    g.Wu = S("Wu", [128, 8, 1024]); g.Wqkv = S("Wqkv", [128, 8, 768]); g.Wglu = S("Wglu", [128, 8, 1024])
    g.Wos = S("Wos", [128, 8, 1024]); g.Woa = S("Woa", [128, 4, 1024])
    g.Wg = S("Wg", [128, 8, DFF]); g.Wup = S("Wup", [128, 8, DFF]); g.Wdn = S("Wdn", [128, NFF, 1024])
    g.KTp = S("KTp", [128, LP]); g.KTs = S("KTs", [128, LS])
    g.Vp = S("Vp", [128, LP // 128, 130]); g.Vs = S("Vs", [128, LS // 128, 130])
    g.Qp = S("Qp", [LP // 128, 128, 512]); g.Qs = S("Qs", [LEXT // 128, 128, 512])
    g.Up = S("Up", [8, 128, LP]); g.Us = S("Us", [8, 128, LS]); g.Ue = S("Ue", [8, 128, LEXT])
    g.Sp = S("Sp", [8, 128, LP]); g.Se = S("Se", [8, 128, LEXT])
    g.X1p = S("X1p", [LP, D], F32); g.X1e = S("X1e", [LEXT, D], F32)
    g.VTd = S("VTd", [8, 128, 2 * 2 * TC * 64]); g.Rd = S("Rd", [8, 128, 4 * 2 * TC * 32])
    g.BDd = S("BDd", [8, 128, 2 * TC * 128])
    g.ATd = S("ATd", [128, 2, 32], F32)
    g.dbg = S("dbg", [128, 16384], F32)
    g.X2p = S("X2p", [LP, D], F32); g.X2e = S("X2e", [LEXT, D], F32)
    g.INITd = S("INITd", [128, 32, 2], F32)
    g.SBp = S("SBp", [128, (LP // TC) * 64]); g.SBe = S("SBe", [128, (LEXT // TC) * 64])
    return g


def phase_weights(nc, P, g):
    nc = NCProxy(nc)
    with contextlib.ExitStack() as st:
        E = st.enter_context
        stg = [E(nc.sbuf_tensor("w_stg%d" % i, [128, DFF], F32)) for i in range(2)]
        stb = [E(nc.sbuf_tensor("w_stb%d" % i, [128, DFF], BF16)) for i in range(2)]
        gains = E(nc.sbuf_tensor("w_gains", [128, 32], F32))
        blk = E(nc.Block())
        P.begin()
        for i, (src, n) in enumerate([(g.g1, 8), (g.gso, 8), (g.gao, 4), (g.g2, 8)]):
            P.dma("sp", gains[:, 8 * i:8 * i + n], src[:, :], writes=["gains"])
        jobs = [(g.wu, g.Wu, 8, 1024, 0), (g.wqkv, g.Wqkv, 8, 768, 0), (g.wglu, g.Wglu, 8, 1024, None),
                (g.wos, g.Wos, 8, 1024, 8), (g.woa, g.Woa, 4, 1024, 16), (g.wg, g.Wg, 8, DFF, 24),
                (g.wup, g.Wup, 8, DFF, 24), (g.wdn, g.Wdn, NFF, 1024, None)]
        n = 0
        for src, dst, kcs, width, goff in jobs:
            for kc in range(kcs):
                b = n % 2
                n += 1
                P.dma("sp", stg[b][:, 0:width], src[:, kc, :], writes=["stg%d" % b])
                if goff is None:
                    cp(P, "dve" if n % 2 else "pool", stb[b][:, 0:width], stg[b][:, 0:width],
                       ["stg%d" % b], ["stb%d" % b])
                else:
                    ts(P, "dve", stb[b][:, 0:width], stg[b][:, 0:width], gains[:, goff + kc:goff + kc + 1], ALU.mult,
                       ["stg%d" % b, "gains"], ["stb%d" % b])
                P.dma("act", dst[:, kc, :], stb[b][:, 0:width], reads=["stb%d" % b])
        P.finish(blk)


def build_rope(nc, P, E, name, ntiles, rowbase_ap):
    CS = E(nc.sbuf_tensor(name, [128, ntiles, 64], F32))
    pi_ = E(nc.sbuf_tensor(name + "_pi", [128, 4], I32))
    pf = E(nc.sbuf_tensor(name + "_pf", [128, 4], F32))
    fi = E(nc.sbuf_tensor(name + "_fi", [128, 16], I32))
    invf = E(nc.sbuf_tensor(name + "_invf", [128, 16], F32))
    ti = E(nc.sbuf_tensor(name + "_ti", [128, ntiles], I32))
    rowp = E(nc.sbuf_tensor(name + "_rowp", [128, ntiles], F32))
    ang = E(nc.sbuf_tensor(name + "_ang", [128, ntiles, 32], F32))
    t1 = E(nc.sbuf_tensor(name + "_t1", [128, ntiles, 32], F32))
    t2 = E(nc.sbuf_tensor(name + "_t2", [128, ntiles, 32], F32))
    ni = E(nc.sbuf_tensor(name + "_ni", [128, ntiles, 32], I32))
    N = name
    P.add("pool", lambda e: e.iota(pi_[:, 0:1], [[0, 1]], base=0, channel_multiplier=1), [], [N + "pi0"])
    P.add("pool", lambda e: e.iota(fi[:], [[1, 16]], base=0, channel_multiplier=0), [], [N + "fi"])
    P.add("pool", lambda e: e.iota(ti[:], [[2, ntiles]], base=0, channel_multiplier=0), [], [N + "ti"])
    ts(P, "dve", pi_[:, 1:2], pi_[:, 0:1], 6, ALU.arith_shift_right, [N + "pi0"], [N + "pi1"])
    ts(P, "dve", pi_[:, 2:3], pi_[:, 0:1], 63, ALU.bitwise_and, [N + "pi0"], [N + "pi2"])
    cp(P, "dve", pf[:, 1:3], pi_[:, 1:3], [N + "pi1", N + "pi2"], [N + "pf"])
    cp(P, "dve", invf[:], fi[:], [N + "fi"], [N + "invf0"])
    act(P, invf[:], invf[:], AF.Exp, [N + "invf0"], [N + "invf"], scale=-math.log(10000.0) / 16.0)
    cp(P, "dve", rowp[:], ti[:], [N + "ti"], [N + "rowp0"])
    if rowbase_ap is not None:
        ts(P, "dve", rowp[:], rowp[:], pf[:, 1:2], ALU.add, [N + "rowp0", N + "pf", "misc"], [N + "rowp"],
           s2=rowbase_ap, op1=ALU.add)
    else:
        ts(P, "dve", rowp[:], rowp[:], pf[:, 1:2], ALU.add, [N + "rowp0", N + "pf"], [N + "rowp"])
    tt(P, "dve", ang[:, :, 0:16], rap(rowp, 0, 128, 0, [[1, ntiles], [0, 16]]),
       rap(invf, 0, 128, 0, [[0, ntiles], [1, 16]]), ALU.mult, [N + "rowp", N + "invf"], [N + "angA"])
    ts(P, "dve", ang[:, :, 16:32], rap(invf, 0, 128, 0, [[0, ntiles], [1, 16]]), pf[:, 2:3], ALU.mult,
       [N + "invf", N + "pf"], [N + "angB"])
    for which, off in ((0, 0.25), (1, 0.0)):
        ts(P, "dve", t1[:], ang[:], 1.0 / TWO_PI, ALU.mult, [N + "angA", N + "angB"], [N + "t1"],
           s2=off, op1=ALU.add)
        sin_frac(P, CS[:, :, 32 * which:32 * which + 32], t1[:], t2[:], ni[:], N + "t1", N + "t2", N + "ni",
                 N + "w%dout" % which)
    return CS


def sin_frac(P, out, t, tmp, ni, nt, ntmp, nni, nout):
    cp(P, "dve", ni, t, [nt], [nni])
    cp(P, "dve", tmp, ni, [nni], [ntmp])
    tt(P, "dve", t, t, tmp, ALU.subtract, [nt, ntmp], [nt])
    ts(P, "dve", tmp, t, 0.5, ALU.is_gt, [nt], [ntmp])
    tt(P, "dve", t, t, tmp, ALU.subtract, [nt, ntmp], [nt])
    ts(P, "dve", tmp, t, -0.5, ALU.is_lt, [nt], [ntmp])
    tt(P, "dve", t, t, tmp, ALU.add, [nt, ntmp], [nt])
    act(P, out, t, AF.Sin, [nt], [nout], scale=TWO_PI)


def phase_inproj(nc, P, g, seq):
    nc = NCProxy(nc)
    xin, L = {"p": (g.xp, LP), "s": (g.xsf, LS), "e": (g.xse, LEXT)}[seq]
    do_kv = seq in ("p", "s")
    do_q = seq in ("p", "e")
    KT, V = (g.KTp, g.Vp) if seq == "p" else (g.KTs, g.Vs)
    Q = g.Qp if seq == "p" else g.Qs
    U = {"p": g.Up, "s": g.Us, "e": g.Ue}[seq]
    ntile = L // 128
    with contextlib.ExitStack() as st:
        E = st.enter_context
        wu = E(nc.sbuf_tensor("i_wu", [128, 8, 1024], BF16))
        wqkv = E(nc.sbuf_tensor("i_wqkv", [128, 8, 768], BF16))
        ident = E(nc.sbuf_tensor("i_ident", [128, 128], BF16))
        identf = E(nc.sbuf_tensor("i_identf", [128, 128], F32))
        misc = E(nc.sbuf_tensor("i_misc", [128, 8], F32))
        gq = E(nc.sbuf_tensor("i_gq", [128, 64], F32))
        gk = E(nc.sbuf_tensor("i_gk", [128, 64], F32))
        epsc = E(nc.sbuf_tensor("i_eps", [128, 1], F32))
        xt = [E(nc.sbuf_tensor("i_xt%d" % i, [128, D], F32)) for i in range(2)]
        junk = E(nc.sbuf_tensor("i_junk", [128, D], F32))
        xn = [E(nc.sbuf_tensor("i_xn%d" % i, [128, D], BF16)) for i in range(2)]
        stat = [E(nc.sbuf_tensor("i_stat%d" % i, [128, 48], F32)) for i in range(2)]
        hT = [E(nc.sbuf_tensor("i_hT%d" % i, [128, 8, 512], BF16)) for i in range(2)]
        ust = [E(nc.sbuf_tensor("i_ust%d" % i, [128, 8, 512], BF16)) for i in range(2)]
        kst = [E(nc.sbuf_tensor("i_kst%d" % i, [128, 512], BF16)) for i in range(2)]
        vst = [E(nc.sbuf_tensor("i_vst%d" % i, [128, 4, 130], BF16)) for i in range(2)]
        qst = [E(nc.sbuf_tensor("i_qst%d" % i, [128, 4, 128], BF16)) for i in range(2)]
        sq2 = [E(nc.sbuf_tensor("i_sq%d" % i, [128, 640], F32)) for i in range(2)]
        qn2 = [E(nc.sbuf_tensor("i_qn%d" % i, [128, 10, 64], F32)) for i in range(2)]
        ra2 = [E(nc.sbuf_tensor("i_ra%d" % i, [128, 10, 32], F32)) for i in range(2)]
        rb2 = [E(nc.sbuf_tensor("i_rb%d" % i, [128, 10, 32], F32)) for i in range(2)]
        qr2 = [E(nc.sbuf_tensor("i_qr%d" % i, [128, 10, 64], BF16)) for i in range(2)]
        pT = E(nc.psum_tensor("i_pT", [128, 1024], BF16))
        pT2 = E(nc.psum_tensor("i_pT2", [128, 1024], BF16))
        pT3 = E(nc.psum_tensor("i_pT3", [128, 1024], BF16))
        pQ2 = [E(nc.psum_tensor("i_pQ%d" % i, [128, 512], F32)) for i in range(2)]
        pKVb = E(nc.psum_tensor("i_pKV", [128, 512], F32))
        pU = [E(nc.psum_tensor("i_pU%d" % i, [128, 512], F32)) for i in range(2)]
        blk = E(nc.Block())
        P.begin()
        EPS_AP[0] = epsc[:, 0:1]
        P.add("pool", lambda e: e.memset(epsc[:], EPS), [], ["eps"])
        P.dma("sp", wu[:], g.Wu[:, :, :], writes=["wu"])
        P.dma("sp", wqkv[:], g.Wqkv[:, :, :], writes=["wqkv"])
        P.dma("sp", misc[:], g.misc[:, :], writes=["misc"])
        P.dma("sp", gq[:], g.gq[:, :], writes=["gq"])
        P.dma("sp", gk[:], g.gk[:, :], writes=["gk"])
        make_ident(nc, P, E, ident, identf)
        CS = build_rope(nc, P, E, "i_cs", ntile, misc[:, 0:1] if seq == "e" else None)
        for b in range(2):
            P.add("pool", lambda e, b=b: e.memset(vst[b][:], 1.0), [], ["vst%d" % b])
        for t in range(ntile):
            b = t % 2
            sb = (t // 4) % 2
            sub = t % 4
            X, S_ = xt[b], stat[b]
            sq, qn, ra, rb, qr, pQ = sq2[b], qn2[b], ra2[b], rb2[b], qr2[b], pQ2[b]
            pKV = pKVb[:, 256 * b:256 * b + 256]
            B_ = "_%d" % b
            P.dma("sp", X[:], xin[t * 128:(t + 1) * 128, :], writes=["xt%d" % b])
            act(P, junk[:], X[:], AF.Square, ["xt%d" % b], ["junk", "ssq%d" % b], accum=S_[:, 0:1])
            rsqrt_col(P, S_[:, 2:3], S_[:, 0:1], S_[:, 1:2], 1.0 / D, ["ssq%d" % b, "eps"], ["rs%d" % b])
            ts(P, "dve", xn[b][:], X[:], S_[:, 2:3], ALU.mult, ["xt%d" % b, "rs%d" % b], ["xn%d" % b])
            for kc in range(8):
                tr(P, pT[:, kc * 128:(kc + 1) * 128], xn[b][:, kc * 128:(kc + 1) * 128], ident[:],
                   ["xn%d" % b, "ident"], ["pT"])
            cp(P, "act", hT[sb][:, :, sub * 128:(sub + 1) * 128], pT[:].rearrange("p (k t) -> p k t", k=8),
               ["pT"], ["hT%d_%d" % (sb, sub)])
            hname = "hT%d_%d" % (sb, sub)
            nh = 0
            if do_q:
                for kc in range(8):
                    mm(P, pQ[:], hT[sb][:, kc, sub * 128:(sub + 1) * 128], wqkv[:, kc, 0:512], kc == 0, kc == 7,
                       [hname, "wqkv"], ["pQ" + B_])
                nh = 8
            if do_kv:
                for kc in range(8):
                    mm(P, pKV, hT[sb][:, kc, sub * 128:(sub + 1) * 128], wqkv[:, kc, 512:768],
                       kc == 0, kc == 7, [hname, "wqkv"], ["pKV" + B_])
            H = nh + (2 if do_kv else 0)
            if do_q:
                act(P, sq[:, 0:512], pQ[:], AF.Square, ["pQ" + B_], ["sqq" + B_])
            if do_kv:
                act(P, sq[:, 512:640], pKV[:, 0:128], AF.Square, ["pKV" + B_], ["sqk" + B_])
            h0 = 0 if do_q else 8
            P.add("dve", lambda e, S_=S_, h0=h0, H=H, sq=sq: e.tensor_reduce(
                out=S_[:, 4 + h0:4 + h0 + H], in_=sq[:, h0 * 64:(h0 + H) * 64].rearrange("p (h d) -> p h d", d=64),
                axis=AX.X, op=ALU.add), ["sqq" + B_, "sqk" + B_], ["hss%d" % b])
            rsqrt_col(P, S_[:, 28 + h0:28 + h0 + H], S_[:, 4 + h0:4 + h0 + H], S_[:, 16 + h0:16 + h0 + H],
                      1.0 / 64.0, ["hss%d" % b, "eps"], ["hrs%d" % b])
            if do_q:
                tt(P, "dve", rap(qn, 0, 128, 0, [[64, 2], [128, 4], [1, 64]]),
                   rap(pQ, 0, 128, 0, [[256, 2], [64, 4], [1, 64]]),
                   rap(S_, 0, 128, 28, [[4, 2], [1, 4], [0, 64]]), ALU.mult, ["pQ" + B_, "hrs%d" % b], ["qnq" + B_])
                tt(P, "pool", qn[:, 0:8, :], qn[:, 0:8, :], rap(gq, 0, 128, 0, [[0, 8], [1, 64]]), ALU.mult,
                   ["qnq" + B_, "gq"], ["qnq" + B_])
            if do_kv:
                tt(P, "dve", qn[:, 8:10, :], pKV[:, 0:128].rearrange("p (h d) -> p h d", d=64),
                   rap(S_, 0, 128, 36, [[1, 2], [0, 64]]), ALU.mult, ["pKV" + B_, "hrs%d" % b], ["qnk" + B_])
                tt(P, "pool", qn[:, 8:10, :], qn[:, 8:10, :], rap(gk, 0, 128, 0, [[0, 2], [1, 64]]), ALU.mult,
                   ["qnk" + B_, "gk"], ["qnk" + B_])
            cosb = rap(CS, 0, 128, t * 64, [[0, H], [1, 32]])
            sinb = rap(CS, 0, 128, t * 64 + 32, [[0, H], [1, 32]])
            x1 = qn[:, h0:h0 + H, 0:32]
            x2 = qn[:, h0:h0 + H, 32:64]
            rd = ["qnq" + B_, "qnk" + B_, "i_csw0out", "i_csw1out"]
            tt(P, "dve", ra[:, 0:H, :], x1, cosb, ALU.mult, rd, ["ra" + B_])
            tt(P, "pool", rb[:, 0:H, :], x2, sinb, ALU.mult, rd, ["rb" + B_])
            tt(P, "dve", qr[:, h0:h0 + H, 0:32], ra[:, 0:H, :], rb[:, 0:H, :], ALU.subtract, ["ra" + B_, "rb" + B_], ["qr" + B_])
            tt(P, "dve", ra[:, 0:H, :], x2, cosb, ALU.mult, rd + ["qr" + B_], ["ra" + B_])
            tt(P, "pool", rb[:, 0:H, :], x1, sinb, ALU.mult, rd + ["qr" + B_], ["rb" + B_])
            tt(P, "dve", qr[:, h0:h0 + H, 32:64], ra[:, 0:H, :], rb[:, 0:H, :], ALU.add, ["ra" + B_, "rb" + B_], ["qr" + B_])
            if do_q:
                for hh in range(4):
                    tr(P, pT2[:, hh * 128:(hh + 1) * 128],
                       qr[:, 2 * hh:2 * hh + 2, :].rearrange("p h d -> p (h d)"), ident[:],
                       ["qr" + B_, "qr" + B_, "ident"], ["pT2q"])
                cp(P, "act", qst[b][:], pT2[:, 0:512].rearrange("p (h t) -> p h t", h=4), ["pT2q"], ["qst%d" % b])
                P.dma("act", Q[t, :, :], qst[b][:].rearrange("p h t -> p (h t)"), reads=["qst%d" % b])
            if do_kv:
                tr(P, pT3[:, 0:128], qr[:, 8:10, :].rearrange("p h d -> p (h d)"), ident[:],
                   ["qr" + B_, "qr" + B_, "ident"], ["pT3k"])
                cp(P, "act", kst[sb][:, sub * 128:(sub + 1) * 128], pT3[:, 0:128], ["pT3k"], ["kst%d_%d" % (sb, sub)])
                cp(P, "dve", vst[sb][:, sub, :].rearrange("p (h e) -> p h e", h=2)[:, :, 0:64],
                   pKV[:, 128:256].rearrange("p (h d) -> p h d", d=64), ["pKV" + B_, "vst%d" % sb], ["vst%d_%d" % (sb, sub)])
            if sub == 3 or t == ntile - 1:
                s0 = (t // 4) * 512
                nt = (sub + 1) * 128
                for ut in range(8):
                    pu = pU[ut % 2]
                    for kc in range(8):
                        mm(P, pu[:, 0:nt], wu[:, kc, ut * 128:(ut + 1) * 128], hT[sb][:, kc, 0:nt], kc == 0, kc == 7,
                           ["hT%d_%d" % (sb, i) for i in range(sub + 1)] + ["wu"], ["pU%d" % (ut % 2)])
                    cp(P, "act" if ut % 2 else "dve", ust[sb][:, ut, 0:nt], pu[:, 0:nt], ["pU%d" % (ut % 2)],
                       ["ust%d_%d" % (sb, ut)])
                P.dma("pool", U[:, :, s0:s0 + nt].rearrange("u p t -> p u t"), ust[sb][:, :, 0:nt],
                      reads=["ust%d_%d" % (sb, i) for i in range(8)])
                if do_kv:
                    P.dma("pool", KT[:, s0:s0 + 512], kst[sb][:],
                          reads=["kst%d_%d" % (sb, i) for i in range(4)])
                    P.dma("pool", V[:, (t // 4) * 4:(t // 4) * 4 + 4, :], vst[sb][:],
                          reads=["vst%d_%d" % (sb, i) for i in range(4)])
        if DEBUG and seq == "p":
            P.dma("sp", g.dbg[:, 0:4096], CS[:].rearrange("p t c -> p (t c)"), reads=["i_csw0out", "i_csw1out"])
            P.dma("sp", g.dbg[:, 4096:4144], stat[1][:], reads=["hrs1"])
            P.dma("sp", g.dbg[:, 4200:4840], qn[:].rearrange("p h d -> p (h d)"), reads=["qnq" + B_, "qnk" + B_])
            P.dma("sp", g.dbg[:, 5000:5640], sq[:], reads=["sqq" + B_, "sqk" + B_])
        P.finish(blk)


def make_ident(nc, P, E, ident, identf):
    ci = E(nc.sbuf_tensor(ident.name + "_ci", [128, 128], I32))
    pi_ = E(nc.sbuf_tensor(ident.name + "_pi", [128, 1], I32))
    cf = E(nc.sbuf_tensor(ident.name + "_cf", [128, 128], F32))
    pf = E(nc.sbuf_tensor(ident.name + "_pf", [128, 1], F32))
    P.add("pool", lambda e: e.iota(ci[:], [[1, 128]], base=0, channel_multiplier=0), [], ["id_ci"])
    P.add("pool", lambda e: e.iota(pi_[:], [[0, 1]], base=0, channel_multiplier=1), [], ["id_pi"])
    cp(P, "dve", cf[:], ci[:], ["id_ci"], ["id_cf"])
    cp(P, "dve", pf[:], pi_[:], ["id_pi"], ["id_pf"])
    ts(P, "dve", identf[:], cf[:], pf[:, 0:1], ALU.is_equal, ["id_cf", "id_pf"], ["identf"])
    cp(P, "dve", ident[:], identf[:], ["identf"], ["ident"])


def build(dbg=(), upto=99):
    nc = bass.Bass("TRN2", target_bir_lowering=False)
    g = declare_io(nc, dbg)
    with contextlib.ExitStack() as st:
        E = st.enter_context
        E(nc.allow_low_precision("bf16 matmul operands, fp32 accumulation"))
        sems = [E(nc.semaphore("s_" + e)) for e in Prog.ENG]
        dsems = [E(nc.semaphore("d%d" % i)) for i in range(NDMA)]
        P = Prog(nc, sems, dsems)
        phase_weights(nc, P, g)
        if upto >= 1:
            phase_s5_derive(nc, P, g)
        if upto >= 2:
            phase_inproj(nc, P, g, "p")
            if not ONLY_P:
                phase_inproj(nc, P, g, "s")
                phase_inproj(nc, P, g, "e")
        if upto >= 3:
            phase_s5_states(nc, P, g, "p")
            if STOP > 20:
                phase_s5_out(nc, P, g, "p")
            if not ONLY_P:
                phase_s5_states(nc, P, g, "s")
                phase_s5_states(nc, P, g, "e")
                phase_s5_out(nc, P, g, "e")
        if upto >= 4:
            phase_attn(nc, P, g, "p", nqb_limit=NQB_LIMIT)
            if not ONLY_P:
                phase_attn(nc, P, g, "e", nqb_limit=NQB_LIMIT)
        if upto >= 5:
            for seq in (("p",) if ONLY_P else ("p", "e")):
                phase_ffn2(nc, P, g, seq, 0, 11, True, False)
                phase_ffn2(nc, P, g, seq, 11, 22, False, True)
    return nc


def slot_pad_cols(w):
    out = np.zeros(w.shape[:-1] + (32, 32), w.dtype)
    out[..., :, :16] = w.reshape(w.shape[:-1] + (32, 16))
    return out.reshape(w.shape[:-1] + (1024,))


def kcl(w, kc):
    return np.ascontiguousarray(w.reshape(kc, 128, -1).transpose(1, 0, 2))


def col(v, kc):
    return np.ascontiguousarray(v.reshape(kc, 128).T)


def prep_inputs(inp):
    f = lambda a: np.ascontiguousarray(a, dtype=np.float32)
    w_in = f(inp["w_in"])[0]
    sh = {}
    sh["wu"] = kcl(slot_pad_cols(w_in[:, 0:512]), 8)
    sh["wqkv"] = kcl(w_in[:, 512:1280], 8)
    sh["g1"] = col(f(inp["norm1_g"])[0], 8)
    wglu = f(inp["w_glu"])[0]
    sh["wglu"] = kcl(slot_pad_cols(slot_pad_cols(wglu).T).T, 8)
    sh["bglu"] = col(slot_pad_cols(f(inp["b_glu"])[0]), 8)
    sh["dskip"] = col(slot_pad_cols(f(inp["d_skip"])[0].reshape(512)), 8)
    sh["gso"] = col(slot_pad_cols(f(inp["ssm_out_g"])[0]), 8)
    w_out = f(inp["w_out"])[0]
    sh["wos"] = kcl(slot_pad_cols(w_out[0:512].T).T, 8)
    sh["woa"] = kcl(w_out[512:1024], 4)
    sh["gao"] = col(f(inp["attn_out_g"])[0], 4)
    sh["g2"] = col(f(inp["norm2_g"])[0], 8)
    sh["wg"] = kcl(f(inp["w_gate"])[0], 8)
    sh["wup"] = kcl(f(inp["w_up"])[0], 8)
    cw = f(inp["conv_w"])[0]
    sh["cw"] = np.ascontiguousarray(cw.reshape(3, NFF, 128).transpose(2, 1, 0))
    sh["cb"] = col(f(inp["conv_b"])[0], NFF)
    sh["wdn"] = kcl(f(inp["w_down"])[0], NFF)
    sh["gf"] = f(np.broadcast_to(f(inp["final_norm_g"])[None, :], (128, D)))
    sh["gq"] = f(np.broadcast_to(f(inp["q_norm_g"])[0][None, :], (128, 64)))
    sh["gk"] = f(np.broadcast_to(f(inp["k_norm_g"])[0][None, :], (128, 64)))
    lam_re, lam_im, log_dt = f(inp["lam_re"])[0], f(inp["lam_im"])[0], f(inp["log_dt"])[0]
    ldt = np.broadcast_to(log_dt[:, :, None], (2, 32, 64))
    A3 = np.stack([lam_re, lam_im, ldt], 0)
    sh["lamA"] = np.ascontiguousarray(A3.transpose(3, 0, 1, 2).reshape(64, 3, 64))
    sh["lamB"] = np.ascontiguousarray(A3.transpose(1, 3, 0, 2).reshape(128, 3, 32))
    b2 = np.stack([f(inp["b_re"])[0], f(inp["b_im"])[0]], 0)
    bA = np.zeros((64, 2, 64, 32), np.float32)
    bA[..., :16] = b2.transpose(3, 0, 1, 2, 4).reshape(64, 2, 64, 16)
    sh["bA"] = bA
    c2 = np.stack([f(inp["c_re"])[0], f(inp["c_im"])[0]], 0)
    cA = np.zeros((64, 2, 64, 32), np.float32)
    cA[..., :16] = c2.transpose(4, 0, 1, 2, 3).reshape(64, 2, 64, 16)
    sh["cA"] = cA
    cB = np.zeros((128, 2, 32, 32), np.float32)
    cB[..., :16] = c2.transpose(1, 4, 0, 2, 3).reshape(128, 2, 32, 16)
    sh["cB"] = cB
    return sh


def core_inputs(inp, sh, c):
    xp = np.ascontiguousarray(inp["x_prompt"], dtype=np.float32)
    xs = np.ascontiguousarray(inp["x_sample"], dtype=np.float32)[0]
    m = dict(sh)
    m["xp"] = xp[c]
    m["xsf"] = xs
    lo = c * LOWN - 128
    xe = np.zeros((LEXT, D), np.float32)
    a, b = max(lo, 0), min(lo + LEXT, LS)
    xe[a - lo:b - lo] = xs[a:b]
    m["xse"] = xe
    misc = np.zeros((128, 8), np.float32)
    misc[:, 0] = lo // 64
    misc[:, 1] = 1.0 if c > 0 else 0.0
    misc[:, 2] = 1.0 if c < 7 else 0.0
    m["misc"] = misc
    oh = np.zeros((128, 2, LS // TC), np.float32)
    cl = lo // TC - 1
    cr = (lo + LEXT) // TC
    if cl >= 0:
        oh[:, 0, cl] = 1.0
    if cr < LS // TC:
        oh[:, 1, cr] = 1.0
    m["oh"] = oh
    return m


def kernel(**inp):
    sh = prep_inputs(inp)
    nc = build()
    in_maps = [core_inputs(inp, sh, c) for c in range(8)]
    res = run_bass_kernel_spmd(nc, in_maps, core_ids=list(range(8)))
    yp = np.stack([res.results[c]["yp"] for c in range(8)], axis=0)
    ys = np.concatenate([res.results[c]["ys"] for c in range(8)], axis=0)[None]
    return yp.astype(np.float32), ys.astype(np.float32)


def phase_attn(nc, P, g, seq, nqb_limit=None):
    nc = NCProxy(nc)
    if seq == "p":
        KTd, Vd, Qd, Sd, xin, X1, L, nqb = g.KTp, g.Vp, g.Qp, g.Sp, g.xp, g.X1p, LP, LP // 128
    else:
        KTd, Vd, Qd, Sd, xin, X1, L, nqb = g.KTs, g.Vs, g.Qs, g.Se, g.xse, g.X1e, LS, LEXT // 128
    nkb = L // 128
    if nqb_limit:
        nqb = min(nqb, nqb_limit)
    with contextlib.ExitStack() as st:
        E = st.enter_context
        KT = E(nc.sbuf_tensor("a_KT", [128, L], BF16))
        V = E(nc.sbuf_tensor("a_V", [128, nkb, 130], BF16))
        wglu = E(nc.sbuf_tensor("a_wglu", [128, 8, 1024], BF16))
        wos = E(nc.sbuf_tensor("a_wos", [128, 8, 1024], BF16))
        woa = E(nc.sbuf_tensor("a_woa", [128, 4, 1024], BF16))
        bglu = E(nc.sbuf_tensor("a_bglu", [128, 8], F32))
        ident = E(nc.sbuf_tensor("a_ident", [128, 128], BF16))
        identf = E(nc.sbuf_tensor("a_identf", [128, 128], F32))
        ones = E(nc.sbuf_tensor("a_ones", [128, 2], BF16))
        epsc = E(nc.sbuf_tensor("a_eps", [128, 1], F32))
        nbglu = E(nc.sbuf_tensor("a_nbglu", [128, 8], F32))
        nhalf = E(nc.sbuf_tensor("a_nhalf", [128, 1], F32))
        ge = E(nc.sbuf_tensor("a_ge", [128, 8, 128], F32))
        Qb = [E(nc.sbuf_tensor("a_Qb%d" % i, [128, 512], BF16)) for i in range(2)]
        PT = [E(nc.sbuf_tensor("a_PT%d" % i, [128, 1024], BF16)) for i in range(3)]
        OT = [E(nc.sbuf_tensor("a_OT%d" % i, [65, 512], F32)) for i in range(2)]
        o = E(nc.sbuf_tensor("a_o", [128, 8, 64], F32))
        junk = E(nc.sbuf_tensor("a_junk", [128, 512], F32))
        on = E(nc.sbuf_tensor("a_on", [128, 512], F32))
        onT = E(nc.sbuf_tensor("a_onT", [128, 4, 128], BF16))
        stl = [E(nc.sbuf_tensor("a_st%d" % i, [128, 8, 128], BF16)) for i in range(2)]
        gt = E(nc.sbuf_tensor("a_gt", [128, 8, 128], BF16))
        s2 = E(nc.sbuf_tensor("a_s2", [128, 8, 128], BF16))
        s2sq = E(nc.sbuf_tensor("a_s2sq", [128, 8, 128], BF16))
        xt = [E(nc.sbuf_tensor("a_xt%d" % i, [128, D], F32)) for i in range(2)]
        x1 = [E(nc.sbuf_tensor("a_x1%d" % i, [128, D], F32)) for i in range(2)]
        stat = [E(nc.sbuf_tensor("a_stat%d" % i, [128, 32], F32)) for i in range(2)]
        pS = E(nc.psum_tensor("a_pS", [128, 2048], F32))
        pO = [E(nc.psum_tensor("a_pO%d" % i, [128, 512], F32)) for i in range(2)]
        pG = [E(nc.psum_tensor("a_pG%d" % i, [128, 512], F32)) for i in range(2)]
        blk = E(nc.Block())
        P.begin()
        EPS_AP[0] = epsc[:, 0:1]
        P.add("pool", lambda e: e.memset(epsc[:], EPS), [], ["eps"])
        P.add("pool", lambda e: e.memset(ones[:], 1.0), [], ["ones"])
        nk4 = 8
        for i in range(nk4):
            w = L // nk4
            P.dma("sp", KT[:, i * w:(i + 1) * w], KTd[:, i * w:(i + 1) * w], writes=["KT"])
            kw = nkb // nk4
            P.dma("sp", V[:, i * kw:(i + 1) * kw, :], Vd[:, i * kw:(i + 1) * kw, :], writes=["V"])
        P.dma("sp", wglu[:], g.Wglu[:, :, :], writes=["wglu"])
        P.dma("sp", wos[:], g.Wos[:, :, :], writes=["wos"])
        P.dma("sp", woa[:], g.Woa[:, :, :], writes=["woa"])
        P.dma("sp", bglu[:], g.bglu[:, :], writes=["bglu"])
        ts(P, "dve", nbglu[:], bglu[:], -1.0, ALU.mult, ["bglu"], ["nbglu"])
        P.add("pool", lambda e: e.memset(nhalf[:], -0.5), [], ["nhalf"])

        def rsqrt_pow(out, in_, tmp, scale, r, w):
            tn = w[0] + "_t"
            ts(P, "dve", tmp, in_, scale, ALU.mult, r, [tn], s2=EPS, op1=ALU.add)
            tt(P, "pool", out, tmp, nhalf[:, 0:1], ALU.pow, [tn, "nhalf"], w)
        make_ident(nc, P, E, ident, identf)

        def tail_gen(qb, b, S_):
            for j in range(2):
                for hh in range(4):
                    tr(P, pG[j][:, hh * 65:(hh + 1) * 65], OT[j][:, hh * 128:(hh + 1) * 128], identf[0:65, 0:65],
                       ["OT%d" % j, "identf"], ["pG%d" % j])
                P.add("dve", lambda e, j=j, S_=S_: e.reciprocal(out=S_[:, 8 + 4 * j:12 + 4 * j],
                                                                 in_=rap(pG[j], 0, 128, 64, [[65, 4]])),
                      ["pG%d" % j], ["rden%d_%d" % (b, j)])
                tt(P, "dve", o[:, 4 * j:4 * j + 4, :], rap(pG[j], 0, 128, 0, [[65, 4], [1, 64]]),
                   rap(S_, 0, 128, 8 + 4 * j, [[1, 4], [0, 64]]), ALU.mult,
                   ["pG%d" % j, "rden%d_%d" % (b, j)], ["o%d" % j])
            yield
            of = o[:].rearrange("p h d -> p (h d)")
            act(P, junk[:], of, AF.Square, ["o0", "o1"], ["junk", "ssqa%d" % b], accum=S_[:, 0:1])
            rsqrt_pow(S_[:, 2:3], S_[:, 0:1], S_[:, 1:2], 1.0 / 512.0, ["ssqa%d" % b], ["rsa%d" % b])
            ts(P, "dve", on[:], of, S_[:, 2:3], ALU.mult, ["o0", "o1", "rsa%d" % b], ["on"])
            for k in range(4):
                tr(P, pG[0][:, k * 128:(k + 1) * 128], on[:, k * 128:(k + 1) * 128], identf[:], ["on", "identf"], ["pG0"])
            cp(P, "act", onT[:], pG[0][:].rearrange("p (k t) -> p k t", k=4), ["pG0"], ["onT"])
            yield
            sname = "st%d" % b
            for m in range(8):
                for k in range(8):
                    mm(P, pG[m // 4][:, (m % 4) * 128:(m % 4 + 1) * 128], wglu[:, k, m * 128:(m + 1) * 128],
                       stl[b][:, k, :], k == 0, k == 7, ["wglu", sname], ["pG%d" % (m // 4)])
                if m % 2 == 1:
                    yield
            for m in range(8):
                act(P, ge[:, m, :], pG[m // 4][:, (m % 4) * 128:(m % 4 + 1) * 128], AF.Exp,
                    ["pG%d" % (m // 4), "nbglu"], ["ge%d" % m], scale=-1.0, bias=nbglu[:, m:m + 1])
            gen_ = ["ge%d" % m for m in range(8)]
            ts(P, "dve", ge[:], ge[:], 1.0, ALU.add, gen_, gen_)
            P.add("dve", lambda e: e.reciprocal(out=gt[:], in_=ge[:]), gen_, ["gt"])
            tt(P, "dve", s2[:], stl[b][:], gt[:], ALU.mult, [sname, "gt"], ["s2"])
            tt(P, "pool", s2sq[:], s2[:], s2[:], ALU.mult, ["s2"], ["s2sq"])
            yield
            for k in range(8):
                mm(P, pG[1][:, 0:1], s2sq[:, k, :], ones[:, 0:1], k == 0, k == 7, ["s2sq", "ones"], ["pG1"])
            rsqrt_pow(S_[:, 5:6], pG[1][:, 0:1], S_[:, 4:5], 1.0 / 512.0, ["pG1"], ["rss%d" % b])
            yield
            for half in range(2):
                for k in range(8):
                    mm(P, pG[half][:], s2[:, k, :], wos[:, k, half * 512:(half + 1) * 512], k == 0, k == 7,
                       ["s2", "wos"], ["pG%d" % half])
                stt(P, x1[b][:, half * 512:(half + 1) * 512], pG[half][:], S_[:, 5:6],
                    xt[b][:, half * 512:(half + 1) * 512], ALU.mult, ALU.add,
                    ["pG%d" % half, "rss%d" % b, "xt%d" % b], ["x1a%d_%d" % (b, half)])
                yield
            for half in range(2):
                for k in range(4):
                    mm(P, pG[half][:], onT[:, k, :], woa[:, k, half * 512:(half + 1) * 512], k == 0, k == 3,
                       ["onT", "woa"], ["pG%d" % half])
                tt(P, "dve", x1[b][:, half * 512:(half + 1) * 512], x1[b][:, half * 512:(half + 1) * 512],
                   pG[half][:], ALU.add, ["pG%d" % half, "x1a%d_%d" % (b, half)], ["x1_%d_%d" % (b, half)])
                yield
            P.dma("pool", X1[qb * 128:(qb + 1) * 128, :], x1[b][:],
                  reads=["x1_%d_0" % b, "x1_%d_1" % b, "x1a%d_0" % b, "x1a%d_1" % b])

        pending = []
        npt = 0
        for qb in range(nqb):
            b = qb % 2
            S_ = stat[b]
            P.dma("sp", Qb[b][:], Qd[qb, :, :], writes=["Qb%d" % b])
            P.dma("sp", stl[b][:], Sd[:, :, qb * 128:(qb + 1) * 128].rearrange("u p t -> p u t"), writes=["st%d" % b])
            P.dma("sp", xt[b][:], xin[qb * 128:(qb + 1) * 128, :], writes=["xt%d" % b])
            def emitS(kb):
                for j in range(2):
                    c0 = (kb % 2) * 1024 + j * 512
                    mm(P, pS[:, c0:c0 + 512], KT[64 * j:64 * j + 64, kb * 128:(kb + 1) * 128],
                       Qb[b][64 * j:64 * j + 64, :], True, True, ["KT", "Qb%d" % b], ["pS%d" % (kb % 2)])

            emitS(0)
            emitS(1)
            for kb in range(nkb):
                if pending and kb >= 2 and kb % 3 == 2:
                    try:
                        next(pending[0])
                    except StopIteration:
                        pending.pop(0)
                pt = PT[npt % 3]
                ptn = "PT%d" % (npt % 3)
                npt += 1
                act(P, pt[:], pS[:, (kb % 2) * 1024:(kb % 2) * 1024 + 1024], AF.Exp, ["pS%d" % (kb % 2)], [ptn],
                    scale=0.125)
                if kb + 2 < nkb:
                    emitS(kb + 2)
                for j in range(2):
                    mm(P, pO[j][0:65, :], V[:, kb, j * 65:(j + 1) * 65], pt[:, j * 512:(j + 1) * 512],
                       kb == 0, kb == nkb - 1, [ptn, "V"], ["pO%d" % j])
            for j in range(2):
                cp(P, "act" if j else "dve", OT[j][:], pO[j][0:65, :], ["pO%d" % j], ["OT%d" % j])
            pending.append(tail_gen(qb, b, S_))
        for gen_ in pending:
            for _ in gen_:
                pass
        P.finish(blk)


def phase_ffn(nc, P, g, seq, mlo, mhi, first, last, ntile_limit=None):
    nc = NCProxy(nc)
    if seq == "p":
        X1, X2, Y, ntile, t_lo, t_hi = g.X1p, g.X2p, g.yp, LP // 128, 0, LP // 128
    else:
        X1, X2, Y, ntile, t_lo, t_hi = g.X1e, g.X2e, g.ys, LEXT // 128, 1, LEXT // 128 - 1
    if ntile_limit:
        ntile = min(ntile, ntile_limit)
        t_hi = min(t_hi, ntile)
    nm = mhi - mlo
    with contextlib.ExitStack() as st:
        E = st.enter_context
        wg = E(nc.sbuf_tensor("f_wg", [128, 8, nm * 128], BF16))
        wup = E(nc.sbuf_tensor("f_wup", [128, 8, nm * 128], BF16))
        wdn = E(nc.sbuf_tensor("f_wdn", [128, nm, 1024], BF16))
        cw = E(nc.sbuf_tensor("f_cw", [128, NFF, 3], F32))
        cb = E(nc.sbuf_tensor("f_cb", [128, NFF], F32))
        gf = E(nc.sbuf_tensor("f_gf", [128, D], F32))
        misc = E(nc.sbuf_tensor("f_misc", [128, 8], F32))
        ident = E(nc.sbuf_tensor("f_ident", [128, 128], BF16))
        identf = E(nc.sbuf_tensor("f_identf", [128, 128], F32))
        epsc = E(nc.sbuf_tensor("f_eps", [128, 1], F32))
        x1 = [E(nc.sbuf_tensor("f_x1%d" % i, [128, D], F32)) for i in range(3)]
        xr = [E(nc.sbuf_tensor("f_xr%d" % i, [128, D], F32)) for i in range(3)] if not first else x1
        junk = E(nc.sbuf_tensor("f_junk", [128, D], F32))
        h2 = E(nc.sbuf_tensor("f_h2", [128, D], BF16))
        h2T = E(nc.sbuf_tensor("f_h2T", [128, 8, 128], BF16))
        Gb = [E(nc.sbuf_tensor("f_G%d" % i, [128, nm, 130], BF16)) for i in range(2)]
        Ub = [E(nc.sbuf_tensor("f_U%d" % i, [128, nm, 128], BF16)) for i in range(2)]
        tcv = [E(nc.sbuf_tensor("f_tc%d" % i, [128, 128], F32)) for i in range(2)]
        sg = [E(nc.sbuf_tensor("f_sg%d" % i, [128, 128], F32)) for i in range(2)]
        A = E(nc.sbuf_tensor("f_A", [128, nm, 128], BF16))
        y = [E(nc.sbuf_tensor("f_y%d" % i, [128, D], F32)) for i in range(2)]
        stat = [E(nc.sbuf_tensor("f_stat%d" % i, [128, 16], F32)) for i in range(2)]
        pT = E(nc.psum_tensor("f_pT", [128, 1024], BF16))
        pG = [E(nc.psum_tensor("f_pG%d" % i, [128, 512], F32)) for i in range(2)]
        pU = [E(nc.psum_tensor("f_pU%d" % i, [128, 512], F32)) for i in range(2)]
        pD = [E(nc.psum_tensor("f_pD%d" % i, [128, 512], F32)) for i in range(2)]
        blk = E(nc.Block())
        P.begin()
        EPS_AP[0] = epsc[:, 0:1]
        P.add("pool", lambda e: e.memset(epsc[:], EPS), [], ["eps"])
        for k in range(8):
            P.dma("sp", wg[:, k, :], g.Wg[:, k, mlo * 128:mhi * 128], writes=["wg"])
            P.dma("sp", wup[:, k, :], g.Wup[:, k, mlo * 128:mhi * 128], writes=["wup"])
        P.dma("sp", wdn[:], g.Wdn[:, mlo:mhi, :], writes=["wdn"])
        P.dma("sp", cw[:], g.cw[:, :, :], writes=["cw"])
        P.dma("sp", cb[:], g.cb[:, :], writes=["cb"])
        P.dma("sp", gf[:], g.gf[:, :], writes=["gf"])
        P.dma("sp", misc[:], g.misc[:, :], writes=["misc"])
        make_ident(nc, P, E, ident, identf)

        def finalize(i):
            b, b3 = i % 2, i % 3
            G, U_ = Gb[b], Ub[b]
            for mi in range(nm):
                m = mlo + mi
                t, s = tcv[mi % 2], sg[mi % 2]
                tn, sn = "tc%d" % (mi % 2), "sg%d" % (mi % 2)
                gin = ["G%d" % b, "Gl%d" % b, "Gr%d" % b, "cw", "cb"]
                ts(P, "dve", t[:], G[:, mi, 0:128], cw[:, m, 0:1], ALU.mult, gin, [tn], s2=cb[:, m:m + 1], op1=ALU.add)
                stt(P, t[:], G[:, mi, 1:129], cw[:, m, 1:2], t[:], ALU.mult, ALU.add, gin + [tn], [tn])
                stt(P, t[:], G[:, mi, 2:130], cw[:, m, 2:3], t[:], ALU.mult, ALU.add, gin + [tn], [tn])
                act(P, s[:], t[:], AF.Silu, [tn], [sn])
                tt(P, "pool", A[:, mi, :], s[:], U_[:, mi, :], ALU.mult, [sn, "U%d" % b], ["A%d" % mi])
            for half in range(2):
                for mi in range(nm):
                    mm(P, pD[half][:], A[:, mi, :], wdn[:, mi, half * 512:(half + 1) * 512], mi == 0, mi == nm - 1,
                       ["A%d" % mi, "wdn"], ["pD%d" % half])
            yb = y[b]
            for half in range(2):
                hs = slice(half * 512, (half + 1) * 512)
                tt(P, "dve", yb[:, hs], pD[half][:], xr[b3][:, hs], ALU.add,
                   ["pD%d" % half, ("x1%d" if first else "xr%d") % b3],
                   ["y%d_%d" % (b, half)])
            yn = ["y%d_0" % b, "y%d_1" % b]
            if last:
                S_ = stat[b]
                act(P, junk[:], yb[:], AF.Square, yn, ["junk", "fssq%d" % b], accum=S_[:, 4:5])
                rsqrt_col(P, S_[:, 6:7], S_[:, 4:5], S_[:, 5:6], 1.0 / D, ["fssq%d" % b, "eps"], ["frs%d" % b])
                stt(P, yb[:], yb[:], S_[:, 6:7], gf[:], ALU.mult, ALU.mult, yn + ["frs%d" % b, "gf"], ["yo%d" % b])
                P.dma("pool", Y[(i - t_lo) * 128:(i - t_lo + 1) * 128, :], yb[:], reads=["yo%d" % b] + yn)
            else:
                P.dma("pool", X2[i * 128:(i + 1) * 128, :], yb[:], reads=yn)

        for i in range(ntile):
            b, b3 = i % 2, i % 3
            S_ = stat[b]
            P.dma("sp", x1[b3][:], X1[i * 128:(i + 1) * 128, :], writes=["x1%d" % b3])
            if not first:
                P.dma("sp", xr[b3][:], X2[i * 128:(i + 1) * 128, :], writes=["xr%d" % b3])
            else:
                pass
            xn1 = "x1%d" % b3
            act(P, junk[:], x1[b3][:], AF.Square, [xn1], ["junk", "ssq%d" % b], accum=S_[:, 0:1])
            rsqrt_col(P, S_[:, 2:3], S_[:, 0:1], S_[:, 1:2], 1.0 / D, ["ssq%d" % b, "eps"], ["rs%d" % b])
            ts(P, "dve", h2[:], x1[b3][:], S_[:, 2:3], ALU.mult, [xn1, "rs%d" % b], ["h2"])
            for kc in range(8):
                tr(P, pT[:, kc * 128:(kc + 1) * 128], h2[:, kc * 128:(kc + 1) * 128], ident[:], ["h2", "ident"], ["pT"])
            cp(P, "act", h2T[:], pT[:].rearrange("p (k t) -> p k t", k=8), ["pT"], ["h2T"])
            G, U_ = Gb[b], Ub[b]
            need_u = t_lo <= i < t_hi
            for m0 in range(0, nm, 4):
                mc = min(4, nm - m0)
                pg, pu = pG[(m0 // 4) % 2], pU[(m0 // 4) % 2]
                pgn, pun = "pG%d" % ((m0 // 4) % 2), "pU%d" % ((m0 // 4) % 2)
                for mi in range(m0, m0 + mc):
                    for kc in range(8):
                        mm(P, pg[:, (mi - m0) * 128:(mi - m0 + 1) * 128], wg[:, kc, mi * 128:(mi + 1) * 128],
                           h2T[:, kc, :], kc == 0, kc == 7, ["wg", "h2T"], [pgn])
                cp(P, "act", G[:, m0:m0 + mc, 1:129], pg[:, 0:mc * 128].rearrange("p (m t) -> p m t", m=mc),
                   [pgn], ["G%d" % b])
                if need_u:
                    for mi in range(m0, m0 + mc):
                        for kc in range(8):
                            mm(P, pu[:, (mi - m0) * 128:(mi - m0 + 1) * 128], wup[:, kc, mi * 128:(mi + 1) * 128],
                               h2T[:, kc, :], kc == 0, kc == 7, ["wup", "h2T"], [pun])
                    cp(P, "dve", U_[:, m0:m0 + mc, :], pu[:, 0:mc * 128].rearrange("p (m t) -> p m t", m=mc),
                       [pun], ["U%d" % b])
            if i == 0:
                P.add("pool", lambda e, G=G: e.memset(G[:, :, 0:1], 0.0), [], ["Gl%d" % b])
            else:
                Gp = Gb[1 - b]
                if seq == "e" and i == 1:
                    ts(P, "dve", G[:, :, 0:1], Gp[:, :, 128:129], misc[:, 1:2], ALU.mult, ["G%d" % (1 - b), "misc"],
                       ["Gl%d" % b])
                else:
                    cp(P, "dve", G[:, :, 0:1], Gp[:, :, 128:129], ["G%d" % (1 - b)], ["Gl%d" % b])
                if seq == "e" and i == ntile - 1:
                    ts(P, "dve", Gp[:, :, 129:130], G[:, :, 1:2], misc[:, 2:3], ALU.mult, ["G%d" % b, "misc"],
                       ["Gr%d" % (1 - b)])
                else:
                    cp(P, "dve", Gp[:, :, 129:130], G[:, :, 1:2], ["G%d" % b], ["Gr%d" % (1 - b)])
                if t_lo <= i - 1 < t_hi:
                    finalize(i - 1)
        if t_lo <= ntile - 1 < t_hi:
            bl = (ntile - 1) % 2
            P.add("pool", lambda e: e.memset(Gb[bl][:, :, 129:130], 0.0), [], ["Gr%d" % bl])
            finalize(ntile - 1)
        P.finish(blk)


def s5_scalars(nc, P, E, pre, lam, npart, nf, sh):
    NE = TC + 1
    T = lambda n, shp, dt=F32: E(nc.sbuf_tensor(pre + n, shp, dt))
    V = lambda t: t[0:npart, :, 0:nf]
    dt_, lrdt, ang = T("dt", [npart, nf]), T("lrdt", [npart, nf]), T("ang", [npart, nf])
    ei, ev = T("ei", [npart, NE], I32), T("ev", [npart, NE])
    earg, t1, t2, ni, magp = sh
    PR, PI = T("PR", [npart, NE, nf]), T("PI", [npart, NE, nf])
    den, nr, za, zb = T("den", [npart, nf]), T("nr", [npart, nf]), T("za", [npart, nf]), T("zb", [npart, nf])
    ZR, ZI = T("ZR", [npart, nf]), T("ZI", [npart, nf])
    N = pre
    lr, li, ldt = lam[:, 0, :], lam[:, 1, :], lam[:, 2, :]
    ek, er, eni = T("ek", [npart, nf]), T("er", [npart, nf]), T("eni", [npart, nf], I32)
    exp_acc(P, dt_[:], ldt, ek[:], er[:], eni[:], N + "lam", N + "dt", N + "x")
    tt(P, "dve", lrdt[:], lr, dt_[:], ALU.mult, [N + "lam", N + "dt"], [N + "lrdt"])
    tt(P, "dve", ang[:], li, dt_[:], ALU.mult, [N + "lam", N + "dt"], [N + "ang"])
    P.add("pool", lambda e: e.iota(ei[:], [[1, NE]], base=0, channel_multiplier=0), [], [N + "ei"])
    cp(P, "dve", ev[:], ei[:], [N + "ei"], [N + "ev"])
    evb = rap(ev, 0, npart, 0, [[1, NE], [0, nf]])
    tt(P, "dve", V(earg), evb, rap(lrdt, 0, npart, 0, [[0, NE], [1, nf]]), ALU.mult, [N + "ev", N + "lrdt"], ["sh_earg"])
    act(P, V(magp), V(earg), AF.Exp, ["sh_earg"], ["sh_magp"])
    tt(P, "dve", V(earg), evb, rap(ang, 0, npart, 0, [[0, NE], [1, nf]]), ALU.mult,
       [N + "ev", N + "ang", "sh_magp"], ["sh_earg"])
    for which, off, dst in ((0, 0.25, PR), (1, 0.0, PI)):
        ts(P, "dve", V(t1), V(earg), 1.0 / TWO_PI, ALU.mult, ["sh_earg"], ["sh_t1"], s2=off, op1=ALU.add)
        sin_frac(P, dst[:], V(t1), V(t2), V(ni), "sh_t1", "sh_t2", "sh_ni", N + "trig%d" % which)
    tt(P, "dve", PR[:], V(magp), PR[:], ALU.mult, ["sh_magp", N + "trig0"], [N + "PR"])
    tt(P, "dve", PI[:], V(magp), PI[:], ALU.mult, ["sh_magp", N + "trig1"], [N + "PI"])
    ar, ai = PR[:, 1, :], PI[:, 1, :]
    tt(P, "dve", den[:], lr, lr, ALU.mult, [N + "lam"], [N + "den0"])
    tt(P, "dve", nr[:], li, li, ALU.mult, [N + "lam"], [N + "nr0"])
    tt(P, "dve", den[:], den[:], nr[:], ALU.add, [N + "den0", N + "nr0"], [N + "den1"])
    P.add("dve", lambda e: e.reciprocal(out=den[:], in_=den[:]), [N + "den1"], [N + "rden"])
    ts(P, "dve", nr[:], ar, -1.0, ALU.add, [N + "PR", N + "den1"], [N + "nr"])
    tt(P, "dve", za[:], nr[:], lr, ALU.mult, [N + "nr", N + "lam"], [N + "za0"])
    tt(P, "dve", zb[:], ai, li, ALU.mult, [N + "PI", N + "lam"], [N + "zb0"])
    tt(P, "dve", za[:], za[:], zb[:], ALU.add, [N + "za0", N + "zb0"], [N + "za1"])
    tt(P, "dve", ZR[:], za[:], den[:], ALU.mult, [N + "za1", N + "rden"], [N + "ZR"])
    tt(P, "dve", za[:], ai, lr, ALU.mult, [N + "PI", N + "lam", N + "ZR"], [N + "za2"])
    tt(P, "dve", zb[:], nr[:], li, ALU.mult, [N + "nr", N + "lam", N + "za1"], [N + "zb2"])
    tt(P, "dve", za[:], za[:], zb[:], ALU.subtract, [N + "za2", N + "zb2"], [N + "za3"])
    tt(P, "dve", ZI[:], za[:], den[:], ALU.mult, [N + "za3", N + "rden"], [N + "ZI"])
    return PR, PI, ZR, ZI


def phase_s5_derive(nc, P, g):
    nc = NCProxy(nc)
    with contextlib.ExitStack() as st:
        E = st.enter_context
        lamA = E(nc.sbuf_tensor("d_lamA", [64, 3, 64], F32))
        lamB = E(nc.sbuf_tensor("d_lamB", [128, 3, 32], F32))
        bA = E(nc.sbuf_tensor("d_bA", [64, 2, 64, 32], F32))
        cA = E(nc.sbuf_tensor("d_cA", [64, 2, 64, 32], F32))
        cB = E(nc.sbuf_tensor("d_cB", [128, 2, 32, 32], F32))
        dsk = E(nc.sbuf_tensor("d_dsk", [128, 8], F32))
        BbR = E(nc.sbuf_tensor("d_BbR", [64, 64, 32], F32))
        BbI = E(nc.sbuf_tensor("d_BbI", [64, 64, 32], F32))
        tA = E(nc.sbuf_tensor("d_tA", [64, 64, 32], F32))
        Wre = [E(nc.sbuf_tensor("d_Wre%d" % i, [64, 2, 4, 32], F32)) for i in range(2)]
        Wim = [E(nc.sbuf_tensor("d_Wim%d" % i, [64, 2, 4, 32], F32)) for i in range(2)]
        tW = [E(nc.sbuf_tensor("d_tW%d" % i, [64, 2, 4, 32], F32)) for i in range(2)]
        Rre = [E(nc.sbuf_tensor("d_Rre%d" % i, [128, 4, 32], F32)) for i in range(2)]
        Rim = [E(nc.sbuf_tensor("d_Rim%d" % i, [128, 4, 32], F32)) for i in range(2)]
        tR = [E(nc.sbuf_tensor("d_tR%d" % i, [128, 4, 32], F32)) for i in range(2)]
        VTst = E(nc.sbuf_tensor("d_VTst", [128, 2 * 2 * TC * 64], BF16))
        BDst = E(nc.sbuf_tensor("d_BDst", [128, 2 * TC * 128], BF16))
        Rst = E(nc.sbuf_tensor("d_Rst", [128, 4 * 2 * TC * 32], BF16))
        ident = E(nc.sbuf_tensor("d_ident", [128, 128], BF16))
        identf = E(nc.sbuf_tensor("d_identf", [128, 128], F32))
        mask = E(nc.sbuf_tensor("d_mask", [128, 128], F32))
        mi_ = E(nc.sbuf_tensor("d_mi", [128, 132], I32))
        mf = E(nc.sbuf_tensor("d_mf", [128, 132], F32))
        AT = E(nc.sbuf_tensor("d_AT", [128, 2, 32], F32))
        pV = [E(nc.psum_tensor("d_pV%d" % i, [128, 512], F32)) for i in range(2)]
        pB = [E(nc.psum_tensor("d_pB%d" % i, [128, 512], F32)) for i in range(2)]
        blk = E(nc.Block())
        P.begin()
        P.dma("sp", lamA[:], g.lamA[:, :, :], writes=["Alam"])
        P.dma("sp", lamB[:], g.lamB[:, :, :], writes=["Blam"])
        P.dma("sp", bA[:], g.bA[:, :, :, :], writes=["bA"])
        P.dma("sp", cA[:], g.cA[:, :, :, :], writes=["cA"])
        P.dma("sp", cB[:], g.cB[:, :, :, :], writes=["cB"])
        P.dma("sp", dsk[:], g.dskip[:, :], writes=["dsk"])
        make_ident(nc, P, E, ident, identf)
        P.add("pool", lambda e: e.iota(mi_[:, 0:128], [[1, 128]], base=0, channel_multiplier=0), [], ["mi0"])
        P.add("pool", lambda e: e.iota(mi_[:, 128:129], [[0, 1]], base=0, channel_multiplier=1), [], ["mi1"])
        ts(P, "dve", mi_[:, 0:129], mi_[:, 0:129], 5, ALU.arith_shift_right, ["mi0", "mi1"], ["mi2"])
        cp(P, "dve", mf[:, 0:129], mi_[:, 0:129], ["mi2"], ["mf"])
        ts(P, "dve", mask[:], mf[:, 0:128], mf[:, 128:129], ALU.is_equal, ["mf"], ["mask"])
        shf = [E(nc.sbuf_tensor("d_sh%d" % i, [128, TC + 1, 64], I32 if i == 3 else F32)) for i in range(5)]
        if STOP <= 0:
            P.finish(blk)
            return
        PRA, PIA, ZRA, ZIA = s5_scalars(nc, P, E, "A", lamA, 64, 64, shf)
        if STOP <= 1:
            P.finish(blk)
            return
        PRB, PIB, _, _ = s5_scalars(nc, P, E, "B", lamB, 128, 32, shf)
        cp(P, "dve", AT[:, 0, :], PRB[:, TC, :], ["BPR"], ["AT0"])
        cp(P, "dve", AT[:, 1, :], PIB[:, TC, :], ["BPI"], ["AT1"])
        P.dma("sp", g.ATd[:, :, :], AT[:], reads=["AT0", "AT1"])
        if STOP <= 2:
            P.finish(blk)
            return
        zrb = rap(ZRA, 0, 64, 0, [[1, 64], [0, 32]])
        zib = rap(ZIA, 0, 64, 0, [[1, 64], [0, 32]])
        tt(P, "dve", BbR[:], bA[:, 0, :, :], zrb, ALU.mult, ["bA", "AZR"], ["BbR0"])
        tt(P, "dve", tA[:], bA[:, 1, :, :], zib, ALU.mult, ["bA", "AZI"], ["tA"])
        tt(P, "dve", BbR[:], BbR[:], tA[:], ALU.subtract, ["BbR0", "tA"], ["BbR"])
        tt(P, "dve", BbI[:], bA[:, 1, :, :], zrb, ALU.mult, ["bA", "AZR"], ["BbI0"])
        tt(P, "dve", tA[:], bA[:, 0, :, :], zib, ALU.mult, ["bA", "AZI", "BbR"], ["tA2"])
        tt(P, "dve", BbI[:], BbI[:], tA[:], ALU.add, ["BbI0", "tA2"], ["BbI"])
        ts(P, "dve", cA[:, 1, :, :], cA[:, 1, :, :], -1.0, ALU.mult, ["cA"], ["cAn"])
        if STOP <= 3:
            P.finish(blk)
            return
        n = 0
        for ut in range(8 if STOP > 10 else 1):
            g0 = 4 * ut
            for e in range(TC + 1):
                b = n % 2
                n += 1
                if e < TC:
                    def psel(tn):
                        return rap(tn, 0, 64, e * 64 + g0, [[32, 2], [1, 4], [0, 32]])
                    BR4 = rap(BbR, 0, 64, g0 * 32, [[32 * 32, 2], [32, 4], [1, 32]])
                    BI4 = rap(BbI, 0, 64, g0 * 32, [[32 * 32, 2], [32, 4], [1, 32]])
                    wr, wi, tw = Wre[b], Wim[b], tW[b]
                    tt(P, "dve", wr[:], BR4, psel(PRA), ALU.mult, ["BbR", "APR"], ["wr%d" % b, "Wre%d" % b])
                    tt(P, "pool", tw[:], BI4, psel(PIA), ALU.mult, ["BbI", "API"], ["tw%d" % b, "tw2%d" % b])
                    tt(P, "dve", wr[:], wr[:], tw[:], ALU.subtract, ["wr%d" % b, "tw%d" % b], ["Wre%d" % b])
                    tt(P, "dve", wi[:], BI4, psel(PRA), ALU.mult, ["BbI", "APR"], ["wi%d" % b, "Wim%d" % b])
                    tt(P, "pool", tw[:], BR4, psel(PIA), ALU.mult, ["BbR", "API", "Wre%d" % b], ["tw2%d" % b])
                    tt(P, "dve", wi[:], wi[:], tw[:], ALU.add, ["wi%d" % b, "tw2%d" % b], ["Wim%d" % b])
                    if STOP <= 4:
                        continue
                    pv = pV[b]
                    for d in range(2):
                        for part, wsrc in ((0, wr), (1, wi)):
                            col = (d * 2 + part) * 64
                            P.add("pe", lambda e_, pv=pv, col=col, wsrc=wsrc, d=d: e_.transpose(
                                pv[:, col:col + 64], wsrc[:, d, :, :].rearrange("p g h -> p (g h)"),
                                identf[0:64, 0:64]), ["Wre%d" % b, "Wim%d" % b, "identf"], ["pV%d" % b])
                    for d in range(2):
                        j = (TC - 1 - e) if d == 0 else e
                        outap = rap(VTst, 0, 128, (d * 2 * TC + j) * 64, [[TC * 64, 2], [1, 64]])
                        inap = rap(pv, 0, 128, d * 128, [[64, 2], [1, 64]])
                        cp(P, "act", outap, inap, ["pV%d" % b], ["VTst"])
                    if STOP <= 5:
                        continue
                    pb = pB[b]
                    for d in range(2):
                        col = d * 128
                        gsl = slice(d * 32 + g0, d * 32 + g0 + 4)
                        mm(P, pb[:, col:col + 128], wr[:, d, :, :].rearrange("p g h -> p (g h)"),
                           cA[:, 0, gsl, :].rearrange("p g h -> p (g h)"), True, False,
                           ["Wre%d" % b, "cA", "cAn"], ["pB%d" % b])
                        mm(P, pb[:, col:col + 128], wi[:, d, :, :].rearrange("p g h -> p (g h)"),
                           cA[:, 1, gsl, :].rearrange("p g h -> p (g h)"), False, True,
                           ["Wim%d" % b, "cA", "cAn"], ["pB%d" % b])
                    outap = rap(BDst, 0, 128, e * 128, [[TC * 128, 2], [1, 128]])
                    tt(P, "dve", outap, pb[:, 0:256].rearrange("p (d c) -> p d c", d=2),
                       rap(mask, 0, 128, 0, [[0, 2], [1, 128]]), ALU.mult, ["pB%d" % b, "mask"], ["BDst"])
                if e >= 1 and STOP > 6:
                    CR = cB[:, 0, g0:g0 + 4, :]
                    CI = cB[:, 1, g0:g0 + 4, :]
                    prb = rap(PRB, 0, 128, e * 32 + g0, [[1, 4], [0, 32]])
                    pib = rap(PIB, 0, 128, e * 32 + g0, [[1, 4], [0, 32]])
                    rr, ri, tr_ = Rre[b], Rim[b], tR[b]
                    tt(P, "dve", rr[:], CR, prb, ALU.mult, ["cB", "BPR"], ["rr%d" % b, "Rre%d" % b])
                    tt(P, "pool", tr_[:], CI, pib, ALU.mult, ["cB", "BPI"], ["tr%d" % b, "tr2%d" % b])
                    tt(P, "dve", rr[:], rr[:], tr_[:], ALU.subtract, ["rr%d" % b, "tr%d" % b], ["Rre%d" % b])
                    tt(P, "dve", ri[:], CR, pib, ALU.mult, ["cB", "BPI"], ["ri%d" % b, "Rim%d" % b])
                    tt(P, "pool", tr_[:], CI, prb, ALU.mult, ["cB", "BPR", "Rre%d" % b], ["tr2%d" % b])
                    stt(P, ri[:], ri[:], -1.0, tr_[:], ALU.mult, ALU.subtract, ["ri%d" % b, "tr2%d" % b], ["Rim%d" % b])
                    for part, src in ((0, rr), (1, ri)):
                        for d in range(2):
                            t = (e - 1) if d == 0 else (TC - e)
                            outap = rap(Rst, 64 * d, 64, (part * TC + t) * 32, [[2 * TC * 32, 4], [1, 32]])
                            inap = rap(src, 64 * d, 64, 0, [[32, 4], [1, 32]])
                            cp(P, "act" if d else "pool", outap, inap, ["Rre%d" % b, "Rim%d" % b], ["Rst"])
            o_ = rap(BDst, 0, 128, 0, [[1, 128]])
            stt(P, o_, identf[:], dsk[:, ut:ut + 1], o_, ALU.mult, ALU.add, ["BDst", "identf", "dsk"], ["BDst"])
            P.dma("sp", g.VTd[ut, :, :], VTst[:], reads=["VTst"])
            P.dma("sp", g.BDd[ut, :, :], BDst[:], reads=["BDst"])
            P.dma("sp", g.Rd[ut, :, :], Rst[:], reads=["Rst"])
        P.finish(blk)


def phase_s5_states(nc, P, g, seq):
    nc = NCProxy(nc)
    U, L = {"p": (g.Up, LP), "s": (g.Us, LS), "e": (g.Ue, LEXT)}[seq]
    nch = L // TC
    PIECE = min(L, 8192)
    npiece = L // PIECE
    ncp = PIECE // TC
    with contextlib.ExitStack() as st:
        E = st.enter_context
        XS = E(nc.sbuf_tensor("s_XS", [128, nch, 32, 2], F32))
        VT = [E(nc.sbuf_tensor("s_VT%d" % i, [128, 2 * 2 * TC * 64], BF16)) for i in range(2)]
        ub = [E(nc.sbuf_tensor("s_u%d" % i, [128, PIECE], BF16)) for i in range(2)]
        AT = E(nc.sbuf_tensor("s_AT", [128, 2, 32], F32))
        A1 = E(nc.sbuf_tensor("s_A1", [128, 32, 2], F32))
        A2 = E(nc.sbuf_tensor("s_A2", [128, 32, 2], F32))
        INIT = E(nc.sbuf_tensor("s_INIT", [128, 32, 2], F32))
        t1 = E(nc.sbuf_tensor("s_t1", [128, 32, 2], F32))
        t2 = E(nc.sbuf_tensor("s_t2", [128, 32, 2], F32))
        ohc = E(nc.sbuf_tensor("s_ohc", [128, LS // TC], F32))
        SB = E(nc.sbuf_tensor("s_SB", [128, nch if seq != "s" else 1, 64], BF16))
        pX = [E(nc.psum_tensor("s_pX%d" % i, [128, 512], F32)) for i in range(8)]
        blk = E(nc.Block())
        P.begin()
        P.dma("sp", AT[:], g.ATd[:, :, :], writes=["AT"])
        cp(P, "dve", A1[:, :, 0], AT[:, 0, :], ["AT"], ["A1a"])
        cp(P, "dve", A1[:, :, 1], AT[:, 0, :], ["AT"], ["A1b"])
        ts(P, "dve", A2[:, :, 0], AT[:, 1, :], -1.0, ALU.mult, ["AT"], ["A2a"])
        cp(P, "dve", A2[:, :, 1], AT[:, 1, :], ["AT"], ["A2b"])
        if seq == "e":
            P.dma("sp", INIT[:], g.INITd[:, :, :], writes=["INIT"])
        else:
            P.add("pool", lambda e: e.memset(INIT[:], 0.0), [], ["INIT"])
        if seq == "s":
            P.dma("sp", ohc[0:64, :], g.oh[0:64, 0, :], writes=["ohc0"])
            P.dma("sp", ohc[64:128, :], g.oh[64:128, 1, :], writes=["ohc1"])
        nld = 0
        nev = 0
        for gh in range(2):
            for ul in range(4):
                ut = 4 * gh + ul
                vb = VT[ut % 2]
                P.dma("sp", vb[:], g.VTd[ut, :, :], writes=["VT%d" % (ut % 2)])
                for pc in range(npiece):
                    k = nld % 2
                    nld += 1
                    P.dma("sp", ub[k][:], U[ut, :, pc * PIECE:(pc + 1) * PIECE], writes=["u%d" % k])
                    for part in range(2):
                        for j in range(TC):
                            for slot in range(4):
                                bank = pX[part * 4 + slot]
                                for d in range(2):
                                    off = ((d * 2 + part) * TC + j) * 64
                                    mm(P, bank[64 * d:64 * d + 64, 0:ncp], vb[32 * slot:32 * slot + 32, off:off + 64],
                                       rap(ub[k], 32 * slot, 32, j, [[TC, ncp]]), j == 0, j == TC - 1,
                                       ["VT%d" % (ut % 2), "u%d" % k], ["pX%d" % (part * 4 + slot)],
                                       tile_position=(32 * slot, 64 * d))
                        for slot in range(4):
                            outap = rap(XS, 0, 128, pc * ncp * 64 + (ut * 4 + slot) * 2 + part, [[64, ncp]])
                            nev += 1
                            cp(P, "act" if nev % 2 else "dve", outap, pX[part * 4 + slot][:, 0:ncp],
                               ["pX%d" % (part * 4 + slot)], ["XS%d" % gh])
            go = 16 * gh * 2
            for eng, p0, order, nm in (("dve", 0, range(nch), "XSf%d" % gh),
                                       ("pool", 64, range(nch - 1, -1, -1), "XSb%d" % gh)):
                prev_t, prev_off = INIT, go
                rd = ["XS%d" % gh, "INIT", "A1a", "A1b", "A2a", "A2b"]
                a1 = rap(A1, p0, 64, go, [[2, 16], [1, 2]])
                a2 = rap(A2, p0, 64, go, [[2, 16], [1, 2]])
                tA = rap(t1, p0, 64, go, [[2, 16], [1, 2]])
                tB = rap(t2, p0, 64, go, [[2, 16], [1, 2]])
                tn = "t" + nm
                for c in order:
                    sp_ = rap(prev_t, p0, 64, prev_off, [[2, 16], [1, 2]])
                    sps = rap(prev_t, p0, 64, prev_off + 1, [[2, 16], [-1, 2]])
                    x = rap(XS, p0, 64, c * 64 + go, [[2, 16], [1, 2]])
                    tt(P, eng, tA, sp_, a1, ALU.mult, rd + [nm], [tn + "1"])
                    tt(P, eng, tB, sps, a2, ALU.mult, rd + [nm], [tn + "2"])
                    tt(P, eng, x, x, tA, ALU.add, [tn + "1", "XS%d" % gh, nm], [nm])
                    tt(P, eng, x, x, tB, ALU.add, [tn + "2", nm], [nm])
                    prev_t, prev_off = XS, c * 64 + go
        fin = ["XSf0", "XSb0", "XSf1", "XSb1"]
        tnames = ["tXSf01", "tXSb01", "tXSf11", "tXSb11", "tXSf02", "tXSb02", "tXSf12", "tXSb12"]
        xs3 = XS[:].rearrange("p c g t -> p c (g t)")
        if seq == "s":
            tt(P, "dve", xs3, xs3, rap(ohc, 0, 128, 0, [[1, nch], [0, 64]]), ALU.mult, fin + ["ohc0", "ohc1"], ["XSm"])
            P.add("dve", lambda e: e.tensor_reduce(out=t1[:].rearrange("p g t -> p (g t)"),
                                                   in_=rap(XS, 0, 128, 0, [[1, 64], [64, nch]]), axis=AX.X, op=ALU.add),
                  ["XSm"] + tnames, ["sel"])
            P.dma("sp", g.INITd[:, :, :], t1[:], reads=["sel"])
        else:
            SBd = g.SBp if seq == "p" else g.SBe
            cp(P, "dve", SB[0:64, 1:nch, :], xs3[0:64, 0:nch - 1, :], fin, ["SB1"])
            cp(P, "pool", SB[64:128, 0:nch - 1, :], xs3[64:128, 1:nch, :], fin, ["SB1b"])
            cp(P, "dve", SB[0:64, 0, :], INIT[0:64].rearrange("p g t -> p (g t)"), ["INIT"], ["SB2"])
            cp(P, "pool", SB[64:128, nch - 1, :], INIT[64:128].rearrange("p g t -> p (g t)"), ["INIT"], ["SB3"])
            P.dma("sp", SBd[:, :], SB[:].rearrange("p c x -> p (c x)"), reads=["SB1", "SB1b", "SB2", "SB3"])
        P.finish(blk)


def phase_s5_out(nc, P, g, seq):
    nc = NCProxy(nc)
    U, Sd, SBd, L = {"p": (g.Up, g.Sp, g.SBp, LP), "e": (g.Ue, g.Se, g.SBe, LEXT)}[seq]
    nch = L // TC
    NPT = 512 if seq == "p" else 256
    ncq = NPT // TC
    npiece = L // NPT
    with contextlib.ExitStack() as st:
        E = st.enter_context
        SB = E(nc.sbuf_tensor("o_SB", [128, nch, 64], BF16))
        YS = E(nc.sbuf_tensor("o_YS", [128, nch, TC], F32))
        R = E(nc.sbuf_tensor("o_R", [128, 4 * 2 * TC * 32], BF16))
        BD = E(nc.sbuf_tensor("o_BD", [128, 2 * TC * 128], BF16))
        ub = [E(nc.sbuf_tensor("o_u%d" % i, [128, L], BF16)) for i in range(2)]
        yv = [E(nc.sbuf_tensor("o_yv%d" % i, [128, NPT], F32)) for i in range(2)]
        x2 = [E(nc.sbuf_tensor("o_x2%d" % i, [128, NPT], F32)) for i in range(2)]
        sgm = [E(nc.sbuf_tensor("o_sg%d" % i, [128, NPT], F32)) for i in range(2)]
        sst = [E(nc.sbuf_tensor("o_ss%d" % i, [128, NPT], BF16)) for i in range(2)]
        pY = [E(nc.psum_tensor("o_pY%d" % i, [128, 512], F32)) for i in range(2)]
        pF = [E(nc.psum_tensor("o_pF%d" % i, [128, 512], F32)) for i in range(2)]
        blk = E(nc.Block())
        P.begin()
        P.dma("sp", SB[:].rearrange("p c x -> p (c x)"), SBd[:, :], writes=["SB"])
        npc = 0
        for ut in range(8):
            ul = ut
            u = ub[ul % 2]
            un = "u%d" % (ul % 2)
            P.dma("sp", u[:], U[ut, :, :], writes=[un])
            P.dma("sp", R[:], g.Rd[ut, :, :], writes=["R"])
            P.dma("sp", BD[:], g.BDd[ut, :, :], writes=["BD"])
            for t in range(TC):
                py = pY[t % 2]
                pn = "pY%d" % (t % 2)
                for part in range(2):
                    for slot in range(4):
                        off = ((slot * 2 + part) * TC + t) * 32
                        soff = (ut * 4 + slot) * 2 + part
                        mm(P, py[32 * slot:32 * slot + 32, 0:nch], R[:, off:off + 32],
                           rap(SB, 0, 128, soff, [[64, nch]]), part == 0, part == 1, ["R", "SB"], [pn],
                           tile_position=(0, 32 * slot))
                cp(P, "act", rap(YS, 0, 128, t, [[TC, nch]]), py[:, 0:nch], [pn], ["YS"])
            for pc in range(npiece if STOP > 21 else 0):
                k = npc % 2
                npc += 1
                pf = pF[k]
                pfn = "pF%d" % k
                u3 = u[:, pc * NPT:(pc + 1) * NPT].rearrange("p (c t) -> p c t", t=TC)
                f3 = pf[:, 0:NPT].rearrange("p (c t) -> p c t", t=TC)
                for d in range(2):
                    for lag in range(TC):
                        lo_, hi_ = (lag, TC) if d == 0 else (0, TC - lag)
                        ro_, rh_ = (0, TC - lag) if d == 0 else (lag, TC)
                        mm(P, f3[:, :, lo_:hi_], BD[:, (d * TC + lag) * 128:(d * TC + lag + 1) * 128],
                           u3[:, :, ro_:rh_], d == 0 and lag == 0, d == 1 and lag == TC - 1, ["BD", un], [pfn])
                if STOP <= 22:
                    continue
                y_, x2_, sg_, ss_ = yv[k], x2[k], sgm[k], sst[k]
                tt(P, "dve", y_[:], pf[:, 0:NPT], YS[:, pc * ncq:(pc + 1) * ncq, :].rearrange("p c t -> p (c t)"),
                   ALU.add, [pfn, "YS"], ["yv%d" % k])
                tt(P, "pool", x2_[:], y_[:], y_[:], ALU.mult, ["yv%d" % k], ["x2%d" % k])
                ts(P, "dve", x2_[:], x2_[:], 0.044715, ALU.mult, ["x2%d" % k], ["x2%d" % k], s2=1.0, op1=ALU.add)
                tt(P, "pool", x2_[:], x2_[:], y_[:], ALU.mult, ["x2%d" % k, "yv%d" % k], ["x2%d" % k])
                act(P, sg_[:], x2_[:], AF.Sigmoid, ["x2%d" % k], ["sg%d" % k], scale=1.5957691216057308)
                tt(P, "dve", ss_[:], y_[:], sg_[:], ALU.mult, ["yv%d" % k, "sg%d" % k], ["ss%d" % k])
                P.dma("pool", Sd[ut, :, pc * NPT:(pc + 1) * NPT], ss_[:], reads=["ss%d" % k])
        P.finish(blk)


def phase_ffn2(nc, P, g, seq, mlo, mhi, first, last):
    nc = NCProxy(nc)
    if seq == "p":
        X1, X2, Y, ntile, t_lo, t_hi = g.X1p, g.X2p, g.yp, LP // 128, 0, LP // 128
    else:
        X1, X2, Y, ntile, t_lo, t_hi = g.X1e, g.X2e, g.ys, LEXT // 128, 1, LEXT // 128 - 1
    nm = mhi - mlo
    nsup = (ntile + 3) // 4
    with contextlib.ExitStack() as st:
        E = st.enter_context
        wg = E(nc.sbuf_tensor("f_wg", [128, 8, nm * 128], BF16))
        wup = E(nc.sbuf_tensor("f_wup", [128, 8, nm * 128], BF16))
        wdn = E(nc.sbuf_tensor("f_wdn", [128, nm, 1024], BF16))
        cw = E(nc.sbuf_tensor("f_cw", [128, NFF, 3], F32))
        cb = E(nc.sbuf_tensor("f_cb", [128, NFF], F32))
        gf = E(nc.sbuf_tensor("f_gf", [128, D], F32))
        misc = E(nc.sbuf_tensor("f_misc", [128, 8], F32))
        ident = E(nc.sbuf_tensor("f_ident", [128, 128], BF16))
        identf = E(nc.sbuf_tensor("f_identf", [128, 128], F32))
        epsc = E(nc.sbuf_tensor("f_eps", [128, 1], F32))
        if first:
            xr = [E(nc.sbuf_tensor("f_xr%d" % i, [128, 4, D], F32)) for i in range(2)]
            x1 = None
        else:
            xr = [E(nc.sbuf_tensor("f_xr%d" % i, [128, 4, D], F32)) for i in range(2)]
            x1 = [E(nc.sbuf_tensor("f_x1%d" % i, [128, D], F32)) for i in range(2)]
        junk = E(nc.sbuf_tensor("f_junk", [128, D], F32))
        h2 = [E(nc.sbuf_tensor("f_h2%d" % i, [128, D], BF16)) for i in range(2)]
        h2T = E(nc.sbuf_tensor("f_h2T", [128, 8, 512], BF16))
        Gb = [E(nc.sbuf_tensor("f_G%d" % i, [128, nm, 514], BF16)) for i in range(2)]
        Ub = [E(nc.sbuf_tensor("f_U%d" % i, [128, nm, 512], BF16)) for i in range(2)]
        tcv = [E(nc.sbuf_tensor("f_tc%d" % i, [128, 512], F32)) for i in range(2)]
        sg = [E(nc.sbuf_tensor("f_sg%d" % i, [128, 512], F32)) for i in range(2)]
        A = E(nc.sbuf_tensor("f_A", [128, nm, 512], BF16))
        y = [E(nc.sbuf_tensor("f_y%d" % i, [128, D], F32)) for i in range(2)]
        stat = [E(nc.sbuf_tensor("f_stat%d" % i, [128, 16], F32)) for i in range(2)]
        pT = E(nc.psum_tensor("f_pT", [128, 1024], BF16))
        pG = [E(nc.psum_tensor("f_pG%d" % i, [128, 512], F32)) for i in range(2)]
        pU = [E(nc.psum_tensor("f_pU%d" % i, [128, 512], F32)) for i in range(2)]
        pD = [E(nc.psum_tensor("f_pD%d" % i, [128, 512], F32)) for i in range(2)]
        blk = E(nc.Block())
        P.begin()
        EPS_AP[0] = epsc[:, 0:1]
        P.add("pool", lambda e: e.memset(epsc[:], EPS), [], ["eps"])
        for k in range(8):
            P.dma("sp", wg[:, k, :], g.Wg[:, k, mlo * 128:mhi * 128], writes=["wg"])
            P.dma("sp", wup[:, k, :], g.Wup[:, k, mlo * 128:mhi * 128], writes=["wup"])
        P.dma("sp", wdn[:], g.Wdn[:, mlo:mhi, :], writes=["wdn"])
        P.dma("sp", cw[:], g.cw[:, :, :], writes=["cw"])
        P.dma("sp", cb[:], g.cb[:, :], writes=["cb"])
        P.dma("sp", gf[:], g.gf[:, :], writes=["gf"])
        P.dma("sp", misc[:], g.misc[:, :], writes=["misc"])
        make_ident(nc, P, E, ident, identf)
        widths = [min(4, ntile - 4 * k) * 128 for k in range(nsup)]
        ny = [0]

        def finalize(k):
            b = k % 2
            W = widths[k]
            G, U_ = Gb[b], Ub[b]
            gin = ["G%d" % b, "Gl%d" % b, "Gr%d" % b, "Gm%d" % b, "cw", "cb"]
            for mi in range(nm):
                m = mlo + mi
                t, s = tcv[mi % 2], sg[mi % 2]
                tn, sn = "tc%d" % (mi % 2), "sg%d" % (mi % 2)
                ts(P, "dve", t[:, 0:W], G[:, mi, 0:W], cw[:, m, 0:1], ALU.mult, gin, [tn], s2=cb[:, m:m + 1], op1=ALU.add)
                stt(P, t[:, 0:W], G[:, mi, 1:W + 1], cw[:, m, 1:2], t[:, 0:W], ALU.mult, ALU.add, gin + [tn], [tn])
                stt(P, t[:, 0:W], G[:, mi, 2:W + 2], cw[:, m, 2:3], t[:, 0:W], ALU.mult, ALU.add, gin + [tn], [tn])
                act(P, s[:, 0:W], t[:, 0:W], AF.Silu, [tn], [sn])
                tt(P, "pool", A[:, mi, 0:W], s[:, 0:W], U_[:, mi, 0:W], ALU.mult, [sn, "U%d" % b], ["A%d" % mi])
            for sub in range(W // 128):
                i = 4 * k + sub
                if not (t_lo <= i < t_hi):
                    continue
                yb = y[ny[0] % 2]
                yi = ny[0] % 2
                ny[0] += 1
                for half in range(2):
                    for mi in range(nm):
                        mm(P, pD[half][:], A[:, mi, sub * 128:(sub + 1) * 128], wdn[:, mi, half * 512:(half + 1) * 512],
                           mi == 0, mi == nm - 1, ["A%d" % mi, "wdn"], ["pD%d" % half])
                for half in range(2):
                    hs = slice(half * 512, (half + 1) * 512)
                    tt(P, "dve", yb[:, hs], pD[half][:], xr[b][:, sub, hs], ALU.add,
                       ["pD%d" % half, "xr%d_%d" % (b, sub)], ["y%d_%d" % (yi, half)])
                yn = ["y%d_0" % yi, "y%d_1" % yi]
                if last:
                    S_ = stat[yi]
                    act(P, junk[:], yb[:], AF.Square, yn, ["junk", "fssq%d" % yi], accum=S_[:, 4:5])
                    rsqrt_col(P, S_[:, 6:7], S_[:, 4:5], S_[:, 5:6], 1.0 / D, ["fssq%d" % yi, "eps"], ["frs%d" % yi])
                    stt(P, yb[:], yb[:], S_[:, 6:7], gf[:], ALU.mult, ALU.mult, yn + ["frs%d" % yi, "gf"], ["yo%d" % yi])
                    P.dma("pool", Y[(i - t_lo) * 128:(i - t_lo + 1) * 128, :], yb[:], reads=["yo%d" % yi] + yn)
                else:
                    P.dma("pool", X2[i * 128:(i + 1) * 128, :], yb[:], reads=yn)

        nt = 0
        for k in range(nsup):
            b = k % 2
            W = widths[k]
            nsub = W // 128
            G, U_ = Gb[b], Ub[b]
            for sub in range(nsub):
                i = 4 * k + sub
                bb = nt % 2
                nt += 1
                S_ = stat[bb]
                if first:
                    xs_, xn1 = xr[b][:, sub, :], "xr%d_%d" % (b, sub)
                    P.dma("sp", xs_, X1[i * 128:(i + 1) * 128, :], writes=[xn1])
                else:
                    xs_, xn1 = x1[bb][:], "x1%d" % bb
                    P.dma("sp", xs_, X1[i * 128:(i + 1) * 128, :], writes=[xn1])
                    P.dma("sp", xr[b][:, sub, :], X2[i * 128:(i + 1) * 128, :], writes=["xr%d_%d" % (b, sub)])
                act(P, junk[:], xs_, AF.Square, [xn1], ["junk", "ssq%d" % bb], accum=S_[:, 0:1])
                rsqrt_col(P, S_[:, 2:3], S_[:, 0:1], S_[:, 1:2], 1.0 / D, ["ssq%d" % bb, "eps"], ["rs%d" % bb])
                ts(P, "dve", h2[bb][:], xs_, S_[:, 2:3], ALU.mult, [xn1, "rs%d" % bb], ["h2%d" % bb])
                for kc in range(8):
                    tr(P, pT[:, kc * 128:(kc + 1) * 128], h2[bb][:, kc * 128:(kc + 1) * 128], ident[:],
                       ["h2%d" % bb, "ident"], ["pT"])
                cp(P, "act", h2T[:, :, sub * 128:(sub + 1) * 128], pT[:].rearrange("p (k t) -> p k t", k=8),
                   ["pT"], ["h2T%d" % sub])
            hn = ["h2T%d" % sub for sub in range(nsub)]
            for mi in range(nm):
                pg, pu = pG[mi % 2], pU[mi % 2]
                for kc in range(8):
                    mm(P, pg[:, 0:W], wg[:, kc, mi * 128:(mi + 1) * 128], h2T[:, kc, 0:W], kc == 0, kc == 7,
                       ["wg"] + hn, ["pG%d" % (mi % 2)])
                cp(P, "act", G[:, mi, 1:W + 1], pg[:, 0:W], ["pG%d" % (mi % 2)], ["G%d" % b])
                for kc in range(8):
                    mm(P, pu[:, 0:W], wup[:, kc, mi * 128:(mi + 1) * 128], h2T[:, kc, 0:W], kc == 0, kc == 7,
                       ["wup"] + hn, ["pU%d" % (mi % 2)])
                cp(P, "dve", U_[:, mi, 0:W], pu[:, 0:W], ["pU%d" % (mi % 2)], ["U%d" % b])
            mk = []
            if seq == "e" and k == 0:
                ts(P, "dve", G[:, :, 128:129], G[:, :, 128:129], misc[:, 1:2], ALU.mult, ["G%d" % b, "misc"], ["Gm%d" % b])
            if seq == "e" and k == nsup - 1:
                c0 = 1 + ((ntile - 1) % 4) * 128
                ts(P, "dve", G[:, :, c0:c0 + 1], G[:, :, c0:c0 + 1], misc[:, 2:3], ALU.mult, ["G%d" % b, "misc"], ["Gm%d" % b])
            if k == 0:
                P.add("pool", lambda e, G=G: e.memset(G[:, :, 0:1], 0.0), [], ["Gl%d" % b])
            else:
                Gp = Gb[1 - b]
                Wp = widths[k - 1]
                cp(P, "dve", G[:, :, 0:1], Gp[:, :, Wp:Wp + 1], ["G%d" % (1 - b), "Gm%d" % (1 - b)], ["Gl%d" % b])
                cp(P, "dve", Gp[:, :, Wp + 1:Wp + 2], G[:, :, 1:2], ["G%d" % b, "Gm%d" % b], ["Gr%d" % (1 - b)])
                finalize(k - 1)
        bl = (nsup - 1) % 2
        Wl = widths[nsup - 1]
        P.add("pool", lambda e: e.memset(Gb[bl][:, :, Wl + 1:Wl + 2], 0.0), [], ["Gr%d" % bl])
        finalize(nsup - 1)
        P.finish(blk)
```
